# Optimizing a Trainium2 kernel written in Bass

```python
import math
import jax, jax.numpy as jnp
from jax import lax
import numpy as np

D_MODEL = 2048
BATCH = 2
SEQ = 4096
DEPTH = 4

N_META = 16
HEAD_DIM = 64
ATTN_WIDTH = D_MODEL // 2
N_HEADS = ATTN_WIDTH // HEAD_DIM
N_KV_HEADS = N_HEADS // 4
KV_GROUP = N_HEADS // N_KV_HEADS
KV_WIDTH = N_KV_HEADS * HEAD_DIM
SSM_WIDTH = D_MODEL - ATTN_WIDTH
SSM_GROUP_CH = 16
SSM_GROUPS = SSM_WIDTH // SSM_GROUP_CH
SSM_STATE = 64
WINDOW = 128
BLOCK = 128
PAD = BLOCK - N_META
D_FF = 4 * D_MODEL
IN_WIDTH = ATTN_WIDTH + 2 * KV_WIDTH + SSM_WIDTH
NORM_EPS = 1e-6
NEG_INF = -1e30
STEP_MIN = 1e-3
STEP_MAX = 1e-1

kernel_name = "hymba_s5_swa_alibi_trunk"


def _rmsnorm(x, g):
    xf = x.astype(jnp.float32)
    y = xf * lax.rsqrt(jnp.mean(xf * xf, axis=-1, keepdims=True) + NORM_EPS)
    return (y * g.astype(jnp.float32)).astype(x.dtype)


def _alibi_slopes():
    h = jnp.arange(1, N_HEADS + 1, dtype=jnp.float32)
    return jnp.exp2(-8.0 * h / N_HEADS)


def _sliding_window_attention(q, k, v, sinks):
    b, L = q.shape[0], q.shape[1]
    dtype = q.dtype
    Lp = L + PAD
    nb = Lp // BLOCK
    qf = q.astype(jnp.float32)
    kf = k.astype(jnp.float32)
    vf = v.astype(jnp.float32)
    pad4 = ((0, 0), (PAD, 0), (0, 0), (0, 0))
    qp = jnp.pad(qf, pad4).reshape(b, nb, BLOCK, N_KV_HEADS, KV_GROUP, HEAD_DIM)
    kp = jnp.pad(kf, pad4).reshape(b, nb, BLOCK, N_KV_HEADS, HEAD_DIM)
    vp = jnp.pad(vf, pad4).reshape(b, nb, BLOCK, N_KV_HEADS, HEAD_DIM)
    pad5 = ((0, 0), (1, 0), (0, 0), (0, 0), (0, 0))
    k_band = jnp.concatenate([jnp.pad(kp, pad5)[:, :-1], kp], axis=2)
    v_band = jnp.concatenate([jnp.pad(vp, pad5)[:, :-1], vp], axis=2)
    k_meta = kf[:, :N_META]
    v_meta = vf[:, :N_META]

    n_idx = jnp.arange(nb)[:, None, None]
    i_idx = jnp.arange(BLOCK)[None, :, None]
    j_idx = jnp.arange(2 * BLOCK)[None, None, :]
    t_pos = n_idx * BLOCK + i_idx - PAD
    s_pos = (n_idx - 1) * BLOCK + j_idx - PAD
    band_mask = (s_pos >= N_META) & (s_pos <= t_pos) & (t_pos - s_pos < WINDOW)
    band_dist = jnp.abs(t_pos - s_pos).astype(jnp.float32)
    m_pos = jnp.arange(N_META)[None, None, :]
    meta_mask = m_pos <= t_pos
    meta_dist = jnp.abs(t_pos - m_pos).astype(jnp.float32)

    slopes = _alibi_slopes().reshape(N_KV_HEADS, KV_GROUP, 1, 1)
    scale = 1.0 / math.sqrt(HEAD_DIM)
    s_band = jnp.einsum('bnqkgd,bnskd->bnkgqs', qp, k_band) * scale
    s_band = s_band - slopes * band_dist[:, None, None]
    s_band = jnp.where(band_mask[:, None, None], s_band, NEG_INF)
    s_meta = jnp.einsum('bnqkgd,bmkd->bnkgqm', qp, k_meta) * scale
    s_meta = s_meta - slopes * meta_dist[:, None, None]
    s_meta = jnp.where(meta_mask[:, None, None], s_meta, NEG_INF)
    sink = jnp.broadcast_to(sinks.astype(jnp.float32).reshape(N_KV_HEADS, KV_GROUP, 1, 1),
                            s_band.shape[:-1] + (1,))
    probs = jax.nn.softmax(jnp.concatenate([s_band, s_meta, sink], axis=-1), axis=-1)
    p_band = probs[..., :2 * BLOCK]
    p_meta = probs[..., 2 * BLOCK:2 * BLOCK + N_META]
    out = (jnp.einsum('bnkgqs,bnskd->bnqkgd', p_band, v_band)
           + jnp.einsum('bnkgqm,bmkd->bnqkgd', p_meta, v_meta))
    out = out.reshape(b, Lp, ATTN_WIDTH)[:, PAD:]
    return out.astype(dtype)


def _ssm_combine(e_i, e_j):
    a_i, b_i = e_i
    a_j, b_j = e_j
    return a_j * a_i, a_j * b_i + b_j


def _s5_mixer(u, lam_re, lam_im, log_step, b_re, b_im, c_re, c_im, d, w_glu, b_glu):
    dtype = u.dtype
    b, L = u.shape[0], u.shape[1]
    ul = jnp.moveaxis(u.astype(jnp.float32).reshape(b, L, SSM_GROUPS, SSM_GROUP_CH), 1, 0)
    lam = lax.complex(lam_re.astype(jnp.float32), lam_im.astype(jnp.float32))
    delta = jnp.exp(log_step.astype(jnp.float32))[:, None]
    lam_bar = jnp.exp(lam * delta)
    b_c = lax.complex(b_re.astype(jnp.float32), b_im.astype(jnp.float32))
    b_bar = ((lam_bar - 1.0) / lam)[..., None] * b_c
    c_c = lax.complex(c_re.astype(jnp.float32), c_im.astype(jnp.float32))
    bu = jnp.einsum('lbgh,gph->lbgp', ul.astype(jnp.complex64), b_bar)
    a = jnp.broadcast_to(lam_bar, (L, 1, SSM_GROUPS, SSM_STATE))
    _, states = lax.associative_scan(_ssm_combine, (a, bu), axis=0)
    y = jnp.real(jnp.einsum('lbgp,ghp->lbgh', states, c_c))
    y = y + d.astype(jnp.float32).reshape(SSM_GROUPS, SSM_GROUP_CH) * ul
    y = jnp.moveaxis(y, 0, 1).reshape(b, L, SSM_WIDTH)
    g = jax.nn.gelu(y)
    out = g * jax.nn.sigmoid(g @ w_glu.astype(jnp.float32) + b_glu.astype(jnp.float32))
    return out.astype(dtype)


def setup_inputs(seed: int = 0) -> dict:
    key = jax.random.key(seed)
    ks = jax.random.split(key, 24)
    f32 = jnp.float32
    nrm = lambda k, shape, s: jax.random.normal(k, shape, f32) * s
    x = jax.random.normal(ks[0], (BATCH, SEQ, D_MODEL), f32)
    meta_tokens = nrm(ks[1], (N_META, D_MODEL), 1.0)
    norm_mix_g = 1.0 + nrm(ks[2], (DEPTH, D_MODEL), 0.02)
    w_in = nrm(ks[3], (DEPTH, D_MODEL, IN_WIDTH), D_MODEL ** -0.5)
    q_norm_g = 1.0 + nrm(ks[4], (DEPTH, HEAD_DIM), 0.02)
    k_norm_g = 1.0 + nrm(ks[5], (DEPTH, HEAD_DIM), 0.02)
    attn_sinks = nrm(ks[6], (DEPTH, N_HEADS), 0.5)
    n = jnp.arange(SSM_STATE, dtype=f32)
    ssm_lambda_re = -0.5 + nrm(ks[7], (DEPTH, SSM_GROUPS, SSM_STATE), 1e-3)
    ssm_lambda_im = math.pi * n + nrm(ks[8], (DEPTH, SSM_GROUPS, SSM_STATE), 1e-3)
    ssm_log_step = jax.random.uniform(ks[9], (DEPTH, SSM_GROUPS), f32,
                                      math.log(STEP_MIN), math.log(STEP_MAX))
    bs = (SSM_GROUP_CH ** -0.5) / math.sqrt(2.0)
    cs = (SSM_STATE ** -0.5) / math.sqrt(2.0)
    ssm_b_re = nrm(ks[10], (DEPTH, SSM_GROUPS, SSM_STATE, SSM_GROUP_CH), bs)
    ssm_b_im = nrm(ks[11], (DEPTH, SSM_GROUPS, SSM_STATE, SSM_GROUP_CH), bs)
    ssm_c_re = nrm(ks[12], (DEPTH, SSM_GROUPS, SSM_GROUP_CH, SSM_STATE), cs)
    ssm_c_im = nrm(ks[13], (DEPTH, SSM_GROUPS, SSM_GROUP_CH, SSM_STATE), cs)
    ssm_d = nrm(ks[14], (DEPTH, SSM_WIDTH), 1.0)
    w_glu = nrm(ks[15], (DEPTH, SSM_WIDTH, SSM_WIDTH), SSM_WIDTH ** -0.5)
    b_glu = nrm(ks[16], (DEPTH, SSM_WIDTH), 0.01)
    attn_out_g = 1.0 + nrm(ks[17], (DEPTH, ATTN_WIDTH), 0.02)
    ssm_out_g = 1.0 + nrm(ks[18], (DEPTH, SSM_WIDTH), 0.02)
    w_out = nrm(ks[19], (DEPTH, D_MODEL, D_MODEL), D_MODEL ** -0.5)
    norm_mlp_g = 1.0 + nrm(ks[20], (DEPTH, D_MODEL), 0.02)
    w_up = nrm(ks[21], (DEPTH, D_MODEL, D_FF), D_MODEL ** -0.5)
    w_down = nrm(ks[22], (DEPTH, D_FF, D_MODEL), D_FF ** -0.5)
    return {"x": x, "meta_tokens": meta_tokens, "norm_mix_g": norm_mix_g, "w_in": w_in,
            "q_norm_g": q_norm_g, "k_norm_g": k_norm_g, "attn_sinks": attn_sinks,
            "ssm_lambda_re": ssm_lambda_re, "ssm_lambda_im": ssm_lambda_im,
            "ssm_log_step": ssm_log_step, "ssm_b_re": ssm_b_re, "ssm_b_im": ssm_b_im,
            "ssm_c_re": ssm_c_re, "ssm_c_im": ssm_c_im, "ssm_d": ssm_d,
            "w_glu": w_glu, "b_glu": b_glu, "attn_out_g": attn_out_g, "ssm_out_g": ssm_out_g,
            "w_out": w_out, "norm_mlp_g": norm_mlp_g, "w_up": w_up, "w_down": w_down}


def reference(x, meta_tokens, norm_mix_g, w_in, q_norm_g, k_norm_g, attn_sinks,
              ssm_lambda_re, ssm_lambda_im, ssm_log_step, ssm_b_re, ssm_b_im,
              ssm_c_re, ssm_c_im, ssm_d, w_glu, b_glu, attn_out_g, ssm_out_g,
              w_out, norm_mlp_g, w_up, w_down):
    b = x.shape[0]
    meta = jnp.broadcast_to(meta_tokens.astype(x.dtype)[None], (b, N_META, D_MODEL))
    h_res = jnp.concatenate([meta, x], axis=1)
    L = h_res.shape[1]
    for l in range(DEPTH):
        h = _rmsnorm(h_res, norm_mix_g[l])
        proj = h @ w_in[l]
        q = proj[..., :ATTN_WIDTH].reshape(b, L, N_HEADS, HEAD_DIM)
        k = proj[..., ATTN_WIDTH:ATTN_WIDTH + KV_WIDTH].reshape(b, L, N_KV_HEADS, HEAD_DIM)
        v = proj[..., ATTN_WIDTH + KV_WIDTH:ATTN_WIDTH + 2 * KV_WIDTH].reshape(b, L, N_KV_HEADS, HEAD_DIM)
        u = proj[..., ATTN_WIDTH + 2 * KV_WIDTH:]
        q = _rmsnorm(q, q_norm_g[l])
        k = _rmsnorm(k, k_norm_g[l])
        attn = _sliding_window_attention(q, k, v, attn_sinks[l])
        ssm = _s5_mixer(u, ssm_lambda_re[l], ssm_lambda_im[l], ssm_log_step[l],
                        ssm_b_re[l], ssm_b_im[l], ssm_c_re[l], ssm_c_im[l], ssm_d[l],
                        w_glu[l], b_glu[l])
        mix = jnp.concatenate([_rmsnorm(attn, attn_out_g[l]), _rmsnorm(ssm, ssm_out_g[l])], axis=-1)
        h_res = h_res + mix @ w_out[l]
        h2 = _rmsnorm(h_res, norm_mlp_g[l])
        h_res = h_res + jnp.square(jax.nn.relu(h2 @ w_up[l])) @ w_down[l]
    return h_res[:, N_META:]
```

```python
import numpy as np
from contextlib import ExitStack
import concourse.bass as bass
import concourse.mybir as mybir
from concourse.bass_utils import run_bass_kernel_spmd
import ml_dtypes

BF16NP = ml_dtypes.bfloat16

F32 = mybir.dt.float32
BF16 = mybir.dt.bfloat16
AF = mybir.ActivationFunctionType
ALU = mybir.AluOpType

D = 2048
NT = 1040
NMETA = 16
DFF = 8192
EPS = 1e-6
TILES = [(0, 16), (16, 512), (528, 512)]
SLOT = 4096
NSLOT = 3


PFX = [""]


def sbt(nc, name, shape, dt):
    return nc.sbuf_tensor(name + PFX[0], shape, dt)


class Buf:
    __slots__ = ("name", "w", "r")

    def __init__(self, name):
        self.name = name
        self.w = None
        self.r = {}


class Sched:
    def __init__(self, nc, es, strict_same=True, n_dma_sems=8):
        self.nc = nc
        self.E = {"pe": nc.tensor, "act": nc.scalar, "dve": nc.vector, "pool": nc.gpsimd, "sp": nc.sync}
        self.sem, self.cnt, self.pending = {}, {}, {}
        for e in self.E:
            self.sem[e] = es.enter_context(nc.semaphore("s_" + e))
            self.cnt[e] = 0
            self.pending[e] = False
        self.known = {e: {} for e in self.E}
        self.strict_same = strict_same
        self.dsem, self.dcnt, self.drr = {}, {}, {}
        for e in ("sp", "pool"):
            self.dsem[e] = [es.enter_context(nc.semaphore(f"d_{e}{i}")) for i in range(n_dma_sems)]
            self.dcnt[e] = [0] * n_dma_sems
            self.drr[e] = 0
        self.n_wait = 0
        self.n_ins = 0

    def _wait(self, e, tok):
        sem, val, src = tok
        if src == e and (e == "pe" or not self.strict_same):
            return
        k = id(sem)
        if self.known[e].get(k, 0) >= val:
            return
        self.E[e].wait_ge(sem, val)
        self.known[e][k] = val
        self.n_wait += 1

    def _deps(self, e, reads, writes):
        for b in reads:
            if b.w is not None:
                self._wait(e, b.w)
        for b in writes:
            if b.w is not None:
                self._wait(e, b.w)
            for t in b.r.values():
                self._wait(e, t)

    def _mark(self, tok, reads, writes):
        for b in reads:
            b.r[id(tok[0])] = tok
        for b in writes:
            b.w = tok
            b.r = {}

    def op(self, e, emit, reads=(), writes=(), inc=True):
        self._deps(e, reads, writes)
        ins = emit()
        self.n_ins += 1
        if inc:
            self.cnt[e] += 1
            ins.then_inc(self.sem[e], 1)
            self.pending[e] = False
            tok = (self.sem[e], self.cnt[e], e)
        else:
            self.pending[e] = True
            tok = (self.sem[e], self.cnt[e] + 1, e)
        self._mark(tok, reads, writes)
        return ins

    def dma(self, q, out, in_, reads=(), writes=()):
        self._deps(q, reads, writes)
        i = self.drr[q]
        self.drr[q] = (i + 1) % len(self.dsem[q])
        sem = self.dsem[q][i]
        if self.dcnt[q][i] > 0:
            self._wait(q, (sem, 16 * self.dcnt[q][i], "dma"))
        ins = self.E[q].dma_start(out=out, in_=in_)
        self.n_ins += 1
        self.dcnt[q][i] += 1
        ins.then_inc(sem, 16)
        tok = (sem, 16 * self.dcnt[q][i], "dma")
        self._mark(tok, reads, writes)
        return ins

    def barrier_all(self):
        toks = []
        for e in self.E:
            if e == "sp":
                continue
            assert not self.pending[e], e
            if self.cnt[e] > 0:
                toks.append((self.sem[e], self.cnt[e], e))
        for q in self.dsem:
            for i, sem in enumerate(self.dsem[q]):
                if self.dcnt[q][i] > 0:
                    toks.append((sem, 16 * self.dcnt[q][i], "dma"))
        for e in self.E:
            for t in toks:
                if t[2] == e:
                    continue
                self._wait(e, t)


class Ctx:
    def __init__(self, nc, es):
        self.nc = nc
        self.es = es
        self.S = Sched(nc, es)
        S = self.S
        self.slots = [es.enter_context(sbt(nc, f"wslot{i}", [128, SLOT], BF16)) for i in range(NSLOT)]
        self.slotB = [Buf(f"wslot{i}") for i in range(NSLOT)]
        self.slot_rr = 0
        self.ps = [es.enter_context(nc.psum_tensor(f"ps{i}", [128, 512], F32)) for i in range(8)]
        self.psB = [Buf(f"ps{i}") for i in range(8)]
        self.ps_rr = 0
        self.ones_bf = es.enter_context(sbt(nc, "ones_bf", [128, 128], BF16))
        self.onesB = Buf("ones")
        S.op("dve", lambda: nc.vector.memset(self.ones_bf[:], 1.0), writes=[self.onesB])
        self.sq = [es.enter_context(sbt(nc, f"sq{i}", [128, 512], BF16)) for i in range(2)]
        self.sqB = [Buf(f"sq{i}") for i in range(2)]
        self.sq_rr = 0
        self.rstd = es.enter_context(sbt(nc, "rstd", [128, NT], F32))
        self.rstdB = Buf("rstd")
        self.lnt = es.enter_context(sbt(nc, "lnt", [128, 512], F32))
        self.lntB = Buf("lnt")

    def bank(self, lo=0, hi=6):
        n = hi - lo
        i = lo + (self.ps_rr % n)
        self.ps_rr += 1
        return self.ps[i], self.psB[i]

    def load_piece(self, src_ap, nk, ncols):
        i = self.slot_rr % NSLOT
        self.slot_rr += 1
        view = self.slots[i][:, 0:nk * ncols].rearrange("p (k f) -> p k f", k=nk)
        self.S.dma("pool", view, src_ap, writes=[self.slotB[i]])
        return view, self.slotB[i]

    def rmsnorm(self, src, srcB, ndc, gcol, gB, dst, dstB, dst_dc0=0):
        nc, S = self.nc, self.S
        inv = 1.0 / (ndc * 128)
        for (t0, tn) in TILES:
            ps, psB = self.bank(6, 8)
            for dc in range(ndc):
                j = self.sq_rr % 2
                self.sq_rr += 1
                sq, sqB = self.sq[j], self.sqB[j]
                S.op("act", lambda: nc.scalar.activation(out=sq[:, 0:tn], in_=src[:, dc, t0:t0 + tn], func=AF.Square),
                     reads=[srcB], writes=[sqB])
                S.op("pe", lambda: nc.tensor.matmul(ps[:, 0:tn], self.ones_bf[:], sq[:, 0:tn], start=(dc == 0), stop=(dc == ndc - 1)),
                     reads=[sqB, self.onesB], writes=[psB], inc=True)
            S.op("act", lambda: nc.scalar.activation(out=self.lnt[:, 0:tn], in_=ps[:, 0:tn], func=AF.Ln, scale=inv, bias=self.epsc[:, 0:1]),
                 reads=[psB, self.epsB], writes=[self.lntB])
            S.op("act", lambda: nc.scalar.activation(out=self.rstd[:, t0:t0 + tn], in_=self.lnt[:, 0:tn], func=AF.Exp, scale=-0.5),
                 reads=[self.lntB], writes=[self.rstdB])
        for dc in range(ndc):
            S.op("dve", lambda: nc.vector.scalar_tensor_tensor(out=dst[:, dst_dc0 + dc, :], in0=src[:, dc, :], scalar=gcol[:, dc:dc + 1],
                                                               in1=self.rstd[:, :], op0=ALU.mult, op1=ALU.mult),
                 reads=[srcB, gB, self.rstdB], writes=[dstB])

    def consts(self):
        nc, S = self.nc, self.S
        self.epsc = self.es.enter_context(sbt(nc, "epsc", [128, 1], F32))
        self.epsB = Buf("epsc")
        S.op("dve", lambda: nc.vector.memset(self.epsc[:], EPS), writes=[self.epsB])


def wpiece(w_ap, k0, nk, c0, ncols):
    return w_ap.rearrange("(kc p) f -> p kc f", p=128)[:, k0:k0 + nk, c0:c0 + ncols]


def dense_acc_into_x(C, w_ap, nkc, row0_chunk, act, actB, xT, xB, m_chunks, piece_cols):
    nc, S = C.nc, C.S
    per = piece_cols // 128
    for p0 in range(0, m_chunks, per):
        view, vB = C.load_piece(wpiece(w_ap, row0_chunk, nkc, p0 * 128, piece_cols), nkc, piece_cols)
        for mm in range(per):
            m = p0 + mm
            for (t0, tn) in TILES:
                ps, psB = C.bank()
                for k in range(nkc):
                    S.op("pe", lambda: nc.tensor.matmul(ps[:, 0:tn], view[:, k, mm * 128:(mm + 1) * 128], act[:, k, t0:t0 + tn],
                                                        start=(k == 0), stop=(k == nkc - 1)),
                         reads=[vB, actB], writes=[psB], inc=(k == nkc - 1))
                S.op("dve", lambda: nc.vector.tensor_tensor(out=xT[:, m, t0:t0 + tn], in0=ps[:, 0:tn], in1=xT[:, m, t0:t0 + tn], op=ALU.add),
                     reads=[psB, xB], writes=[xB])


def ffn(C, w_up, w_down, h2T, h2B, xT, xB, hid, hidB, tmp, tmpB):
    nc, S = C.nc, C.S
    NFG = DFF // 1024
    for fg in range(NFG):
        hb, hbB = hid[fg % 2], hidB[fg % 2]
        for pc in range(4):
            view, vB = C.load_piece(wpiece(w_up, 0, 16, fg * 1024 + pc * 256, 256), 16, 256)
            for mm in range(2):
                j = pc * 2 + mm
                for (t0, tn) in TILES:
                    ps, psB = C.bank()
                    for k in range(16):
                        S.op("pe", lambda: nc.tensor.matmul(ps[:, 0:tn], view[:, k, mm * 128:(mm + 1) * 128], h2T[:, k, t0:t0 + tn],
                                                            start=(k == 0), stop=(k == 15)),
                             reads=[vB, h2B], writes=[psB], inc=(k == 15))
                    tb = C.tmp_rr % 2
                    C.tmp_rr += 1
                    S.op("act", lambda: nc.scalar.activation(out=tmp[tb][:, 0:tn], in_=ps[:, 0:tn], func=AF.Relu),
                         reads=[psB], writes=[tmpB[tb]])
                    S.op("dve", lambda: nc.vector.tensor_tensor(out=hb[:, j, t0:t0 + tn], in0=tmp[tb][:, 0:tn], in1=tmp[tb][:, 0:tn], op=ALU.mult),
                         reads=[tmpB[tb]], writes=[hbB])
        dense_acc_into_x(C, w_down, 8, fg * 8, hb, hbB, xT, xB, 16, 512)


KTW = 1184
SLOPES = [2.0 ** (-(h + 1) / 2.0) for h in range(16)]


def dup_cols(ap2, reps):
    a = [list(x) for x in ap2.ap]
    return bass.AP(ap2.tensor, ap2.offset, [a[0], [0, reps]] + a[1:])


def attn_consts(C, es, dband_in, dmeta_in, dqm_in, bias_in, garg_in, valid_in, gq_in, gk_in):
    nc, S = C.nc, C.S
    A = type("A", (), {})()
    def ld(name, shape, src):
        t = es.enter_context(sbt(nc, "sb_" + name, shape, F32))
        b = Buf(name)
        S.dma("sp", t[:], src, writes=[b])
        return t, b
    A.dband, A.dbandB = ld("dband", [128, 2, 128], dband_in)
    A.dmeta, A.dmetaB = ld("dmeta", [128, 128], dmeta_in)
    A.dqm, A.dqmB = ld("dqm", [128, 16], dqm_in)
    A.bias, A.biasB = ld("abias", [128, 16], bias_in)
    A.garg, A.gargB = ld("garg", [128, 128], garg_in)
    A.valid, A.validB = ld("valid", [128, 1], valid_in)
    A.gq, A.gqB = ld("gq2", [128, 1], gq_in)
    A.gk, A.gkB = ld("gk2", [128, 1], gk_in)
    S.op("act", lambda: nc.scalar.activation(out=A.garg[:], in_=A.garg[:], func=AF.Exp), reads=[A.gargB], writes=[A.gargB])
    A.blk = es.enter_context(sbt(nc, "blkones", [128, 128], BF16))
    A.blkB = Buf("blkones")
    S.op("dve", lambda: nc.vector.memset(A.blk[:], 0.0), writes=[A.blkB])
    S.op("dve", lambda: nc.vector.memset(A.blk[0:64, 0:64], 1.0), writes=[A.blkB])
    S.op("dve", lambda: nc.vector.memset(A.blk[64:128, 64:128], 1.0), writes=[A.blkB])
    return A


def headnorm_evac(C, A, ps, psB, tn, gcol, gB, out_ap, outB, tmpq, tmpqB):
    nc, S = C.nc, C.S
    j = C.sq_rr % 2
    C.sq_rr += 1
    sq, sqB = C.sq[j], C.sqB[j]
    S.op("act", lambda: nc.scalar.activation(out=sq[:, 0:tn], in_=ps[:, 0:tn], func=AF.Square), reads=[psB], writes=[sqB])
    st, stB = C.ps[2], C.psB[2]
    S.op("pe", lambda: nc.tensor.matmul(st[:, 0:tn], A.blk[:], sq[:, 0:tn], start=True, stop=True), reads=[sqB, A.blkB], writes=[stB])
    S.op("act", lambda: nc.scalar.activation(out=C.lnt[:, 0:tn], in_=st[:, 0:tn], func=AF.Ln, scale=1.0 / 64, bias=C.epsc[:, 0:1]),
         reads=[stB, C.epsB], writes=[C.lntB])
    S.op("act", lambda: nc.scalar.activation(out=tmpq[:, 0:tn], in_=C.lnt[:, 0:tn], func=AF.Exp, scale=-0.5), reads=[C.lntB], writes=[tmpqB])
    if isinstance(out_ap, tuple):
        for (r0, oap) in ((0, out_ap[0]), (64, out_ap[1])):
            S.op("dve", lambda: nc.vector.scalar_tensor_tensor(out=oap, in0=ps[r0:r0 + 64, 0:tn], scalar=gcol[r0:r0 + 64, 0:1], in1=tmpq[r0:r0 + 64, 0:tn],
                                                               op0=ALU.mult, op1=ALU.mult),
                 reads=[psB, gB, tmpqB], writes=[outB])
    else:
        S.op("dve", lambda: nc.vector.scalar_tensor_tensor(out=out_ap, in0=ps[:, 0:tn], scalar=gcol[:, 0:1], in1=tmpq[:, 0:tn], op0=ALU.mult, op1=ALU.mult),
             reads=[psB, gB, tmpqB], writes=[outB])


def kv_proj(C, A, w_in, hT, hB, kT, kTB, V2, V2B, kth_in, vh_in, tmpq, tmpqB, ktiles=None, vblocks=None, skip_init=False):
    nc, S = C.nc, C.S
    if not skip_init:
        S.op("dve", lambda: nc.vector.memset(kT[0][:], 0.0), writes=[kTB])
        S.op("dve", lambda: nc.vector.memset(kT[1][:], 0.0), writes=[kTB])
        S.op("dve", lambda: nc.vector.memset(V2[:, 0, :], 0.0), writes=[V2B])
    if kth_in is not None:
        S.dma("sp", kT[0][0:64, :, 32:160], kth_in[0:64], writes=[kTB])
        S.dma("sp", kT[1][64:128, :, 32:160], kth_in[64:128], writes=[kTB])
        S.dma("sp", V2[:, 1, :], vh_in, writes=[V2B])
    view, vB = C.load_piece(wpiece(w_in, 0, 16, 1024, 256), 16, 256)
    for kv in range(4):
        for (t0, tn) in (ktiles or TILES):
            ps, psB = C.bank(0, 2)
            for half in range(2):
                for k in range(16):
                    lhsT = view[:, k, kv * 64:(kv + 1) * 64]
                    S.op("pe", lambda: nc.tensor.matmul(ps[half * 64:(half + 1) * 64, 0:tn], lhsT, hT[:, k, t0:t0 + tn], start=(k == 0), stop=(k == 15),
                                                        tile_position=(0, half * 64)),
                         reads=[vB, hB], writes=[psB], inc=(k == 15))
            c0 = t0 if t0 < 16 else t0 + 144
            headnorm_evac(C, A, ps, psB, tn, A.gk, A.gkB, (kT[0][0:64, kv, c0:c0 + tn], kT[1][64:128, kv, c0:c0 + tn]), kTB, tmpq, tmpqB)
    view, vB = C.load_piece(wpiece(w_in, 0, 16, 1280, 256), 16, 256)
    blocks = vblocks or ([(0, 16, 0)] + [(16 + 128 * j, 128, 2 + j) for j in range(8)])
    for (t0, tn, idx) in blocks:
        ps, psB = C.bank(0, 2)
        for k in range(16):
            r = view[:, k, :]
            a = [list(x) for x in r.ap]
            rhs = bass.AP(r.tensor, r.offset, [a[0], [64, 4], [0, 2], [1, 64]])
            S.op("pe", lambda: nc.tensor.matmul(ps[0:tn, :], hT[:, k, t0:t0 + tn], rhs, start=(k == 0), stop=(k == 15)),
                 reads=[vB, hB], writes=[psB], inc=(k == 15))
        S.op("act", lambda: nc.scalar.copy(out=V2[0:tn, idx, :], in_=ps[0:tn, :]), reads=[psB], writes=[V2B])


def attention(C, A, es, w_in, hT, hB, kT, kTB, V2, V2B, aT, aTB, tmpq, tmpqB, nkv=4, jbs=range(-1, 8), skip_prev0=False, use_valid=True):
    nc, S = C.nc, C.S
    def sb(name, shape, dt):
        return es.enter_context(sbt(nc, name, shape, dt)), Buf(name)
    q2, q2B = sb("q2", [128, 2, NT], BF16)
    wtab, wtabB = sb("wtab", [128, 2, 512], F32)
    wmeta, wmetaB = sb("wmeta", [128, 512], F32)
    wqm, wqmB = sb("wqm", [128, 64], F32)
    expS = [sb(f"expS{i}", [128, 512], F32) for i in range(2)]
    pt = [sb(f"pt{i}", [128, 512], BF16) for i in range(3)]
    rec, recB = sb("rec", [128, 512], F32)
    exp_rr = 0
    for kv in range(nkv):
        view, vB = C.load_piece(wpiece(w_in, 0, 16, kv * 256, 256), 16, 256)
        for mm in range(2):
            for (t0, tn) in TILES:
                ps, psB = C.bank(0, 2)
                for k in range(16):
                    S.op("pe", lambda: nc.tensor.matmul(ps[:, 0:tn], view[:, k, mm * 128:(mm + 1) * 128], hT[:, k, t0:t0 + tn],
                                                        start=(k == 0), stop=(k == 15)),
                         reads=[vB, hB], writes=[psB], inc=(k == 15))
                headnorm_evac(C, A, ps, psB, tn, A.gq, A.gqB, q2[:, mm, t0:t0 + tn], q2B, tmpq, tmpqB)
        for hh in range(4):
            h = 4 * kv + hh
            for tl in range(2):
                S.op("act", lambda: nc.scalar.activation(out=wtab[:, tl, hh * 128:(hh + 1) * 128], in_=A.dband[:, tl, :], func=AF.Exp, scale=-SLOPES[h]),
                     reads=[A.dbandB], writes=[wtabB])
            S.op("act", lambda: nc.scalar.activation(out=wmeta[:, hh * 128:(hh + 1) * 128], in_=A.dmeta[:, :], func=AF.Exp, scale=-SLOPES[h], bias=A.bias[:, h:h + 1]),
                 reads=[A.dmetaB, A.biasB], writes=[wmetaB])
            S.op("act", lambda: nc.scalar.activation(out=wqm[:, hh * 16:(hh + 1) * 16], in_=A.dqm[:, :], func=AF.Exp, scale=-SLOPES[h], bias=A.bias[:, h:h + 1]),
                 reads=[A.dqmB, A.biasB], writes=[wqmB])
        for jb in jbs:
            nq = 16 if jb < 0 else 128
            tq0 = 0 if jb < 0 else 16 + 128 * jb
            W4 = 4 * nq
            tiles = []
            if jb >= 0 and not (skip_prev0 and jb == 0):
                tiles.append((3, 32 + 128 * jb, 128, 1 + jb, "prev"))
            if jb >= 0:
                tiles.append((4, 160 + 128 * jb, 128, 2 + jb, "cur"))
            tiles.append((5, 0, 128, 0, "meta"))
            pts = []
            for ti, (bk, kc0, K, vidx, kind) in enumerate(tiles):
                ps, psB = C.ps[bk], C.psB[bk]
                for hh in range(4):
                    mm, half = hh // 2, hh % 2
                    S.op("pe", lambda: nc.tensor.matmul(ps[0:K, hh * nq:(hh + 1) * nq], kT[half][:, kv, kc0:kc0 + K], q2[:, mm, tq0:tq0 + nq],
                                                        start=True, stop=True),
                         reads=[kTB, q2B], writes=[psB], inc=(hh == 3))
                ex, exB = expS[exp_rr % 2]
                exp_rr += 1
                S.op("act", lambda: nc.scalar.activation(out=ex[0:K, 0:W4], in_=ps[0:K, 0:W4], func=AF.Exp, scale=0.125), reads=[psB], writes=[exB])
                p, pB = pt[ti]
                if kind == "prev":
                    if jb == 0 and use_valid:
                        S.op("dve", lambda: nc.vector.scalar_tensor_tensor(out=p[:, 0:512], in0=ex[:, 0:512], scalar=A.valid[:, 0:1], in1=wtab[:, 0, :], op0=ALU.mult, op1=ALU.mult),
                             reads=[exB, A.validB, wtabB], writes=[pB])
                    else:
                        S.op("dve", lambda: nc.vector.tensor_tensor(out=p[:, 0:512], in0=ex[:, 0:512], in1=wtab[:, 0, :], op=ALU.mult), reads=[exB, wtabB], writes=[pB])
                elif kind == "cur":
                    S.op("dve", lambda: nc.vector.tensor_tensor(out=p[:, 0:512], in0=ex[:, 0:512], in1=wtab[:, 1, :], op=ALU.mult), reads=[exB, wtabB], writes=[pB])
                else:
                    if jb < 0:
                        S.op("dve", lambda: nc.vector.tensor_tensor(out=p[:, 0:64], in0=ex[:, 0:64], in1=wqm[:, :], op=ALU.mult), reads=[exB, wqmB], writes=[pB])
                    else:
                        for hh in range(4):
                            h = 4 * kv + hh
                            S.op("dve", lambda: nc.vector.scalar_tensor_tensor(out=p[:, hh * 128:(hh + 1) * 128], in0=ex[:, hh * 128:(hh + 1) * 128],
                                                                               scalar=A.garg[:, h * 8 + jb:h * 8 + jb + 1], in1=wmeta[:, hh * 128:(hh + 1) * 128],
                                                                               op0=ALU.mult, op1=ALU.mult),
                                 reads=[exB, A.gargB, wmetaB], writes=[pB])
                pts.append((p, pB, K, vidx))
            num, numB = C.ps[6], C.psB[6]
            den, denB = C.ps[7], C.psB[7]
            for ti, (p, pB, K, vidx) in enumerate(pts):
                S.op("pe", lambda: nc.tensor.matmul(num[:, 0:W4], V2[0:K, vidx, kv * 128:(kv + 1) * 128], p[0:K, 0:W4], start=(ti == 0), stop=(ti == len(pts) - 1)),
                     reads=[V2B, pB], writes=[numB], inc=True)
                S.op("pe", lambda: nc.tensor.matmul(den[:, 0:W4], C.ones_bf[0:K, :], p[0:K, 0:W4], start=(ti == 0), stop=(ti == len(pts) - 1)),
                     reads=[C.onesB, pB], writes=[denB], inc=True)
            S.op("dve", lambda: nc.vector.reciprocal(out=rec[:, 0:W4], in_=den[:, 0:W4]), reads=[denB], writes=[recB])
            for hh in range(4):
                mm, half = hh // 2, hh % 2
                r0 = half * 64
                c = 2 * kv + mm
                S.op("dve", lambda: nc.vector.tensor_tensor(out=aT[r0:r0 + 64, c, tq0:tq0 + nq], in0=num[r0:r0 + 64, hh * nq:(hh + 1) * nq],
                                                            in1=rec[r0:r0 + 64, hh * nq:(hh + 1) * nq], op=ALU.mult),
                     reads=[numB, recB], writes=[aTB])


def build_attn_test(stage=2, nkv=4, jbs=range(-1, 8)):
    nc = bass.Bass("TRN2", target_bir_lowering=False)
    def din(name, shape, dt=F32):
        return nc.dram_tensor(name, shape, dt, kind="ExternalInput").ap()
    xin = din("xT_in", [128, 16, NT])
    w_in = din("w_in", [D, 2560])
    gmix = din("g_mix", [128, 16])
    dband, dmeta, dqm = din("dband", [128, 2, 128]), din("dmeta", [128, 128]), din("dqm", [128, 16])
    bias_in, garg_in, valid_in = din("abias", [128, 16]), din("garg", [128, 128]), din("valid", [128, 1])
    gq_in, gk_in = din("gq2", [128, 1]), din("gk2", [128, 1])
    kth_in, vh_in = din("kth", [128, 4, 128], BF16), din("vh", [128, 512], BF16)
    aout = nc.dram_tensor("aT_out", [128, 8, NT], F32, kind="ExternalOutput").ap()
    kout = nc.dram_tensor("kT_out", [128, 4, KTW], BF16, kind="ExternalOutput").ap()
    kout2 = nc.dram_tensor("kT_out2", [128, 4, KTW], BF16, kind="ExternalOutput").ap()
    vout = nc.dram_tensor("V2_out", [128, 10, 512], BF16, kind="ExternalOutput").ap()
    with ExitStack() as es:
        C = Ctx(nc, es)
        S = C.S
        C.consts()
        xT = es.enter_context(sbt(nc, "xT", [128, 16, NT], F32)); xB = Buf("xT")
        regH = es.enter_context(sbt(nc, "regH", [128, 16, NT], BF16)); hB = Buf("regH")
        gm = es.enter_context(sbt(nc, "gm", [128, 16], F32)); gmB = Buf("gm")
        kT = [es.enter_context(sbt(nc, f"kT{i}", [128, 4, KTW], BF16)) for i in range(2)]; kTB = Buf("kT")
        V2 = es.enter_context(sbt(nc, "V2", [128, 10, 512], BF16)); V2B = Buf("V2")
        aT = es.enter_context(sbt(nc, "aT", [128, 8, NT], F32)); aTB = Buf("aT")
        tmpq = es.enter_context(sbt(nc, "tmpq", [128, 512], F32)); tmpqB = Buf("tmpq")
        S.dma("sp", xT[:], xin, writes=[xB])
        S.dma("sp", gm[:], gmix, writes=[gmB])
        A = attn_consts(C, es, dband, dmeta, dqm, bias_in, garg_in, valid_in, gq_in, gk_in)
        C.rmsnorm(xT, xB, 16, gm, gmB, regH, hB)
        kv_proj(C, A, w_in, regH, hB, kT, kTB, V2, V2B, kth_in, vh_in, tmpq, tmpqB)
        S.dma("sp", kout, kT[0][:], reads=[kTB], writes=[Buf("kout")])
        S.dma("sp", kout2, kT[1][:], reads=[kTB], writes=[Buf("kout2")])
        S.dma("sp", vout, V2[:], reads=[V2B], writes=[Buf("vout")])
        with ExitStack() as es2:
            if stage >= 2:
                attention(C, A, es2, w_in, regH, hB, kT, kTB, V2, V2B, aT, aTB, tmpq, tmpqB, nkv, jbs)
                S.dma("sp", aout, aT[:], reads=[aTB], writes=[Buf("aout")])
            S.barrier_all()
        print("instructions", S.n_ins, "waits", S.n_wait)
    return nc


def attn_host_consts(q):
    BIG = 1.0e6
    i = np.arange(128)[None, :]
    s = np.arange(128)[:, None]
    dband = np.zeros((128, 2, 128), np.float32)
    dprev = (i - s + 128).astype(np.float32)
    dband[:, 0, :] = np.where(s > i, dprev, BIG)
    dband[:, 1, :] = np.where(s <= i, (i - s).astype(np.float32), BIG)
    dmeta = np.full((128, 128), BIG, np.float32)
    m = np.arange(16)[:, None]
    dmeta[:16] = (i - m + 16)
    dmeta[16] = 0.0
    dqm = np.full((128, 16), BIG, np.float32)
    t = np.arange(16)[None, :]
    dqm[:16] = np.where(m <= t, (t - m).astype(np.float32), BIG)
    dqm[16] = 0.0
    garg = np.zeros((128, 16, 8), np.float32)
    for h in range(16):
        for j in range(8):
            garg[:16, h, j] = -SLOPES[h] * (1024 * q + 128 * j)
    valid = np.full((128, 1), 1.0 if q > 0 else 0.0, np.float32)
    return dict(dband=dband, dmeta=dmeta, dqm=dqm, garg=garg.reshape(128, 128), valid=valid)


def sink_bias(sinks):
    b = np.zeros((128, 16), np.float32)
    b[16, :] = sinks
    return b


TWO_PI = float(2 * np.pi)
MAGIC = 12582912.0
NTAB = 1041


class T:
    def __init__(self, nc, es, name, shape, dt=F32):
        self.t = es.enter_context(sbt(nc, name, shape, dt))
        self.B = Buf(name)


def v_tt(S, nc, e, out, a, b, op, reads, writes):
    eng = nc.vector if e == "dve" else nc.gpsimd
    S.op(e, lambda: eng.tensor_tensor(out=out, in0=a, in1=b, op=op), reads=reads, writes=writes)


def v_ts(S, nc, e, out, a, s1, s2, op0, op1, reads, writes):
    eng = nc.vector if e == "dve" else nc.gpsimd
    if s2 is None:
        S.op(e, lambda: eng.tensor_scalar(out=out, in0=a, scalar1=s1, scalar2=None, op0=op0), reads=reads, writes=writes)
    else:
        S.op(e, lambda: eng.tensor_scalar(out=out, in0=a, scalar1=s1, scalar2=s2, op0=op0, op1=op1), reads=reads, writes=writes)


def range_reduce(S, nc, x, xB, k, kB):
    v_ts(S, nc, "dve", k, x, 1.0 / TWO_PI, MAGIC, ALU.mult, ALU.add, [xB], [kB])
    v_ts(S, nc, "dve", k, k, MAGIC, -TWO_PI, ALU.subtract, ALU.mult, [kB], [kB])
    v_tt(S, nc, "dve", x, x, k, ALU.add, [xB, kB], [xB])


def sincos(C, x, xB, k, kB, sin_out, sinB, cos_out, cosB):
    nc, S = C.nc, C.S
    range_reduce(S, nc, x, xB, k, kB)
    S.op("act", lambda: nc.scalar.activation(out=sin_out, in_=x, func=AF.Sin), reads=[xB], writes=[sinB])
    S.op("act", lambda: nc.scalar.activation(out=k, in_=x, func=AF.Abs), reads=[xB], writes=[kB])
    S.op("act", lambda: nc.scalar.activation(out=cos_out, in_=k, func=AF.Sin, scale=-1.0, bias=C.halfpi[:, 0:1]), reads=[kB, C.hpB], writes=[cosB])


def ssm_prep(C, es, lamS_in, lamB_in, bB_in, cS_in, tvals_in, mask8_in, sgn_in):
    nc, S = C.nc, C.S
    P = type("P", (), {})()
    def ld(name, shape, src):
        t = T(nc, es, "p_" + name, shape)
        S.dma("sp", t.t[:], src, writes=[t.B])
        return t
    C.halfpi = es.enter_context(sbt(nc, "halfpi", [128, 1], F32))
    C.hpB = Buf("halfpi")
    S.op("dve", lambda: nc.vector.memset(C.halfpi[:], float(np.pi / 2)), writes=[C.hpB])
    P.tvals = ld("tvals", [128, NTAB], tvals_in)
    P.mask8 = ld("mask8", [128, 8], mask8_in)
    P.sgn = ld("sgn", [128, 1], sgn_in)
    lamS = ld("lamS", [128, 3, 64], lamS_in)
    P.thS = T(nc, es, "thS", [128, 64])
    P.rhoS = T(nc, es, "rhoS", [128, 64])
    P.lrS = T(nc, es, "lrS", [128, 64])
    P.BA = T(nc, es, "BAfull", [128, 8, 128])
    P.BB = T(nc, es, "BBfull", [128, 8, 128])
    P.CA = T(nc, es, "CAall", [128, 64, 16])
    P.CB = T(nc, es, "CBall", [128, 64, 16])
    dS = T(nc, es, "dS", [128, 64])
    S.op("act", lambda: nc.scalar.activation(out=dS.t[:], in_=lamS.t[:, 2, :], func=AF.Exp), reads=[lamS.B], writes=[dS.B])
    v_tt(S, nc, "dve", P.lrS.t[:], lamS.t[:, 0, :], dS.t[:], ALU.mult, [lamS.B, dS.B], [P.lrS.B])
    v_tt(S, nc, "dve", P.thS.t[:], lamS.t[:, 1, :], dS.t[:], ALU.mult, [lamS.B, dS.B], [P.thS.B])
    S.op("act", lambda: nc.scalar.activation(out=P.rhoS.t[:], in_=P.lrS.t[:], func=AF.Exp), reads=[P.lrS.B], writes=[P.rhoS.B])
    with ExitStack() as es2:
        cS = T(nc, es2, "p_cS", [128, 2, 64, 16])
        S.dma("sp", cS.t[:], cS_in, writes=[cS.B])
        S.op("dve", lambda: nc.vector.tensor_copy(out=P.CA.t[0:64], in_=cS.t[0:64, 0]), reads=[cS.B], writes=[P.CA.B])
        v_ts(S, nc, "dve", P.CA.t[64:128], cS.t[64:128, 1], -1.0, None, ALU.mult, None, [cS.B], [P.CA.B])
        v_ts(S, nc, "dve", P.CB.t[0:64], cS.t[0:64, 1], -1.0, None, ALU.mult, None, [cS.B], [P.CB.B])
        v_ts(S, nc, "dve", P.CB.t[64:128], cS.t[64:128, 0], -1.0, None, ALU.mult, None, [cS.B], [P.CB.B])
        lamB = T(nc, es2, "p_lamB", [128, 3, 512])
        S.dma("sp", lamB.t[:], lamB_in, writes=[lamB.B])
        bB = T(nc, es2, "p_bB", [128, 2, 512])
        S.dma("sp", bB.t[:], bB_in, writes=[bB.B])
        tm = [T(nc, es2, f"p_tm{i}", [128, 512]) for i in range(8)]
        dB, lr, th, kk, sn, cs, mg, t7 = tm
        lre, lim = lamB.t[:, 0, :], lamB.t[:, 1, :]
        S.op("act", lambda: nc.scalar.activation(out=dB.t[:], in_=lamB.t[:, 2, :], func=AF.Exp), reads=[lamB.B], writes=[dB.B])
        v_tt(S, nc, "dve", lr.t[:], lre, dB.t[:], ALU.mult, [lamB.B, dB.B], [lr.B])
        v_tt(S, nc, "dve", th.t[:], lim, dB.t[:], ALU.mult, [lamB.B, dB.B], [th.B])
        S.op("act", lambda: nc.scalar.activation(out=mg.t[:], in_=lr.t[:], func=AF.Exp), reads=[lr.B], writes=[mg.B])
        sincos(C, th.t[:], th.B, kk.t[:], kk.B, sn.t[:], sn.B, cs.t[:], cs.B)
        v_tt(S, nc, "dve", cs.t[:], cs.t[:], mg.t[:], ALU.mult, [cs.B, mg.B], [cs.B])
        v_ts(S, nc, "dve", cs.t[:], cs.t[:], -1.0, None, ALU.add, None, [cs.B], [cs.B])
        v_tt(S, nc, "dve", sn.t[:], sn.t[:], mg.t[:], ALU.mult, [sn.B, mg.B], [sn.B])
        a, b = cs, sn
        v_tt(S, nc, "dve", mg.t[:], lre, lre, ALU.mult, [lamB.B], [mg.B])
        v_tt(S, nc, "dve", kk.t[:], lim, lim, ALU.mult, [lamB.B], [kk.B])
        v_tt(S, nc, "dve", mg.t[:], mg.t[:], kk.t[:], ALU.add, [mg.B, kk.B], [mg.B])
        S.op("dve", lambda: nc.vector.reciprocal(out=mg.t[:], in_=mg.t[:]), reads=[mg.B], writes=[mg.B])
        v_tt(S, nc, "dve", lr.t[:], a.t[:], lre, ALU.mult, [a.B, lamB.B], [lr.B])
        v_tt(S, nc, "dve", kk.t[:], b.t[:], lim, ALU.mult, [b.B, lamB.B], [kk.B])
        v_tt(S, nc, "dve", lr.t[:], lr.t[:], kk.t[:], ALU.add, [lr.B, kk.B], [lr.B])
        v_tt(S, nc, "dve", lr.t[:], lr.t[:], mg.t[:], ALU.mult, [lr.B, mg.B], [lr.B])
        v_tt(S, nc, "dve", th.t[:], b.t[:], lre, ALU.mult, [b.B, lamB.B], [th.B])
        v_tt(S, nc, "dve", kk.t[:], a.t[:], lim, ALU.mult, [a.B, lamB.B], [kk.B])
        v_tt(S, nc, "dve", th.t[:], th.t[:], kk.t[:], ALU.subtract, [th.B, kk.B], [th.B])
        v_tt(S, nc, "dve", th.t[:], th.t[:], mg.t[:], ALU.mult, [th.B, mg.B], [th.B])
        cr, ci = lr, th
        bre, bim = bB.t[:, 0, :], bB.t[:, 1, :]
        v_tt(S, nc, "dve", dB.t[:], cr.t[:], bre, ALU.mult, [cr.B, bB.B], [dB.B])
        v_tt(S, nc, "dve", kk.t[:], ci.t[:], bim, ALU.mult, [ci.B, bB.B], [kk.B])
        v_tt(S, nc, "dve", dB.t[:], dB.t[:], kk.t[:], ALU.subtract, [dB.B, kk.B], [dB.B])
        v_tt(S, nc, "dve", t7.t[:], cr.t[:], bim, ALU.mult, [cr.B, bB.B], [t7.B])
        v_tt(S, nc, "dve", kk.t[:], ci.t[:], bre, ALU.mult, [ci.B, bB.B], [kk.B])
        v_tt(S, nc, "dve", t7.t[:], t7.t[:], kk.t[:], ALU.add, [t7.B, kk.B], [t7.B])
        bbr = dB.t[:].rearrange("p (c n) -> p c n", c=8)
        bbi = t7.t[:].rearrange("p (c n) -> p c n", c=8)
        S.op("dve", lambda: nc.vector.tensor_copy(out=P.BA.t[:, :, 0:64], in_=bbr), reads=[dB.B], writes=[P.BA.B])
        S.op("dve", lambda: nc.vector.tensor_copy(out=P.BA.t[:, :, 64:128], in_=bbi), reads=[t7.B], writes=[P.BA.B])
        S.op("dve", lambda: nc.vector.tensor_copy(out=P.BB.t[:, :, 0:64], in_=bbi), reads=[t7.B], writes=[P.BB.B])
        v_ts(S, nc, "dve", P.BB.t[:, :, 64:128], bbr, -1.0, None, ALU.mult, None, [dB.B], [P.BB.B])
        C.S.barrier_all()
    return P


def ssm_xstart(C, es, P, nat_in, swp_in):
    nc, S = C.nc, C.S
    xs = T(nc, es, "xstart", [128, 64])
    with ExitStack() as es2:
        nat = T(nc, es2, "x_nat", [128, 4, 64]); S.dma("sp", nat.t[:], nat_in, writes=[nat.B])
        swp = T(nc, es2, "x_swp", [128, 4, 64]); S.dma("sp", swp.t[:], swp_in, writes=[swp.B])
        mg, an, kk, sn, cs, t1 = [T(nc, es2, f"x_t{i}", [128, 64]) for i in range(6)]
        S.op("dve", lambda: nc.vector.memset(xs.t[:], 0.0), writes=[xs.B])
        for p in range(4):
            S.op("act", lambda: nc.scalar.activation(out=mg.t[:], in_=P.lrS.t[:], func=AF.Exp, scale=float(1024 * p)), reads=[P.lrS.B], writes=[mg.B])
            v_ts(S, nc, "dve", an.t[:], P.thS.t[:], float(1024 * (p + 1)), None, ALU.mult, None, [P.thS.B], [an.B])
            sincos(C, an.t[:], an.B, kk.t[:], kk.B, sn.t[:], sn.B, cs.t[:], cs.B)
            v_tt(S, nc, "dve", cs.t[:], cs.t[:], mg.t[:], ALU.mult, [cs.B, mg.B], [cs.B])
            v_tt(S, nc, "dve", sn.t[:], sn.t[:], mg.t[:], ALU.mult, [sn.B, mg.B], [sn.B])
            v_ts(S, nc, "dve", sn.t[:], sn.t[:], P.sgn.t[:, 0:1], None, ALU.mult, None, [sn.B, P.sgn.B], [sn.B])
            v_tt(S, nc, "dve", t1.t[:], cs.t[:], nat.t[:, p, :], ALU.mult, [cs.B, nat.B], [t1.B])
            v_tt(S, nc, "dve", xs.t[:], xs.t[:], t1.t[:], ALU.add, [xs.B, t1.B], [xs.B])
            v_tt(S, nc, "dve", t1.t[:], sn.t[:], swp.t[:, p, :], ALU.mult, [sn.B, swp.B], [t1.B])
            v_tt(S, nc, "dve", xs.t[:], xs.t[:], t1.t[:], ALU.add, [xs.B, t1.B], [xs.B])
        C.S.barrier_all()
    return xs


def tslice(t0, tn, cont=False):
    if cont:
        return (t0 + 1, tn)
    return (1009, 16) if t0 == 0 else (t0 - 15, tn)


def ssm_loop(C, es, P, uT, uB, big, mode, xs=None, yT=None, yB=None, dcol=None, wend=None, cont=False):
    nc, S = C.nc, C.S
    (cosT, cosB), (sinT, sinB), (xa, xaB), (ka, kaB), (Vb, VB), (Wb, WB) = big
    t1 = [T(nc, es, f"s_t1{i}", [128, 512]) for i in range(2)]
    zs = [T(nc, es, f"s_zs{i}", [128, 512]) for i in range(2)]
    t2 = [T(nc, es, f"s_t2{i}", [128, 512]) for i in range(2)]
    bpA = [T(nc, es, f"s_bpA{i}", [128, 128], BF16) for i in range(2)]
    bpB = [T(nc, es, f"s_bpB{i}", [128, 128], BF16) for i in range(2)]
    if mode != "A":
        A1 = T(nc, es, "s_A1", [128, NT], BF16)
        A2 = T(nc, es, "s_A2", [128, NT], BF16)
        wbA = [T(nc, es, f"s_wbA{i}", [128, 240], BF16) for i in range(2)]
        wbB = [T(nc, es, f"s_wbB{i}", [128, 240], BF16) for i in range(2)]
        for w in wbA + wbB:
            S.op("dve", lambda: nc.vector.memset(w.t[:], 0.0), writes=[w.B])
    rr = 0
    zr = 0
    for g in range(64):
        fc, gl = g // 8, g % 8
        i2 = g % 2
        S.op("pool", lambda: nc.gpsimd.tensor_scalar(out=bpA[i2].t[:], in0=P.BA.t[:, fc, :], scalar1=P.mask8.t[:, gl:gl + 1], scalar2=None, op0=ALU.mult),
             reads=[P.BA.B, P.mask8.B], writes=[bpA[i2].B])
        S.op("pool", lambda: nc.gpsimd.tensor_scalar(out=bpB[i2].t[:], in0=P.BB.t[:, fc, :], scalar1=P.mask8.t[:, gl:gl + 1], scalar2=None, op0=ALU.mult),
             reads=[P.BB.B, P.mask8.B], writes=[bpB[i2].B])
        S.op("pool", lambda: nc.gpsimd.tensor_scalar(out=xa[:, 0:NTAB], in0=P.tvals.t[:, :], scalar1=P.thS.t[:, g:g + 1], scalar2=None, op0=ALU.mult),
             reads=[P.tvals.B, P.thS.B], writes=[xaB])
        sincos(C, xa[:, 0:NTAB], xaB, ka[:, 0:NTAB], kaB, sinT[:, 0:NTAB], sinB, cosT[:, 0:NTAB], cosB)
        for (t0, tn) in TILES:
            s0, sn_ = tslice(t0, tn, cont)
            pz, pzB = C.ps[zr % 4], C.psB[zr % 4]; zr += 1
            pzs, pzsB = C.ps[zr % 4], C.psB[zr % 4]; zr += 1
            S.op("pe", lambda: nc.tensor.matmul(pz[:, 0:tn], bpA[i2].t[:], uT[:, fc, t0:t0 + tn], start=True, stop=True), reads=[bpA[i2].B, uB], writes=[pzB])
            S.op("pe", lambda: nc.tensor.matmul(pzs[:, 0:tn], bpB[i2].t[:], uT[:, fc, t0:t0 + tn], start=True, stop=True), reads=[bpB[i2].B, uB], writes=[pzsB])
            j = rr % 2; rr += 1
            v_tt(S, nc, "dve", t1[j].t[:, 0:tn], pz[:, 0:tn], cosT[:, s0:s0 + tn], ALU.mult, [pzB, cosB], [t1[j].B])
            S.op("act", lambda: nc.scalar.copy(out=zs[j].t[:, 0:tn], in_=pzs[:, 0:tn]), reads=[pzsB], writes=[zs[j].B])
            v_tt(S, nc, "pool", t2[j].t[:, 0:tn], zs[j].t[:, 0:tn], sinT[:, s0:s0 + tn], ALU.mult, [zs[j].B, sinB], [t2[j].B])
            v_tt(S, nc, "pool", Vb[:, t0:t0 + tn], t1[j].t[:, 0:tn], t2[j].t[:, 0:tn], ALU.add, [t1[j].B, t2[j].B], [VB])
        rho = P.rhoS.t[:, g:g + 1]
        if cont:
            S.op("dve", lambda: nc.vector.tensor_tensor_scan(out=Wb[:, 0:NT], data0=rho.to_broadcast([128, NT]), data1=Vb[:, 0:NT], initial=0.0, op0=ALU.mult, op1=ALU.add),
                 reads=[VB, P.rhoS.B], writes=[WB])
        else:
            S.op("dve", lambda: nc.vector.tensor_tensor_scan(out=Wb[:, 0:16], data0=rho.to_broadcast([128, 16]), data1=Vb[:, 0:16], initial=0.0, op0=ALU.mult, op1=ALU.add),
                 reads=[VB, P.rhoS.B], writes=[WB])
            init = 0.0 if mode == "A" else xs.t[:, g:g + 1]
            S.op("dve", lambda: nc.vector.tensor_tensor_scan(out=Wb[:, 16:NT], data0=rho.to_broadcast([128, NT - 16]), data1=Vb[:, 16:NT], initial=init, op0=ALU.mult, op1=ALU.add),
                 reads=[VB, P.rhoS.B] + ([xs.B] if mode != "A" else []), writes=[WB])
        if mode == "A":
            S.op("act", lambda: nc.scalar.copy(out=wend.t[:, 0, g:g + 1], in_=Wb[:, NT - 1:NT]), reads=[WB], writes=[wend.B])
            S.op("act", lambda: nc.scalar.copy(out=wend.t[:, 1, g:g + 1], in_=Wb[:, 15:16]), reads=[WB], writes=[wend.B])
            continue
        if mode == "F":
            S.op("act", lambda: nc.scalar.copy(out=wend.t[:, g:g + 1], in_=Wb[:, NT - 1:NT]), reads=[WB], writes=[wend.B])
        if cont:
            v_tt(S, nc, "dve", A1.t[:, 0:NT], Wb[:, 0:NT], cosT[:, 1:NT + 1], ALU.mult, [WB, cosB], [A1.B])
            v_tt(S, nc, "pool", A2.t[:, 0:NT], Wb[:, 0:NT], sinT[:, 1:NT + 1], ALU.mult, [WB, sinB], [A2.B])
        else:
            v_tt(S, nc, "dve", A1.t[:, 0:16], Wb[:, 0:16], cosT[:, 1009:1025], ALU.mult, [WB, cosB], [A1.B])
            v_tt(S, nc, "dve", A1.t[:, 16:NT], Wb[:, 16:NT], cosT[:, 1:1025], ALU.mult, [WB, cosB], [A1.B])
            v_tt(S, nc, "pool", A2.t[:, 0:16], Wb[:, 0:16], sinT[:, 1009:1025], ALU.mult, [WB, sinB], [A2.B])
            v_tt(S, nc, "pool", A2.t[:, 16:NT], Wb[:, 16:NT], sinT[:, 1:1025], ALU.mult, [WB, sinB], [A2.B])
        S.op("act", lambda: nc.scalar.copy(out=wbA[i2].t[:, 112:128], in_=P.CA.t[:, g, :]), reads=[P.CA.B], writes=[wbA[i2].B])
        S.op("act", lambda: nc.scalar.copy(out=wbB[i2].t[:, 112:128], in_=P.CB.t[:, g, :]), reads=[P.CB.B], writes=[wbB[i2].B])
        o = 112 - 16 * gl
        for ti, (t0, tn) in enumerate(TILES):
            py, pyB = C.ps[5 + ti], C.psB[5 + ti]
            S.op("pe", lambda: nc.tensor.matmul(py[:, 0:tn], wbA[i2].t[:, o:o + 128], A1.t[:, t0:t0 + tn], start=(gl == 0), stop=False), reads=[wbA[i2].B, A1.B], writes=[pyB])
            S.op("pe", lambda: nc.tensor.matmul(py[:, 0:tn], wbB[i2].t[:, o:o + 128], A2.t[:, t0:t0 + tn], start=False, stop=(gl == 7)), reads=[wbB[i2].B, A2.B], writes=[pyB])
            if gl == 7:
                S.op("dve", lambda: nc.vector.scalar_tensor_tensor(out=yT[:, fc, t0:t0 + tn], in0=uT[:, fc, t0:t0 + tn], scalar=dcol.t[:, fc:fc + 1], in1=py[:, 0:tn],
                                                                   op0=ALU.mult, op1=ALU.add),
                     reads=[uB, dcol.B, pyB], writes=[yB])


def big_rows(regX):
    flat = regX[:].rearrange("p c t -> p (c t)")
    return [(flat[:, i * 1056:(i + 1) * 1056], Buf(f"big{i}")) for i in range(6)]


def u_proj(C, w_in, hT, hB, uT, uB):
    nc, S = C.nc, C.S
    for pc in range(4):
        view, vB = C.load_piece(wpiece(w_in, 0, 16, 1536 + pc * 256, 256), 16, 256)
        for mm in range(2):
            m = pc * 2 + mm
            for (t0, tn) in TILES:
                ps, psB = C.bank(0, 4)
                for k in range(16):
                    S.op("pe", lambda: nc.tensor.matmul(ps[:, 0:tn], view[:, k, mm * 128:(mm + 1) * 128], hT[:, k, t0:t0 + tn], start=(k == 0), stop=(k == 15)),
                         reads=[vB, hB], writes=[psB], inc=(k == 15))
                S.op("act", lambda: nc.scalar.copy(out=uT[:, m, t0:t0 + tn], in_=ps[:, 0:tn]), reads=[psB], writes=[uB])


def gelu_glu(C, es, w_glu, bglu, yT, yB, gT, gB, tmp, tmpB):
    nc, S = C.nc, C.S
    sg = [T(nc, es, f"g_sg{i}", [128, 512]) for i in range(2)]
    for m in range(8):
        y = yT[:, m, :]
        v_tt(S, nc, "dve", tmp, y, y, ALU.mult, [yB], [tmpB])
        v_ts(S, nc, "dve", tmp, tmp, 0.044715, 1.0, ALU.mult, ALU.add, [tmpB], [tmpB])
        v_tt(S, nc, "dve", tmp, tmp, y, ALU.mult, [tmpB, yB], [tmpB])
        S.op("act", lambda: nc.scalar.activation(out=tmp, in_=tmp, func=AF.Sigmoid, scale=1.5957691216057308), reads=[tmpB], writes=[tmpB])
        v_tt(S, nc, "dve", y, y, tmp, ALU.mult, [yB, tmpB], [yB])
        S.op("act", lambda: nc.scalar.copy(out=gT[:, m, :], in_=y), reads=[yB], writes=[gB])
    r = 0
    for pc in range(4):
        view, vB = C.load_piece(wpiece(w_glu, 0, 8, pc * 256, 256), 8, 256)
        for mm in range(2):
            m = pc * 2 + mm
            for (t0, tn) in TILES:
                ps, psB = C.bank(0, 4)
                for k in range(8):
                    S.op("pe", lambda: nc.tensor.matmul(ps[:, 0:tn], view[:, k, mm * 128:(mm + 1) * 128], gT[:, k, t0:t0 + tn], start=(k == 0), stop=(k == 7)),
                         reads=[vB, gB], writes=[psB], inc=(k == 7))
                j = r % 2; r += 1
                S.op("act", lambda: nc.scalar.activation(out=sg[j].t[:, 0:tn], in_=ps[:, 0:tn], func=AF.Sigmoid, bias=bglu.t[:, m:m + 1]),
                     reads=[psB, bglu.B], writes=[sg[j].B])
                v_tt(S, nc, "dve", yT[:, m, t0:t0 + tn], yT[:, m, t0:t0 + tn], sg[j].t[:, 0:tn], ALU.mult, [yB, sg[j].B], [yB])

def _din(nc, name, shape, dt=F32):
    return nc.dram_tensor(name, shape, dt, kind="ExternalInput").ap()


def _dout(nc, name, shape, dt=F32):
    return nc.dram_tensor(name, shape, dt, kind="ExternalOutput").ap()


def _ssm_inputs(nc):
    return dict(lamS=_din(nc, "lamS", [128, 3, 64]), lamB=_din(nc, "lamB", [128, 3, 512]), bB=_din(nc, "bB", [128, 2, 512]),
                cS=_din(nc, "cS", [128, 2, 64, 16]), tvals=_din(nc, "tvals", [128, NTAB]), mask8=_din(nc, "mask8", [128, 8]),
                sgn=_din(nc, "sgn", [128, 1]))


def build_A():
    nc = bass.Bass("TRN2", target_bir_lowering=False)
    xin = _din(nc, "xT_in", [128, 16, NT])
    w_in = _din(nc, "w_in", [D, 2560])
    gmix = _din(nc, "g_mix", [128, 16])
    gk_in = _din(nc, "gk2", [128, 1])
    si = _ssm_inputs(nc)
    kth_out = _dout(nc, "kth_out", [128, 4, 128], BF16)
    vh_out = _dout(nc, "vh_out", [128, 512], BF16)
    wend_out = _dout(nc, "wend_out", [128, 2, 64])
    with ExitStack() as es:
        C = Ctx(nc, es)
        S = C.S
        C.consts()
        regX = es.enter_context(sbt(nc, "regX", [128, 16, NT], F32)); xB = Buf("xT")
        regH = es.enter_context(sbt(nc, "regH", [128, 16, NT], BF16)); hB = Buf("regH")
        uT = es.enter_context(sbt(nc, "uT", [128, 8, NT], BF16)); uB = Buf("uT")
        gm = T(nc, es, "gm", [128, 16]); S.dma("sp", gm.t[:], gmix, writes=[gm.B])
        A = type("A", (), {})()
        A.gk = es.enter_context(sbt(nc, "sb_gk2", [128, 1], F32)); A.gkB = Buf("gk2")
        S.dma("sp", A.gk[:], gk_in, writes=[A.gkB])
        A.blk = es.enter_context(sbt(nc, "blkones", [128, 128], BF16)); A.blkB = Buf("blkones")
        S.op("dve", lambda: nc.vector.memset(A.blk[:], 0.0), writes=[A.blkB])
        S.op("dve", lambda: nc.vector.memset(A.blk[0:64, 0:64], 1.0), writes=[A.blkB])
        S.op("dve", lambda: nc.vector.memset(A.blk[64:128, 64:128], 1.0), writes=[A.blkB])
        S.dma("sp", regX[:], xin, writes=[xB])
        C.rmsnorm(regX, xB, 16, gm.t, gm.B, regH, hB)
        u_proj(C, w_in, regH, hB, uT, uB)
        with ExitStack() as es1:
            kT = [es1.enter_context(sbt(nc, f"kT{i}", [128, 4, KTW], BF16)) for i in range(2)]; kTB = Buf("kT")
            V2 = es1.enter_context(sbt(nc, "V2", [128, 10, 512], BF16)); V2B = Buf("V2")
            tmpq = es1.enter_context(sbt(nc, "tmpq", [128, 512], F32)); tmpqB = Buf("tmpq")
            kv_proj(C, A, w_in, regH, hB, kT, kTB, V2, V2B, None, None, tmpq, tmpqB, ktiles=[(912, 128)], vblocks=[(912, 128, 9)])
            S.dma("sp", kth_out[0:64], kT[0][0:64, :, 1056:1184], reads=[kTB], writes=[Buf("o1")])
            S.dma("sp", kth_out[64:128], kT[1][64:128, :, 1056:1184], reads=[kTB], writes=[Buf("o2")])
            S.dma("sp", vh_out, V2[:, 9, :], reads=[V2B], writes=[Buf("o3")])
            S.barrier_all()
        with ExitStack() as es2:
            P = ssm_prep(C, es2, si["lamS"], si["lamB"], si["bB"], si["cS"], si["tvals"], si["mask8"], si["sgn"])
            wend = T(nc, es2, "wend", [128, 2, 64])
            big = big_rows(regX)
            ssm_loop(C, es2, P, uT, uB, big, "A", wend=wend)
            S.dma("sp", wend_out, wend.t[:], reads=[wend.B], writes=[Buf("o4")])
            S.barrier_all()
        print("A instructions", S.n_ins, "waits", S.n_wait)
    return nc


def build_B(debug=False):
    nc = bass.Bass("TRN2", target_bir_lowering=False)
    xin = _din(nc, "xT_in", [128, 16, NT])
    xout = _dout(nc, "xT_out", [128, 16, NT])
    w_in = _din(nc, "w_in", [D, 2560])
    w_glu = _din(nc, "w_glu", [1024, 1024])
    w_out = _din(nc, "w_out", [D, D])
    w_up = _din(nc, "w_up", [D, DFF])
    w_down = _din(nc, "w_down", [DFF, D])
    gmix, gmlp = _din(nc, "g_mix", [128, 16]), _din(nc, "g_mlp", [128, 16])
    gao, gso = _din(nc, "g_ao", [128, 8]), _din(nc, "g_so", [128, 8])
    dcol_in, bglu_in = _din(nc, "dcol", [128, 8]), _din(nc, "bglu", [128, 8])
    dband, dmeta, dqm = _din(nc, "dband", [128, 2, 128]), _din(nc, "dmeta", [128, 128]), _din(nc, "dqm", [128, 16])
    bias_in, garg_in, valid_in = _din(nc, "abias", [128, 16]), _din(nc, "garg", [128, 128]), _din(nc, "valid", [128, 1])
    gq_in, gk_in = _din(nc, "gq2", [128, 1]), _din(nc, "gk2", [128, 1])
    kth_in, vh_in = _din(nc, "kth", [128, 4, 128], BF16), _din(nc, "vh", [128, 512], BF16)
    nat_in, swp_in = _din(nc, "nat", [128, 4, 64]), _din(nc, "swp", [128, 4, 64])
    si = _ssm_inputs(nc)
    if debug:
        dbg_out = _dout(nc, "dbg_out", [128, 16, NT])
    with ExitStack() as es:
        C = Ctx(nc, es)
        S = C.S
        C.consts()
        C.tmp_rr = 0
        regX = es.enter_context(sbt(nc, "regX", [128, 16, NT], F32)); xB = Buf("xT")
        regH = es.enter_context(sbt(nc, "regH", [128, 16, NT], BF16)); hB = Buf("regH")
        aT, aTB = regX[:, 0:8, :], Buf("aT")
        yT, yB = regX[:, 8:16, :], Buf("yT")
        def small(name, shape, src):
            t = T(nc, es, name, shape); S.dma("sp", t.t[:], src, writes=[t.B]); return t
        gm, gl2 = small("gm", [128, 16], gmix), small("gl2", [128, 16], gmlp)
        ga, gs = small("ga", [128, 8], gao), small("gs", [128, 8], gso)
        dcol, bglu = small("dcolS", [128, 8], dcol_in), small("bgluS", [128, 8], bglu_in)
        S.dma("sp", regX[:], xin, writes=[xB])
        C.rmsnorm(regX, xB, 16, gm.t, gm.B, regH, hB)
        S.barrier_all()
        with ExitStack() as esM:
            regR = esM.enter_context(sbt(nc, "regR", [128, 8, NT], BF16)); rB = Buf("regR")
            with ExitStack() as es2:
                u_proj(C, w_in, regH, hB, regR, rB)
                P = ssm_prep(C, es2, si["lamS"], si["lamB"], si["bB"], si["cS"], si["tvals"], si["mask8"], si["sgn"])
                xs = ssm_xstart(C, es2, P, nat_in, swp_in)
                big = big_rows(regX)
                with ExitStack() as es3:
                    ssm_loop(C, es3, P, regR, rB, big, "B", xs=xs, yT=yT, yB=yB, dcol=dcol)
                    S.barrier_all()
                with ExitStack() as es3:
                    gelu_glu(C, es3, w_glu, bglu, yT, yB, regR, rB, big[0][0][:, 0:NT], big[0][1])
                    S.barrier_all()
            with ExitStack() as es2:
                A = attn_consts(C, es2, dband, dmeta, dqm, bias_in, garg_in, valid_in, gq_in, gk_in)
                kT = [es2.enter_context(sbt(nc, f"kT{i}", [128, 4, KTW], BF16)) for i in range(2)]; kTB = Buf("kT")
                V2 = es2.enter_context(sbt(nc, "V2", [128, 10, 512], BF16)); V2B = Buf("V2")
                tmpq = es2.enter_context(sbt(nc, "tmpq", [128, 512], F32)); tmpqB = Buf("tmpq")
                kv_proj(C, A, w_in, regH, hB, kT, kTB, V2, V2B, kth_in, vh_in, tmpq, tmpqB)
                attention(C, A, es2, w_in, regH, hB, kT, kTB, V2, V2B, aT, aTB, tmpq, tmpqB)
                S.barrier_all()
        if debug:
            S.dma("sp", dbg_out, regX[:], reads=[aTB, yB], writes=[Buf("dbg")])
        C.rmsnorm(aT, aTB, 8, ga.t, ga.B, regH, hB, 0)
        C.rmsnorm(yT, yB, 8, gs.t, gs.B, regH, hB, 8)
        S.barrier_all()
        S.dma("sp", regX[:], xin, writes=[xB])
        dense_acc_into_x(C, w_out, 16, 0, regH, hB, regX, xB, 16, 256)
        S.barrier_all()
        with ExitStack() as es5:
            hid = [es5.enter_context(sbt(nc, f"hid{i}", [128, 8, NT], BF16)) for i in range(2)]
            hidB = [Buf(f"hid{i}") for i in range(2)]
            tmp = [es5.enter_context(sbt(nc, f"ftmp{i}", [128, 512], F32)) for i in range(2)]
            tmpB = [Buf(f"ftmp{i}") for i in range(2)]
            C.rmsnorm(regX, xB, 16, gl2.t, gl2.B, regH, hB)
            ffn(C, w_up, w_down, regH, hB, regX, xB, hid, hidB, tmp, tmpB)
            S.dma("sp", xout, regX[:], reads=[xB], writes=[Buf("xout")])
            S.barrier_all()
        print("B instructions", S.n_ins, "waits", S.n_wait)
    return nc


def ssm_xstart_f(C, es, P, wend_dram, fend):
    nc, S = C.nc, C.S
    xs = T(nc, es, "xstart", [128, 64])
    wd, wdB = wend_dram
    with ExitStack() as es2:
        nat = T(nc, es2, "x_nat", [128, 64]); S.dma("sp", nat.t[:], wd, reads=[wdB], writes=[nat.B])
        swp = T(nc, es2, "x_swp", [128, 64])
        S.dma("sp", swp.t[0:64], wd[64:128], reads=[wdB], writes=[swp.B])
        S.dma("sp", swp.t[64:128], wd[0:64], reads=[wdB], writes=[swp.B])
        an, kk, sn, cs = [T(nc, es2, f"x_t{i}", [128, 64]) for i in range(4)]
        v_ts(S, nc, "dve", an.t[:], P.thS.t[:], float(fend), None, ALU.mult, None, [P.thS.B], [an.B])
        sincos(C, an.t[:], an.B, kk.t[:], kk.B, sn.t[:], sn.B, cs.t[:], cs.B)
        v_ts(S, nc, "dve", sn.t[:], sn.t[:], P.sgn.t[:, 0:1], None, ALU.mult, None, [sn.B, P.sgn.B], [sn.B])
        v_tt(S, nc, "dve", xs.t[:], cs.t[:], nat.t[:], ALU.mult, [cs.B, nat.B], [xs.B])
        v_tt(S, nc, "dve", kk.t[:], sn.t[:], swp.t[:], ALU.mult, [sn.B, swp.B], [kk.B])
        v_tt(S, nc, "dve", xs.t[:], xs.t[:], kk.t[:], ALU.add, [xs.B, kk.B], [xs.B])
        C.S.barrier_all()
    return xs


def emit_pass(C, regX, regH, l, q, W, xin, xout, halo_prev, halo_cur, wend_prev, wend_cur, cst):
    nc, S = C.nc, C.S
    PFX[0] = f"_L{l}Q{q}"
    xB, hB = Buf("xT"), Buf("regH")
    aT, aTB = regX[:, 0:8, :], Buf("aT")
    yT, yB = regX[:, 8:16, :], Buf("yT")
    xin_ap, xinB = xin
    xout_ap, xoutB = xout
    with ExitStack() as es:
        def small(name, shape, src):
            t = T(nc, es, name, shape); S.dma("sp", t.t[:], src, writes=[t.B]); return t
        gm, gl2 = small("gm", [128, 16], W["g_mix"][l]), small("gl2", [128, 16], W["g_mlp"][l])
        ga, gs = small("ga", [128, 8], W["g_ao"][l]), small("gs", [128, 8], W["g_so"][l])
        dcol, bglu = small("dcolS", [128, 8], W["dcol"][l]), small("bgluS", [128, 8], W["bglu"][l])
        S.dma("sp", regX[:], xin_ap, reads=[xinB], writes=[xB])
        C.rmsnorm(regX, xB, 16, gm.t, gm.B, regH, hB)
        S.barrier_all()
        with ExitStack() as esM:
            regR = esM.enter_context(sbt(nc, "regR", [128, 8, NT], BF16)); rB = Buf("regR")
            with ExitStack() as es2:
                u_proj(C, W["w_in"][l], regH, hB, regR, rB)
                P = ssm_prep(C, es2, W["lamS"][l], W["lamB"][l], W["bB"][l], W["cS"][l], cst["tvals"], cst["mask8"], cst["sgn"])
                xs = None
                if q > 0:
                    xs = ssm_xstart_f(C, es2, P, wend_prev, 1040 if q == 1 else 1024)
                wend = T(nc, es2, "wend", [128, 64])
                big = big_rows(regX)
                with ExitStack() as es3:
                    ssm_loop(C, es3, P, regR, rB, big, "F", xs=xs, yT=yT, yB=yB, dcol=dcol, wend=wend, cont=(q == 0))
                    S.dma("sp", wend_cur[0], wend.t[:], reads=[wend.B], writes=[wend_cur[1]])
                    S.barrier_all()
                with ExitStack() as es3:
                    gelu_glu(C, es3, W["w_glu"][l], bglu, yT, yB, regR, rB, big[0][0][:, 0:NT], big[0][1])
                    S.barrier_all()
            with ExitStack() as es2:
                A = attn_consts(C, es2, cst["dband"], cst["dmeta"], cst["dqm"], W["abias"][l], cst["garg"][q], cst["valid"], W["gq2"][l], W["gk2"][l])
                kT = [es2.enter_context(sbt(nc, f"kT{i}", [128, 4, KTW], BF16)) for i in range(2)]; kTB = Buf("kT")
                V2 = es2.enter_context(sbt(nc, "V2", [128, 10, 512], BF16)); V2B = Buf("V2")
                tmpq = es2.enter_context(sbt(nc, "tmpq", [128, 512], F32)); tmpqB = Buf("tmpq")
                S.op("dve", lambda: nc.vector.memset(kT[0][:], 0.0), writes=[kTB])
                S.op("dve", lambda: nc.vector.memset(kT[1][:], 0.0), writes=[kTB])
                S.op("dve", lambda: nc.vector.memset(V2[:, 0, :], 0.0), writes=[V2B])
                if q > 0:
                    (hk, hkB), (hv, hvB) = halo_prev
                    S.dma("sp", kT[0][0:64, :, 32:160], hk[0:64], reads=[hkB], writes=[kTB])
                    S.dma("sp", kT[1][64:128, :, 32:160], hk[64:128], reads=[hkB], writes=[kTB])
                    S.dma("sp", V2[:, 1, :], hv, reads=[hvB], writes=[V2B])
                kv_proj(C, A, W["w_in"][l], regH, hB, kT, kTB, V2, V2B, None, None, tmpq, tmpqB, skip_init=True)
                (hk, hkB), (hv, hvB) = halo_cur
                S.dma("sp", hk[0:64], kT[0][0:64, :, 1056:1184], reads=[kTB], writes=[hkB])
                S.dma("sp", hk[64:128], kT[1][64:128, :, 1056:1184], reads=[kTB], writes=[hkB])
                S.dma("sp", hv, V2[:, 9, :], reads=[V2B], writes=[hvB])
                attention(C, A, es2, W["w_in"][l], regH, hB, kT, kTB, V2, V2B, aT, aTB, tmpq, tmpqB, skip_prev0=(q == 0), use_valid=False)
                S.barrier_all()
        C.rmsnorm(aT, aTB, 8, ga.t, ga.B, regH, hB, 0)
        C.rmsnorm(yT, yB, 8, gs.t, gs.B, regH, hB, 8)
        S.barrier_all()
        S.dma("sp", regX[:], xin_ap, reads=[xinB], writes=[xB])
        dense_acc_into_x(C, W["w_out"][l], 16, 0, regH, hB, regX, xB, 16, 256)
        S.barrier_all()
        with ExitStack() as es5:
            hid = [es5.enter_context(sbt(nc, f"hid{i}", [128, 8, NT], BF16)) for i in range(2)]
            hidB = [Buf(f"hid{i}") for i in range(2)]
            tmp = [es5.enter_context(sbt(nc, f"ftmp{i}", [128, 512], F32)) for i in range(2)]
            tmpB = [Buf(f"ftmp{i}") for i in range(2)]
            C.rmsnorm(regX, xB, 16, gl2.t, gl2.B, regH, hB)
            ffn(C, W["w_up"][l], W["w_down"][l], regH, hB, regX, xB, hid, hidB, tmp, tmpB)
            S.dma("sp", xout_ap, regX[:], reads=[xB], writes=[xoutB])
            S.barrier_all()


def build_F(nlayers=4, nq=4):
    nc = bass.Bass("TRN2", target_bir_lowering=False)
    xin = _din(nc, "xT_in", [4, 128, 16, NT])
    xout = _dout(nc, "xT_out", [4, 128, 16, NT])
    W = dict(w_in=_din(nc, "w_in", [4, D, 2560]), w_glu=_din(nc, "w_glu", [4, 1024, 1024]), w_out=_din(nc, "w_out", [4, D, D]),
             w_up=_din(nc, "w_up", [4, D, DFF]), w_down=_din(nc, "w_down", [4, DFF, D]),
             g_mix=_din(nc, "g_mix", [4, 128, 16]), g_mlp=_din(nc, "g_mlp", [4, 128, 16]), g_ao=_din(nc, "g_ao", [4, 128, 8]),
             g_so=_din(nc, "g_so", [4, 128, 8]), dcol=_din(nc, "dcol", [4, 128, 8]), bglu=_din(nc, "bglu", [4, 128, 8]),
             abias=_din(nc, "abias", [4, 128, 16]), gq2=_din(nc, "gq2", [4, 128, 1]), gk2=_din(nc, "gk2", [4, 128, 1]),
             lamS=_din(nc, "lamS", [4, 128, 3, 64]), lamB=_din(nc, "lamB", [4, 128, 3, 512]), bB=_din(nc, "bB", [4, 128, 2, 512]),
             cS=_din(nc, "cS", [4, 128, 2, 64, 16]))
    cst = dict(tvals=_din(nc, "tvals", [128, NTAB]), mask8=_din(nc, "mask8", [128, 8]), sgn=_din(nc, "sgn", [128, 1]),
               dband=_din(nc, "dband", [128, 2, 128]), dmeta=_din(nc, "dmeta", [128, 128]), dqm=_din(nc, "dqm", [128, 16]),
               garg=_din(nc, "garg", [4, 128, 128]), valid=_din(nc, "valid", [128, 1]))
    scr = [nc.dram_tensor(f"xscr{i}", [4, 128, 16, NT], F32).ap() for i in range(2)]
    scrB = [[Buf(f"xscr{i}_{q}") for q in range(4)] for i in range(2)]
    hk = [nc.dram_tensor(f"hk{i}", [128, 4, 128], BF16).ap() for i in range(2)]
    hv = [nc.dram_tensor(f"hv{i}", [128, 512], BF16).ap() for i in range(2)]
    hB_ = [(Buf(f"hk{i}"), Buf(f"hv{i}")) for i in range(2)]
    wd = [nc.dram_tensor(f"wd{i}", [128, 64], F32).ap() for i in range(2)]
    wdB = [Buf(f"wd{i}") for i in range(2)]
    with ExitStack() as es:
        C = Ctx(nc, es)
        S = C.S
        C.consts()
        C.tmp_rr = 0
        regX = es.enter_context(sbt(nc, "regX", [128, 16, NT], F32))
        regH = es.enter_context(sbt(nc, "regH", [128, 16, NT], BF16))
        xinB = Buf("xin")
        outB = [Buf(f"xout{q}") for q in range(4)]
        for l in range(nlayers):
            for q in range(nq):
                src = (xin[q], xinB) if l == 0 else (scr[(l - 1) % 2][q], scrB[(l - 1) % 2][q])
                dst = (xout[q], outB[q]) if l == nlayers - 1 else (scr[l % 2][q], scrB[l % 2][q])
                i, j = q % 2, (q + 1) % 2
                emit_pass(C, regX, regH, l, q, W, src, dst,
                          ((hk[j], hB_[j][0]), (hv[j], hB_[j][1])), ((hk[i], hB_[i][0]), (hv[i], hB_[i][1])),
                          (wd[j], wdB[j]), (wd[i], wdB[i]), cst)
        PFX[0] = ""
        print("F instructions", S.n_ins, "waits", S.n_wait)
    return nc

def ssm_host_layout(lre, lim, lst, bre, bim, cre, cim):
    lamS = np.zeros((128, 3, 64), np.float32)
    for ri in range(2):
        lamS[ri * 64:(ri + 1) * 64, 0, :] = lre.T
        lamS[ri * 64:(ri + 1) * 64, 1, :] = lim.T
        lamS[ri * 64:(ri + 1) * 64, 2, :] = lst[None, :]
    def layB(a_gn):
        a = a_gn.reshape(8, 8, 64)
        a = a.transpose(1, 0, 2)
        return np.repeat(a[:, None], 16, axis=1).reshape(128, 8 * 64)
    lamB = np.stack([layB(lre), layB(lim), layB(np.repeat(lst[:, None], 64, 1))], 1).astype(np.float32)
    def layBb(b):
        a = b.reshape(8, 8, 64, 16).transpose(1, 3, 0, 2)
        return a.reshape(128, 512)
    bB = np.stack([layBb(bre), layBb(bim)], 1).astype(np.float32)
    def layC(c):
        a = c.transpose(2, 0, 1)
        return np.concatenate([a, a], 0)
    cS = np.stack([layC(cre), layC(cim)], 1).astype(np.float32)
    return dict(lamS=lamS, lamB=np.ascontiguousarray(lamB), bB=np.ascontiguousarray(bB), cS=np.ascontiguousarray(cS))


def ssm_host_consts():
    tvals = np.broadcast_to(np.arange(NTAB, dtype=np.float32)[None], (128, NTAB)).copy()
    mask8 = np.zeros((128, 8), np.float32)
    for p in range(128):
        mask8[p, p // 16] = 1.0
    sgn = np.ones((128, 1), np.float32)
    sgn[:64] = -1.0
    return dict(tvals=tvals, mask8=mask8, sgn=sgn)


_PROGS = {}
NLAYERS = 4
DEBUG_LAST = {}


def _prog(name):
    if name not in _PROGS:
        _PROGS[name] = build_A() if name == "A" else build_B()
    return _PROGS[name]


def kernel_unfused(x, meta_tokens, norm_mix_g, w_in, q_norm_g, k_norm_g, attn_sinks, ssm_lambda_re, ssm_lambda_im,
           ssm_log_step, ssm_b_re, ssm_b_im, ssm_c_re, ssm_c_im, ssm_d, w_glu, b_glu, attn_out_g, ssm_out_g,
           w_out, norm_mlp_g, w_up, w_down):
    f = lambda a: np.asarray(a, dtype=np.float32)
    x, meta_tokens = f(x), f(meta_tokens)
    ncores = 8
    xs = []
    for c in range(ncores):
        b, q = c // 4, c % 4
        tok = np.concatenate([meta_tokens, x[b, 1024 * q:1024 * (q + 1)]], 0)
        xs.append(to_fm(tok))
    hconst = [attn_host_consts(c % 4) for c in range(ncores)]
    sconst = ssm_host_consts()
    zero_kth = np.zeros((128, 4, 128), np.float32).astype(BF16NP)
    zero_vh = np.zeros((128, 512), np.float32).astype(BF16NP)
    for l in range(NLAYERS):
        sl = ssm_host_layout(f(ssm_lambda_re[l]), f(ssm_lambda_im[l]), f(ssm_log_step[l]), f(ssm_b_re[l]), f(ssm_b_im[l]),
                             f(ssm_c_re[l]), f(ssm_c_im[l]))
        common = dict(w_in=f(w_in[l]), g_mix=gcols(f(norm_mix_g[l])), gk2=np.tile(f(k_norm_g[l]), 2).reshape(128, 1), **sl, **sconst)
        insA = [dict(xT_in=xs[c], **common) for c in range(ncores)]
        resA = run_bass_kernel_spmd(_prog("A"), insA, core_ids=list(range(ncores))).results
        commonB = dict(common, w_glu=f(w_glu[l]), w_out=f(w_out[l]), w_up=f(w_up[l]), w_down=f(w_down[l]),
                       g_mlp=gcols(f(norm_mlp_g[l])), g_ao=gcols(f(attn_out_g[l])), g_so=gcols(f(ssm_out_g[l])),
                       dcol=gcols(f(ssm_d[l])), bglu=gcols(f(b_glu[l])), abias=sink_bias(f(attn_sinks[l])),
                       gq2=np.tile(f(q_norm_g[l]), 2).reshape(128, 1))
        insB = []
        for c in range(ncores):
            b, q = c // 4, c % 4
            nat = np.zeros((128, 4, 64), np.float32)
            for p in range(q):
                nat[:, p, :] = resA[b * 4 + q - 1 - p]["wend_out"][:, 0, :]
            nat[:, q, :] = resA[c]["wend_out"][:, 1, :]
            swp = np.concatenate([nat[64:], nat[:64]], 0)
            kth = resA[c - 1]["kth_out"] if q > 0 else zero_kth
            vh = resA[c - 1]["vh_out"] if q > 0 else zero_vh
            insB.append(dict(xT_in=xs[c], kth=kth, vh=vh, nat=nat, swp=np.ascontiguousarray(swp), **commonB, **hconst[c]))
        resB = run_bass_kernel_spmd(_prog("B"), insB, core_ids=list(range(ncores))).results
        xs = [np.asarray(resB[c]["xT_out"], dtype=np.float32) for c in range(ncores)]
        DEBUG_LAST["resA"], DEBUG_LAST["resB"], DEBUG_LAST["xs"] = resA, resB, xs
    out = np.zeros((2, 4096, D), np.float32)
    for c in range(ncores):
        b, q = c // 4, c % 4
        out[b, 1024 * q:1024 * (q + 1)] = from_fm(xs[c])[NMETA:]
    return out


def _fused_inputs(inp):
    f = lambda a: np.asarray(a, dtype=np.float32)
    x, meta = f(inp["x"]), f(inp["meta_tokens"])
    L4 = range(4)
    sl = [ssm_host_layout(f(inp["ssm_lambda_re"][l]), f(inp["ssm_lambda_im"][l]), f(inp["ssm_log_step"][l]), f(inp["ssm_b_re"][l]),
                          f(inp["ssm_b_im"][l]), f(inp["ssm_c_re"][l]), f(inp["ssm_c_im"][l])) for l in L4]
    st = lambda fn: np.ascontiguousarray(np.stack([fn(l) for l in L4], 0))
    common = dict(
        w_in=f(inp["w_in"]), w_glu=f(inp["w_glu"]), w_out=f(inp["w_out"]), w_up=f(inp["w_up"]), w_down=f(inp["w_down"]),
        g_mix=st(lambda l: gcols(f(inp["norm_mix_g"][l]))), g_mlp=st(lambda l: gcols(f(inp["norm_mlp_g"][l]))),
        g_ao=st(lambda l: gcols(f(inp["attn_out_g"][l]))), g_so=st(lambda l: gcols(f(inp["ssm_out_g"][l]))),
        dcol=st(lambda l: gcols(f(inp["ssm_d"][l]))), bglu=st(lambda l: gcols(f(inp["b_glu"][l]))),
        abias=st(lambda l: sink_bias(f(inp["attn_sinks"][l]))),
        gq2=st(lambda l: np.tile(f(inp["q_norm_g"][l]), 2).reshape(128, 1)), gk2=st(lambda l: np.tile(f(inp["k_norm_g"][l]), 2).reshape(128, 1)),
        lamS=st(lambda l: sl[l]["lamS"]), lamB=st(lambda l: sl[l]["lamB"]), bB=st(lambda l: sl[l]["bB"]), cS=st(lambda l: sl[l]["cS"]),
        **ssm_host_consts())
    hc = [attn_host_consts(q) for q in range(4)]
    common.update(dband=hc[0]["dband"], dmeta=hc[0]["dmeta"], dqm=hc[0]["dqm"], valid=hc[1]["valid"],
                  garg=np.ascontiguousarray(np.stack([hc[q]["garg"] for q in range(4)], 0)))
    ins = []
    for c in range(8):
        b = c // 4
        xq = np.stack([to_fm(np.concatenate([meta, x[b, 1024 * q:1024 * (q + 1)]], 0)) for q in range(4)], 0)
        ins.append(dict(xT_in=np.ascontiguousarray(xq), **common))
    return ins


def kernel_fused(**inp):
    if "F" not in _PROGS:
        _PROGS["F"] = build_F(NLAYERS)
    res = run_bass_kernel_spmd(_PROGS["F"], _fused_inputs(inp), core_ids=list(range(8))).results
    out = np.zeros((2, 4096, D), np.float32)
    for b in range(2):
        xo = np.asarray(res[4 * b]["xT_out"], dtype=np.float32)
        for q in range(4):
            out[b, 1024 * q:1024 * (q + 1)] = from_fm(xo[q])[NMETA:]
    DEBUG_LAST["resF"] = res
    return out


def to_fm(tok):
    return np.ascontiguousarray(tok.T.reshape(16, 128, tok.shape[0]).transpose(1, 0, 2))


def from_fm(fm):
    return np.ascontiguousarray(fm.transpose(1, 0, 2).reshape(D, fm.shape[2]).T)


def gcols(g):
    return np.ascontiguousarray(g.reshape(-1, 128).T)


def kernel(**inputs):
    return kernel_fused(**inputs)
```

```python
import numpy as np
from contextlib import ExitStack
import concourse.bass as bass
import concourse.mybir as mybir
from concourse.bass_utils import run_bass_kernel_spmd
import ml_dtypes

BF16NP = ml_dtypes.bfloat16

F32 = mybir.dt.float32
BF16 = mybir.dt.bfloat16
AF = mybir.ActivationFunctionType
ALU = mybir.AluOpType

D = 2048
NT = 1040
NMETA = 16
DFF = 8192
EPS = 1e-6
TILES = [(0, 16), (16, 512), (528, 512)]
SLOT = 4096
NSLOT = 3


PFX = [""]


def sbt(nc, name, shape, dt):
    return nc.sbuf_tensor(name + PFX[0], shape, dt)


class Buf:
    __slots__ = ("name", "w", "r")

    def __init__(self, name):
        self.name = name
        self.w = None
        self.r = {}


class Sched:
    def __init__(self, nc, es, strict_same=True, n_dma_sems=8):
        self.nc = nc
        self.E = {"pe": nc.tensor, "act": nc.scalar, "dve": nc.vector, "pool": nc.gpsimd, "sp": nc.sync}
        self.sem, self.cnt, self.pending = {}, {}, {}
        for e in self.E:
            self.sem[e] = es.enter_context(nc.semaphore("s_" + e))
            self.cnt[e] = 0
            self.pending[e] = False
        self.known = {e: {} for e in self.E}
        self.strict_same = strict_same
        self.dsem, self.dcnt, self.drr = {}, {}, {}
        for e in ("sp", "pool"):
            self.dsem[e] = [es.enter_context(nc.semaphore(f"d_{e}{i}")) for i in range(n_dma_sems)]
            self.dcnt[e] = [0] * n_dma_sems
            self.drr[e] = 0
        self.n_wait = 0
        self.n_ins = 0

    def _wait(self, e, tok):
        sem, val, src = tok
        if src == e and (e == "pe" or not self.strict_same):
            return
        k = id(sem)
        if self.known[e].get(k, 0) >= val:
            return
        self.E[e].wait_ge(sem, val)
        self.known[e][k] = val
        self.n_wait += 1

    def _deps(self, e, reads, writes):
        for b in reads:
            if b.w is not None:
                self._wait(e, b.w)
        for b in writes:
            if b.w is not None:
                self._wait(e, b.w)
            for t in b.r.values():
                self._wait(e, t)

    def _mark(self, tok, reads, writes):
        for b in reads:
            b.r[id(tok[0])] = tok
        for b in writes:
            b.w = tok
            b.r = {}

    def op(self, e, emit, reads=(), writes=(), inc=True):
        self._deps(e, reads, writes)
        ins = emit()
        self.n_ins += 1
        if inc:
            self.cnt[e] += 1
            ins.then_inc(self.sem[e], 1)
            self.pending[e] = False
            tok = (self.sem[e], self.cnt[e], e)
        else:
            self.pending[e] = True
            tok = (self.sem[e], self.cnt[e] + 1, e)
        self._mark(tok, reads, writes)
        return ins

    def dma(self, q, out, in_, reads=(), writes=()):
        self._deps(q, reads, writes)
        i = self.drr[q]
        self.drr[q] = (i + 1) % len(self.dsem[q])
        sem = self.dsem[q][i]
        if self.dcnt[q][i] > 0:
            self._wait(q, (sem, 16 * self.dcnt[q][i], "dma"))
        ins = self.E[q].dma_start(out=out, in_=in_)
        self.n_ins += 1
        self.dcnt[q][i] += 1
        ins.then_inc(sem, 16)
        tok = (sem, 16 * self.dcnt[q][i], "dma")
        self._mark(tok, reads, writes)
        return ins

    def barrier_all(self):
        toks = []
        for e in self.E:
            if e == "sp":
                continue
            assert not self.pending[e], e
            if self.cnt[e] > 0:
                toks.append((self.sem[e], self.cnt[e], e))
        for q in self.dsem:
            for i, sem in enumerate(self.dsem[q]):
                if self.dcnt[q][i] > 0:
                    toks.append((sem, 16 * self.dcnt[q][i], "dma"))
        for e in self.E:
            for t in toks:
                if t[2] == e:
                    continue
                self._wait(e, t)


class Ctx:
    def __init__(self, nc, es):
        self.nc = nc
        self.es = es
        self.S = Sched(nc, es, strict_same=STRICT[0])
        S = self.S
        self.slots = [es.enter_context(sbt(nc, f"wslot{i}", [128, SLOT], BF16)) for i in range(NSLOT)]
        self.slotB = [Buf(f"wslot{i}") for i in range(NSLOT)]
        self.slot_rr = 0
        self.ps = [es.enter_context(nc.psum_tensor(f"ps{i}", [128, 512], F32)) for i in range(8)]
        self.psB = [Buf(f"ps{i}") for i in range(8)]
        self.ps_rr = 0
        self.ones_bf = es.enter_context(sbt(nc, "ones_bf", [128, 128], BF16))
        self.onesB = Buf("ones")
        S.op("dve", lambda: nc.vector.memset(self.ones_bf[:], 1.0), writes=[self.onesB])
        self.sq = [es.enter_context(sbt(nc, f"sq{i}", [128, 512], BF16)) for i in range(2)]
        self.sqB = [Buf(f"sq{i}") for i in range(2)]
        self.sq_rr = 0
        self.rstd = es.enter_context(sbt(nc, "rstd", [128, NT], F32))
        self.rstdB = Buf("rstd")
        self.lnt = es.enter_context(sbt(nc, "lnt", [128, 512], F32))
        self.lntB = Buf("lnt")

    def bank(self, lo=0, hi=6):
        n = hi - lo
        i = lo + (self.ps_rr % n)
        self.ps_rr += 1
        return self.ps[i], self.psB[i]

    def load_piece(self, src_ap, nk, ncols):
        i = self.slot_rr % NSLOT
        self.slot_rr += 1
        view = self.slots[i][:, 0:nk * ncols].rearrange("p (k f) -> p k f", k=nk)
        self.S.dma("pool", view, src_ap, writes=[self.slotB[i]])
        return view, self.slotB[i]

    def rmsnorm(self, src, srcB, ndc, gcol, gB, dst, dstB, dst_dc0=0):
        nc, S = self.nc, self.S
        inv = 1.0 / (ndc * 128)
        for (t0, tn) in TILES:
            ps, psB = self.bank(6, 8)
            for dc in range(ndc):
                j = self.sq_rr % 2
                self.sq_rr += 1
                sq, sqB = self.sq[j], self.sqB[j]
                S.op("act", lambda: nc.scalar.activation(out=sq[:, 0:tn], in_=src[:, dc, t0:t0 + tn], func=AF.Square),
                     reads=[srcB], writes=[sqB])
                S.op("pe", lambda: nc.tensor.matmul(ps[:, 0:tn], self.ones_bf[:], sq[:, 0:tn], start=(dc == 0), stop=(dc == ndc - 1)),
                     reads=[sqB, self.onesB], writes=[psB], inc=True)
            S.op("act", lambda: nc.scalar.activation(out=self.lnt[:, 0:tn], in_=ps[:, 0:tn], func=AF.Ln, scale=inv, bias=self.epsc[:, 0:1]),
                 reads=[psB, self.epsB], writes=[self.lntB])
            S.op("act", lambda: nc.scalar.activation(out=self.rstd[:, t0:t0 + tn], in_=self.lnt[:, 0:tn], func=AF.Exp, scale=-0.5),
                 reads=[self.lntB], writes=[self.rstdB])
        for dc in range(ndc):
            S.op("dve", lambda: nc.vector.scalar_tensor_tensor(out=dst[:, dst_dc0 + dc, :], in0=src[:, dc, :], scalar=gcol[:, dc:dc + 1],
                                                               in1=self.rstd[:, :], op0=ALU.mult, op1=ALU.mult),
                 reads=[srcB, gB, self.rstdB], writes=[dstB])

    def consts(self):
        nc, S = self.nc, self.S
        self.epsc = self.es.enter_context(sbt(nc, "epsc", [128, 1], F32))
        self.epsB = Buf("epsc")
        S.op("dve", lambda: nc.vector.memset(self.epsc[:], EPS), writes=[self.epsB])


def wpiece(w_ap, k0, nk, c0, ncols):
    return w_ap.rearrange("(kc p) f -> p kc f", p=128)[:, k0:k0 + nk, c0:c0 + ncols]


def dense_acc_into_x(C, w_ap, nkc, row0_chunk, act, actB, xT, xB, m_chunks, piece_cols):
    nc, S = C.nc, C.S
    per = piece_cols // 128
    for p0 in range(0, m_chunks, per):
        view, vB = C.load_piece(wpiece(w_ap, row0_chunk, nkc, p0 * 128, piece_cols), nkc, piece_cols)
        for mm in range(per):
            m = p0 + mm
            for (t0, tn) in TILES:
                ps, psB = C.bank()
                for k in range(nkc):
                    S.op("pe", lambda: nc.tensor.matmul(ps[:, 0:tn], view[:, k, mm * 128:(mm + 1) * 128], act[:, k, t0:t0 + tn],
                                                        start=(k == 0), stop=(k == nkc - 1)),
                         reads=[vB, actB], writes=[psB], inc=(k == nkc - 1))
                S.op("dve", lambda: nc.vector.tensor_tensor(out=xT[:, m, t0:t0 + tn], in0=ps[:, 0:tn], in1=xT[:, m, t0:t0 + tn], op=ALU.add),
                     reads=[psB, xB], writes=[xB])


def ffn(C, w_up, w_down, h2T, h2B, xT, xB, hid, hidB, tmp, tmpB):
    nc, S = C.nc, C.S
    NFG = DFF // 1024
    for fg in range(NFG):
        hb, hbB = hid[fg % 2], hidB[fg % 2]
        for pc in range(4):
            view, vB = C.load_piece(wpiece(w_up, 0, 16, fg * 1024 + pc * 256, 256), 16, 256)
            for mm in range(2):
                j = pc * 2 + mm
                for (t0, tn) in TILES:
                    ps, psB = C.bank()
                    for k in range(16):
                        S.op("pe", lambda: nc.tensor.matmul(ps[:, 0:tn], view[:, k, mm * 128:(mm + 1) * 128], h2T[:, k, t0:t0 + tn],
                                                            start=(k == 0), stop=(k == 15)),
                             reads=[vB, h2B], writes=[psB], inc=(k == 15))
                    tb = C.tmp_rr % 2
                    C.tmp_rr += 1
                    S.op("act", lambda: nc.scalar.activation(out=tmp[tb][:, 0:tn], in_=ps[:, 0:tn], func=AF.Relu),
                         reads=[psB], writes=[tmpB[tb]])
                    S.op("dve", lambda: nc.vector.tensor_tensor(out=hb[:, j, t0:t0 + tn], in0=tmp[tb][:, 0:tn], in1=tmp[tb][:, 0:tn], op=ALU.mult),
                         reads=[tmpB[tb]], writes=[hbB])
        dense_acc_into_x(C, w_down, 8, fg * 8, hb, hbB, xT, xB, 16, 512)


KTW = 1184
SLOPES = [2.0 ** (-(h + 1) / 2.0) for h in range(16)]


def dup_cols(ap2, reps):
    a = [list(x) for x in ap2.ap]
    return bass.AP(ap2.tensor, ap2.offset, [a[0], [0, reps]] + a[1:])


def attn_consts(C, es, dband_in, dmeta_in, dqm_in, bias_in, garg_in, valid_in, gq_in, gk_in):
    nc, S = C.nc, C.S
    A = type("A", (), {})()
    def ld(name, shape, src):
        t = es.enter_context(sbt(nc, "sb_" + name, shape, F32))
        b = Buf(name)
        S.dma("sp", t[:], src, writes=[b])
        return t, b
    A.dband, A.dbandB = ld("dband", [128, 2, 128], dband_in)
    A.dmeta, A.dmetaB = ld("dmeta", [128, 128], dmeta_in)
    A.dqm, A.dqmB = ld("dqm", [128, 16], dqm_in)
    A.bias, A.biasB = ld("abias", [128, 16], bias_in)
    A.garg, A.gargB = ld("garg", [128, 128], garg_in)
    A.valid, A.validB = ld("valid", [128, 1], valid_in)
    A.gq, A.gqB = ld("gq2", [128, 1], gq_in)
    A.gk, A.gkB = ld("gk2", [128, 1], gk_in)
    S.op("act", lambda: nc.scalar.activation(out=A.garg[:], in_=A.garg[:], func=AF.Exp), reads=[A.gargB], writes=[A.gargB])
    A.blk = es.enter_context(sbt(nc, "blkones", [128, 128], BF16))
    A.blkB = Buf("blkones")
    S.op("dve", lambda: nc.vector.memset(A.blk[:], 0.0), writes=[A.blkB])
    S.op("dve", lambda: nc.vector.memset(A.blk[0:64, 0:64], 1.0), writes=[A.blkB])
    S.op("dve", lambda: nc.vector.memset(A.blk[64:128, 64:128], 1.0), writes=[A.blkB])
    return A


def headnorm_evac(C, A, ps, psB, tn, gcol, gB, out_ap, outB, tmpq, tmpqB):
    nc, S = C.nc, C.S
    j = C.sq_rr % 2
    C.sq_rr += 1
    sq, sqB = C.sq[j], C.sqB[j]
    S.op("act", lambda: nc.scalar.activation(out=sq[:, 0:tn], in_=ps[:, 0:tn], func=AF.Square), reads=[psB], writes=[sqB])
    st, stB = C.ps[2], C.psB[2]
    S.op("pe", lambda: nc.tensor.matmul(st[:, 0:tn], A.blk[:], sq[:, 0:tn], start=True, stop=True), reads=[sqB, A.blkB], writes=[stB])
    S.op("act", lambda: nc.scalar.activation(out=C.lnt[:, 0:tn], in_=st[:, 0:tn], func=AF.Ln, scale=1.0 / 64, bias=C.epsc[:, 0:1]),
         reads=[stB, C.epsB], writes=[C.lntB])
    S.op("act", lambda: nc.scalar.activation(out=tmpq[:, 0:tn], in_=C.lnt[:, 0:tn], func=AF.Exp, scale=-0.5), reads=[C.lntB], writes=[tmpqB])
    if isinstance(out_ap, tuple):
        for (r0, oap) in ((0, out_ap[0]), (64, out_ap[1])):
            S.op("dve", lambda: nc.vector.scalar_tensor_tensor(out=oap, in0=ps[r0:r0 + 64, 0:tn], scalar=gcol[r0:r0 + 64, 0:1], in1=tmpq[r0:r0 + 64, 0:tn],
                                                               op0=ALU.mult, op1=ALU.mult),
                 reads=[psB, gB, tmpqB], writes=[outB])
    else:
        S.op("dve", lambda: nc.vector.scalar_tensor_tensor(out=out_ap, in0=ps[:, 0:tn], scalar=gcol[:, 0:1], in1=tmpq[:, 0:tn], op0=ALU.mult, op1=ALU.mult),
             reads=[psB, gB, tmpqB], writes=[outB])


def kv_proj(C, A, w_in, hT, hB, kT, kTB, V2, V2B, kth_in, vh_in, tmpq, tmpqB, ktiles=None, vblocks=None, skip_init=False):
    nc, S = C.nc, C.S
    if not skip_init:
        S.op("dve", lambda: nc.vector.memset(kT[0][:], 0.0), writes=[kTB])
        S.op("dve", lambda: nc.vector.memset(kT[1][:], 0.0), writes=[kTB])
        S.op("dve", lambda: nc.vector.memset(V2[:, 0, :], 0.0), writes=[V2B])
    if kth_in is not None:
        S.dma("sp", kT[0][0:64, :, 32:160], kth_in[0:64], writes=[kTB])
        S.dma("sp", kT[1][64:128, :, 32:160], kth_in[64:128], writes=[kTB])
        S.dma("sp", V2[:, 1, :], vh_in, writes=[V2B])
    view, vB = C.load_piece(wpiece(w_in, 0, 16, 1024, 256), 16, 256)
    for kv in range(4):
        for (t0, tn) in (ktiles or TILES):
            ps, psB = C.bank(0, 2)
            for half in range(2):
                for k in range(16):
                    lhsT = view[:, k, kv * 64:(kv + 1) * 64]
                    S.op("pe", lambda: nc.tensor.matmul(ps[half * 64:(half + 1) * 64, 0:tn], lhsT, hT[:, k, t0:t0 + tn], start=(k == 0), stop=(k == 15),
                                                        tile_position=(0, half * 64)),
                         reads=[vB, hB], writes=[psB], inc=(k == 15))
            c0 = t0 if t0 < 16 else t0 + 144
            headnorm_evac(C, A, ps, psB, tn, A.gk, A.gkB, (kT[0][0:64, kv, c0:c0 + tn], kT[1][64:128, kv, c0:c0 + tn]), kTB, tmpq, tmpqB)
    view, vB = C.load_piece(wpiece(w_in, 0, 16, 1280, 256), 16, 256)
    blocks = vblocks or ([(0, 16, 0)] + [(16 + 128 * j, 128, 2 + j) for j in range(8)])
    for (t0, tn, idx) in blocks:
        ps, psB = C.bank(0, 2)
        for k in range(16):
            r = view[:, k, :]
            a = [list(x) for x in r.ap]
            rhs = bass.AP(r.tensor, r.offset, [a[0], [64, 4], [0, 2], [1, 64]])
            S.op("pe", lambda: nc.tensor.matmul(ps[0:tn, :], hT[:, k, t0:t0 + tn], rhs, start=(k == 0), stop=(k == 15)),
                 reads=[vB, hB], writes=[psB], inc=(k == 15))
        S.op("act", lambda: nc.scalar.copy(out=V2[0:tn, idx, :], in_=ps[0:tn, :]), reads=[psB], writes=[V2B])


def attention(C, A, es, w_in, hT, hB, kT, kTB, V2, V2B, aT, aTB, tmpq, tmpqB, nkv=4, jbs=range(-1, 8), skip_prev0=False, use_valid=True):
    nc, S = C.nc, C.S
    def sb(name, shape, dt):
        return es.enter_context(sbt(nc, name, shape, dt)), Buf(name)
    q2, q2B = sb("q2", [128, 2, NT], BF16)
    wtab, wtabB = sb("wtab", [128, 2, 512], F32)
    wmeta, wmetaB = sb("wmeta", [128, 512], F32)
    wqm, wqmB = sb("wqm", [128, 64], F32)
    expS = [sb(f"expS{i}", [128, 512], F32) for i in range(2)]
    pt = [sb(f"pt{i}", [128, 512], BF16) for i in range(3)]
    rec, recB = sb("rec", [128, 512], F32)
    exp_rr = 0
    for kv in range(nkv):
        view, vB = C.load_piece(wpiece(w_in, 0, 16, kv * 256, 256), 16, 256)
        for mm in range(2):
            for (t0, tn) in TILES:
                ps, psB = C.bank(0, 2)
                for k in range(16):
                    S.op("pe", lambda: nc.tensor.matmul(ps[:, 0:tn], view[:, k, mm * 128:(mm + 1) * 128], hT[:, k, t0:t0 + tn],
                                                        start=(k == 0), stop=(k == 15)),
                         reads=[vB, hB], writes=[psB], inc=(k == 15))
                headnorm_evac(C, A, ps, psB, tn, A.gq, A.gqB, q2[:, mm, t0:t0 + tn], q2B, tmpq, tmpqB)
        for hh in range(4):
            h = 4 * kv + hh
            for tl in range(2):
                S.op("act", lambda: nc.scalar.activation(out=wtab[:, tl, hh * 128:(hh + 1) * 128], in_=A.dband[:, tl, :], func=AF.Exp, scale=-SLOPES[h]),
                     reads=[A.dbandB], writes=[wtabB])
            S.op("act", lambda: nc.scalar.activation(out=wmeta[:, hh * 128:(hh + 1) * 128], in_=A.dmeta[:, :], func=AF.Exp, scale=-SLOPES[h], bias=A.bias[:, h:h + 1]),
                 reads=[A.dmetaB, A.biasB], writes=[wmetaB])
            S.op("act", lambda: nc.scalar.activation(out=wqm[:, hh * 16:(hh + 1) * 16], in_=A.dqm[:, :], func=AF.Exp, scale=-SLOPES[h], bias=A.bias[:, h:h + 1]),
                 reads=[A.dqmB, A.biasB], writes=[wqmB])
        for jb in jbs:
            nq = 16 if jb < 0 else 128
            tq0 = 0 if jb < 0 else 16 + 128 * jb
            W4 = 4 * nq
            tiles = []
            if jb >= 0 and not (skip_prev0 and jb == 0):
                tiles.append((3, 32 + 128 * jb, 128, 1 + jb, "prev"))
            if jb >= 0:
                tiles.append((4, 160 + 128 * jb, 128, 2 + jb, "cur"))
            tiles.append((5, 0, 128, 0, "meta"))
            pts = []
            for ti, (bk, kc0, K, vidx, kind) in enumerate(tiles):
                ps, psB = C.ps[bk], C.psB[bk]
                for hh in range(4):
                    mm, half = hh // 2, hh % 2
                    S.op("pe", lambda: nc.tensor.matmul(ps[0:K, hh * nq:(hh + 1) * nq], kT[half][:, kv, kc0:kc0 + K], q2[:, mm, tq0:tq0 + nq],
                                                        start=True, stop=True),
                         reads=[kTB, q2B], writes=[psB], inc=(hh == 3))
                ex, exB = expS[exp_rr % 2]
                exp_rr += 1
                S.op("act", lambda: nc.scalar.activation(out=ex[0:K, 0:W4], in_=ps[0:K, 0:W4], func=AF.Exp, scale=0.125), reads=[psB], writes=[exB])
                p, pB = pt[ti]
                if kind == "prev":
                    if jb == 0 and use_valid:
                        S.op("dve", lambda: nc.vector.scalar_tensor_tensor(out=p[:, 0:512], in0=ex[:, 0:512], scalar=A.valid[:, 0:1], in1=wtab[:, 0, :], op0=ALU.mult, op1=ALU.mult),
                             reads=[exB, A.validB, wtabB], writes=[pB])
                    else:
                        S.op("dve", lambda: nc.vector.tensor_tensor(out=p[:, 0:512], in0=ex[:, 0:512], in1=wtab[:, 0, :], op=ALU.mult), reads=[exB, wtabB], writes=[pB])
                elif kind == "cur":
                    S.op("dve", lambda: nc.vector.tensor_tensor(out=p[:, 0:512], in0=ex[:, 0:512], in1=wtab[:, 1, :], op=ALU.mult), reads=[exB, wtabB], writes=[pB])
                else:
                    if jb < 0:
                        S.op("dve", lambda: nc.vector.tensor_tensor(out=p[:, 0:64], in0=ex[:, 0:64], in1=wqm[:, :], op=ALU.mult), reads=[exB, wqmB], writes=[pB])
                    else:
                        for hh in range(4):
                            h = 4 * kv + hh
                            S.op("dve", lambda: nc.vector.scalar_tensor_tensor(out=p[:, hh * 128:(hh + 1) * 128], in0=ex[:, hh * 128:(hh + 1) * 128],
                                                                               scalar=A.garg[:, h * 8 + jb:h * 8 + jb + 1], in1=wmeta[:, hh * 128:(hh + 1) * 128],
                                                                               op0=ALU.mult, op1=ALU.mult),
                                 reads=[exB, A.gargB, wmetaB], writes=[pB])
                pts.append((p, pB, K, vidx))
            num, numB = C.ps[6], C.psB[6]
            den, denB = C.ps[7], C.psB[7]
            for ti, (p, pB, K, vidx) in enumerate(pts):
                S.op("pe", lambda: nc.tensor.matmul(num[:, 0:W4], V2[0:K, vidx, kv * 128:(kv + 1) * 128], p[0:K, 0:W4], start=(ti == 0), stop=(ti == len(pts) - 1)),
                     reads=[V2B, pB], writes=[numB], inc=True)
                S.op("pe", lambda: nc.tensor.matmul(den[:, 0:W4], C.ones_bf[0:K, :], p[0:K, 0:W4], start=(ti == 0), stop=(ti == len(pts) - 1)),
                     reads=[C.onesB, pB], writes=[denB], inc=True)
            S.op("dve", lambda: nc.vector.reciprocal(out=rec[:, 0:W4], in_=den[:, 0:W4]), reads=[denB], writes=[recB])
            for hh in range(4):
                mm, half = hh // 2, hh % 2
                r0 = half * 64
                c = 2 * kv + mm
                S.op("dve", lambda: nc.vector.tensor_tensor(out=aT[r0:r0 + 64, c, tq0:tq0 + nq], in0=num[r0:r0 + 64, hh * nq:(hh + 1) * nq],
                                                            in1=rec[r0:r0 + 64, hh * nq:(hh + 1) * nq], op=ALU.mult),
                     reads=[numB, recB], writes=[aTB])


def build_attn_test(stage=2, nkv=4, jbs=range(-1, 8)):
    nc = bass.Bass("TRN2", target_bir_lowering=False)
    def din(name, shape, dt=F32):
        return nc.dram_tensor(name, shape, dt, kind="ExternalInput").ap()
    xin = din("xT_in", [128, 16, NT])
    w_in = din("w_in", [D, 2560])
    gmix = din("g_mix", [128, 16])
    dband, dmeta, dqm = din("dband", [128, 2, 128]), din("dmeta", [128, 128]), din("dqm", [128, 16])
    bias_in, garg_in, valid_in = din("abias", [128, 16]), din("garg", [128, 128]), din("valid", [128, 1])
    gq_in, gk_in = din("gq2", [128, 1]), din("gk2", [128, 1])
    kth_in, vh_in = din("kth", [128, 4, 128], BF16), din("vh", [128, 512], BF16)
    aout = nc.dram_tensor("aT_out", [128, 8, NT], F32, kind="ExternalOutput").ap()
    kout = nc.dram_tensor("kT_out", [128, 4, KTW], BF16, kind="ExternalOutput").ap()
    kout2 = nc.dram_tensor("kT_out2", [128, 4, KTW], BF16, kind="ExternalOutput").ap()
    vout = nc.dram_tensor("V2_out", [128, 10, 512], BF16, kind="ExternalOutput").ap()
    with ExitStack() as es:
        C = Ctx(nc, es)
        S = C.S
        C.consts()
        xT = es.enter_context(sbt(nc, "xT", [128, 16, NT], F32)); xB = Buf("xT")
        regH = es.enter_context(sbt(nc, "regH", [128, 16, NT], BF16)); hB = Buf("regH")
        gm = es.enter_context(sbt(nc, "gm", [128, 16], F32)); gmB = Buf("gm")
        kT = [es.enter_context(sbt(nc, f"kT{i}", [128, 4, KTW], BF16)) for i in range(2)]; kTB = Buf("kT")
        V2 = es.enter_context(sbt(nc, "V2", [128, 10, 512], BF16)); V2B = Buf("V2")
        aT = es.enter_context(sbt(nc, "aT", [128, 8, NT], F32)); aTB = Buf("aT")
        tmpq = es.enter_context(sbt(nc, "tmpq", [128, 512], F32)); tmpqB = Buf("tmpq")
        S.dma("sp", xT[:], xin, writes=[xB])
        S.dma("sp", gm[:], gmix, writes=[gmB])
        A = attn_consts(C, es, dband, dmeta, dqm, bias_in, garg_in, valid_in, gq_in, gk_in)
        C.rmsnorm(xT, xB, 16, gm, gmB, regH, hB)
        kv_proj(C, A, w_in, regH, hB, kT, kTB, V2, V2B, kth_in, vh_in, tmpq, tmpqB)
        S.dma("sp", kout, kT[0][:], reads=[kTB], writes=[Buf("kout")])
        S.dma("sp", kout2, kT[1][:], reads=[kTB], writes=[Buf("kout2")])
        S.dma("sp", vout, V2[:], reads=[V2B], writes=[Buf("vout")])
        with ExitStack() as es2:
            if stage >= 2:
                attention(C, A, es2, w_in, regH, hB, kT, kTB, V2, V2B, aT, aTB, tmpq, tmpqB, nkv, jbs)
                S.dma("sp", aout, aT[:], reads=[aTB], writes=[Buf("aout")])
            S.barrier_all()
        print("instructions", S.n_ins, "waits", S.n_wait)
    return nc


def attn_host_consts(q):
    BIG = 1.0e6
    i = np.arange(128)[None, :]
    s = np.arange(128)[:, None]
    dband = np.zeros((128, 2, 128), np.float32)
    dprev = (i - s + 128).astype(np.float32)
    dband[:, 0, :] = np.where(s > i, dprev, BIG)
    dband[:, 1, :] = np.where(s <= i, (i - s).astype(np.float32), BIG)
    dmeta = np.full((128, 128), BIG, np.float32)
    m = np.arange(16)[:, None]
    dmeta[:16] = (i - m + 16)
    dmeta[16] = 0.0
    dqm = np.full((128, 16), BIG, np.float32)
    t = np.arange(16)[None, :]
    dqm[:16] = np.where(m <= t, (t - m).astype(np.float32), BIG)
    dqm[16] = 0.0
    garg = np.zeros((128, 16, 8), np.float32)
    for h in range(16):
        for j in range(8):
            garg[:16, h, j] = -SLOPES[h] * (1024 * q + 128 * j)
    valid = np.full((128, 1), 1.0 if q > 0 else 0.0, np.float32)
    return dict(dband=dband, dmeta=dmeta, dqm=dqm, garg=garg.reshape(128, 128), valid=valid)


def sink_bias(sinks):
    b = np.zeros((128, 16), np.float32)
    b[16, :] = sinks
    return b


TWO_PI = float(2 * np.pi)
MAGIC = 12582912.0
NTAB = 1041


class T:
    def __init__(self, nc, es, name, shape, dt=F32):
        self.t = es.enter_context(sbt(nc, name, shape, dt))
        self.B = Buf(name)


def v_tt(S, nc, e, out, a, b, op, reads, writes):
    eng = nc.vector if e == "dve" else nc.gpsimd
    S.op(e, lambda: eng.tensor_tensor(out=out, in0=a, in1=b, op=op), reads=reads, writes=writes)


def v_ts(S, nc, e, out, a, s1, s2, op0, op1, reads, writes):
    eng = nc.vector if e == "dve" else nc.gpsimd
    if s2 is None:
        S.op(e, lambda: eng.tensor_scalar(out=out, in0=a, scalar1=s1, scalar2=None, op0=op0), reads=reads, writes=writes)
    else:
        S.op(e, lambda: eng.tensor_scalar(out=out, in0=a, scalar1=s1, scalar2=s2, op0=op0, op1=op1), reads=reads, writes=writes)


def range_reduce(S, nc, x, xB, k, kB):
    v_ts(S, nc, "dve", k, x, 1.0 / TWO_PI, MAGIC, ALU.mult, ALU.add, [xB], [kB])
    v_ts(S, nc, "dve", k, k, MAGIC, -TWO_PI, ALU.subtract, ALU.mult, [kB], [kB])
    v_tt(S, nc, "dve", x, x, k, ALU.add, [xB, kB], [xB])


def sincos(C, x, xB, k, kB, sin_out, sinB, cos_out, cosB):
    nc, S = C.nc, C.S
    range_reduce(S, nc, x, xB, k, kB)
    S.op("act", lambda: nc.scalar.activation(out=sin_out, in_=x, func=AF.Sin), reads=[xB], writes=[sinB])
    S.op("act", lambda: nc.scalar.activation(out=k, in_=x, func=AF.Abs), reads=[xB], writes=[kB])
    S.op("act", lambda: nc.scalar.activation(out=cos_out, in_=k, func=AF.Sin, scale=-1.0, bias=C.halfpi[:, 0:1]), reads=[kB, C.hpB], writes=[cosB])


def ssm_prep(C, es, lamS_in, lamB_in, bB_in, cS_in, tvals_in, mask8_in, sgn_in):
    nc, S = C.nc, C.S
    P = type("P", (), {})()
    def ld(name, shape, src):
        t = T(nc, es, "p_" + name, shape)
        S.dma("sp", t.t[:], src, writes=[t.B])
        return t
    C.halfpi = es.enter_context(sbt(nc, "halfpi", [128, 1], F32))
    C.hpB = Buf("halfpi")
    S.op("dve", lambda: nc.vector.memset(C.halfpi[:], float(np.pi / 2)), writes=[C.hpB])
    P.tvals = ld("tvals", [128, NTAB], tvals_in)
    P.mask8 = ld("mask8", [128, 8], mask8_in)
    P.sgn = ld("sgn", [128, 1], sgn_in)
    lamS = ld("lamS", [128, 3, 64], lamS_in)
    P.thS = T(nc, es, "thS", [128, 64])
    P.rhoS = T(nc, es, "rhoS", [128, 64])
    P.lrS = T(nc, es, "lrS", [128, 64])
    P.BA = T(nc, es, "BAfull", [128, 8, 128])
    P.BB = T(nc, es, "BBfull", [128, 8, 128])
    P.CA = T(nc, es, "CAall", [128, 64, 16])
    P.CB = T(nc, es, "CBall", [128, 64, 16])
    dS = T(nc, es, "dS", [128, 64])
    S.op("act", lambda: nc.scalar.activation(out=dS.t[:], in_=lamS.t[:, 2, :], func=AF.Exp), reads=[lamS.B], writes=[dS.B])
    v_tt(S, nc, "dve", P.lrS.t[:], lamS.t[:, 0, :], dS.t[:], ALU.mult, [lamS.B, dS.B], [P.lrS.B])
    v_tt(S, nc, "dve", P.thS.t[:], lamS.t[:, 1, :], dS.t[:], ALU.mult, [lamS.B, dS.B], [P.thS.B])
    S.op("act", lambda: nc.scalar.activation(out=P.rhoS.t[:], in_=P.lrS.t[:], func=AF.Exp), reads=[P.lrS.B], writes=[P.rhoS.B])
    P.th2p = T(nc, es, "th2p", [128, 64])
    v_ts(S, nc, "dve", P.th2p.t[:], P.thS.t[:], 1.0 / TWO_PI, None, ALU.mult, None, [P.thS.B], [P.th2p.B])
    with ExitStack() as es2:
        cS = T(nc, es2, "p_cS", [128, 2, 64, 16])
        S.dma("sp", cS.t[:], cS_in, writes=[cS.B])
        S.op("dve", lambda: nc.vector.tensor_copy(out=P.CA.t[0:64], in_=cS.t[0:64, 0]), reads=[cS.B], writes=[P.CA.B])
        v_ts(S, nc, "dve", P.CA.t[64:128], cS.t[64:128, 1], -1.0, None, ALU.mult, None, [cS.B], [P.CA.B])
        v_ts(S, nc, "dve", P.CB.t[0:64], cS.t[0:64, 1], -1.0, None, ALU.mult, None, [cS.B], [P.CB.B])
        v_ts(S, nc, "dve", P.CB.t[64:128], cS.t[64:128, 0], -1.0, None, ALU.mult, None, [cS.B], [P.CB.B])
        lamB = T(nc, es2, "p_lamB", [128, 3, 512])
        S.dma("sp", lamB.t[:], lamB_in, writes=[lamB.B])
        bB = T(nc, es2, "p_bB", [128, 2, 512])
        S.dma("sp", bB.t[:], bB_in, writes=[bB.B])
        tm = [T(nc, es2, f"p_tm{i}", [128, 512]) for i in range(8)]
        dB, lr, th, kk, sn, cs, mg, t7 = tm
        lre, lim = lamB.t[:, 0, :], lamB.t[:, 1, :]
        S.op("act", lambda: nc.scalar.activation(out=dB.t[:], in_=lamB.t[:, 2, :], func=AF.Exp), reads=[lamB.B], writes=[dB.B])
        v_tt(S, nc, "dve", lr.t[:], lre, dB.t[:], ALU.mult, [lamB.B, dB.B], [lr.B])
        v_tt(S, nc, "dve", th.t[:], lim, dB.t[:], ALU.mult, [lamB.B, dB.B], [th.B])
        S.op("act", lambda: nc.scalar.activation(out=mg.t[:], in_=lr.t[:], func=AF.Exp), reads=[lr.B], writes=[mg.B])
        sincos(C, th.t[:], th.B, kk.t[:], kk.B, sn.t[:], sn.B, cs.t[:], cs.B)
        v_tt(S, nc, "dve", cs.t[:], cs.t[:], mg.t[:], ALU.mult, [cs.B, mg.B], [cs.B])
        v_ts(S, nc, "dve", cs.t[:], cs.t[:], -1.0, None, ALU.add, None, [cs.B], [cs.B])
        v_tt(S, nc, "dve", sn.t[:], sn.t[:], mg.t[:], ALU.mult, [sn.B, mg.B], [sn.B])
        a, b = cs, sn
        v_tt(S, nc, "dve", mg.t[:], lre, lre, ALU.mult, [lamB.B], [mg.B])
        v_tt(S, nc, "dve", kk.t[:], lim, lim, ALU.mult, [lamB.B], [kk.B])
        v_tt(S, nc, "dve", mg.t[:], mg.t[:], kk.t[:], ALU.add, [mg.B, kk.B], [mg.B])
        S.op("dve", lambda: nc.vector.reciprocal(out=mg.t[:], in_=mg.t[:]), reads=[mg.B], writes=[mg.B])
        v_tt(S, nc, "dve", lr.t[:], a.t[:], lre, ALU.mult, [a.B, lamB.B], [lr.B])
        v_tt(S, nc, "dve", kk.t[:], b.t[:], lim, ALU.mult, [b.B, lamB.B], [kk.B])
        v_tt(S, nc, "dve", lr.t[:], lr.t[:], kk.t[:], ALU.add, [lr.B, kk.B], [lr.B])
        v_tt(S, nc, "dve", lr.t[:], lr.t[:], mg.t[:], ALU.mult, [lr.B, mg.B], [lr.B])
        v_tt(S, nc, "dve", th.t[:], b.t[:], lre, ALU.mult, [b.B, lamB.B], [th.B])
        v_tt(S, nc, "dve", kk.t[:], a.t[:], lim, ALU.mult, [a.B, lamB.B], [kk.B])
        v_tt(S, nc, "dve", th.t[:], th.t[:], kk.t[:], ALU.subtract, [th.B, kk.B], [th.B])
        v_tt(S, nc, "dve", th.t[:], th.t[:], mg.t[:], ALU.mult, [th.B, mg.B], [th.B])
        cr, ci = lr, th
        bre, bim = bB.t[:, 0, :], bB.t[:, 1, :]
        v_tt(S, nc, "dve", dB.t[:], cr.t[:], bre, ALU.mult, [cr.B, bB.B], [dB.B])
        v_tt(S, nc, "dve", kk.t[:], ci.t[:], bim, ALU.mult, [ci.B, bB.B], [kk.B])
        v_tt(S, nc, "dve", dB.t[:], dB.t[:], kk.t[:], ALU.subtract, [dB.B, kk.B], [dB.B])
        v_tt(S, nc, "dve", t7.t[:], cr.t[:], bim, ALU.mult, [cr.B, bB.B], [t7.B])
        v_tt(S, nc, "dve", kk.t[:], ci.t[:], bre, ALU.mult, [ci.B, bB.B], [kk.B])
        v_tt(S, nc, "dve", t7.t[:], t7.t[:], kk.t[:], ALU.add, [t7.B, kk.B], [t7.B])
        bbr = dB.t[:].rearrange("p (c n) -> p c n", c=8)
        bbi = t7.t[:].rearrange("p (c n) -> p c n", c=8)
        S.op("dve", lambda: nc.vector.tensor_copy(out=P.BA.t[:, :, 0:64], in_=bbr), reads=[dB.B], writes=[P.BA.B])
        S.op("dve", lambda: nc.vector.tensor_copy(out=P.BA.t[:, :, 64:128], in_=bbi), reads=[t7.B], writes=[P.BA.B])
        S.op("dve", lambda: nc.vector.tensor_copy(out=P.BB.t[:, :, 0:64], in_=bbi), reads=[t7.B], writes=[P.BB.B])
        v_ts(S, nc, "dve", P.BB.t[:, :, 64:128], bbr, -1.0, None, ALU.mult, None, [dB.B], [P.BB.B])
        C.S.barrier_all()
    return P


def ssm_xstart(C, es, P, nat_in, swp_in):
    nc, S = C.nc, C.S
    xs = T(nc, es, "xstart", [128, 64])
    with ExitStack() as es2:
        nat = T(nc, es2, "x_nat", [128, 4, 64]); S.dma("sp", nat.t[:], nat_in, writes=[nat.B])
        swp = T(nc, es2, "x_swp", [128, 4, 64]); S.dma("sp", swp.t[:], swp_in, writes=[swp.B])
        mg, an, kk, sn, cs, t1 = [T(nc, es2, f"x_t{i}", [128, 64]) for i in range(6)]
        S.op("dve", lambda: nc.vector.memset(xs.t[:], 0.0), writes=[xs.B])
        for p in range(4):
            S.op("act", lambda: nc.scalar.activation(out=mg.t[:], in_=P.lrS.t[:], func=AF.Exp, scale=float(1024 * p)), reads=[P.lrS.B], writes=[mg.B])
            v_ts(S, nc, "dve", an.t[:], P.thS.t[:], float(1024 * (p + 1)), None, ALU.mult, None, [P.thS.B], [an.B])
            sincos(C, an.t[:], an.B, kk.t[:], kk.B, sn.t[:], sn.B, cs.t[:], cs.B)
            v_tt(S, nc, "dve", cs.t[:], cs.t[:], mg.t[:], ALU.mult, [cs.B, mg.B], [cs.B])
            v_tt(S, nc, "dve", sn.t[:], sn.t[:], mg.t[:], ALU.mult, [sn.B, mg.B], [sn.B])
            v_ts(S, nc, "dve", sn.t[:], sn.t[:], P.sgn.t[:, 0:1], None, ALU.mult, None, [sn.B, P.sgn.B], [sn.B])
            v_tt(S, nc, "dve", t1.t[:], cs.t[:], nat.t[:, p, :], ALU.mult, [cs.B, nat.B], [t1.B])
            v_tt(S, nc, "dve", xs.t[:], xs.t[:], t1.t[:], ALU.add, [xs.B, t1.B], [xs.B])
            v_tt(S, nc, "dve", t1.t[:], sn.t[:], swp.t[:, p, :], ALU.mult, [sn.B, swp.B], [t1.B])
            v_tt(S, nc, "dve", xs.t[:], xs.t[:], t1.t[:], ALU.add, [xs.B, t1.B], [xs.B])
        C.S.barrier_all()
    return xs


def tslice(t0, tn, cont=False):
    if cont:
        return (t0 + 1, tn)
    return (1009, 16) if t0 == 0 else (t0 - 15, tn)


def ssm_loop(C, es, P, uT, uB, big, mode, xs=None, yT=None, yB=None, dcol=None, wend=None, cont=False):
    nc, S = C.nc, C.S
    (cosT, cosB), (sinT, sinB), (xa, xaB), (ka, kaB), (Vb, VB), (Wb, WB) = big
    t1 = [T(nc, es, f"s_t1{i}", [128, 512]) for i in range(2)]
    t2 = [T(nc, es, f"s_t2{i}", [128, 512]) for i in range(2)]
    bpA = [T(nc, es, f"s_bpA{i}", [128, 128], BF16) for i in range(2)]
    bpB = [T(nc, es, f"s_bpB{i}", [128, 128], BF16) for i in range(2)]
    if mode != "A":
        A1 = T(nc, es, "s_A1", [128, NT], BF16)
        A2 = T(nc, es, "s_A2", [128, NT], BF16)
        wbA = [T(nc, es, f"s_wbA{i}", [128, 240], BF16) for i in range(2)]
        wbB = [T(nc, es, f"s_wbB{i}", [128, 240], BF16) for i in range(2)]
        for w in wbA + wbB:
            S.op("dve", lambda: nc.vector.memset(w.t[:], 0.0), writes=[w.B])
    rr = 0
    zr = 0
    for g in range(64):
        fc, gl = g // 8, g % 8
        i2 = g % 2
        S.op("pool", lambda: nc.gpsimd.tensor_scalar(out=bpA[i2].t[:], in0=P.BA.t[:, fc, :], scalar1=P.mask8.t[:, gl:gl + 1], scalar2=None, op0=ALU.mult),
             reads=[P.BA.B, P.mask8.B], writes=[bpA[i2].B])
        S.op("pool", lambda: nc.gpsimd.tensor_scalar(out=bpB[i2].t[:], in0=P.BB.t[:, fc, :], scalar1=P.mask8.t[:, gl:gl + 1], scalar2=None, op0=ALU.mult),
             reads=[P.BB.B, P.mask8.B], writes=[bpB[i2].B])
        xr, kr = xa[:, 0:NTAB], ka[:, 0:NTAB]
        S.op("dve", lambda: nc.vector.tensor_scalar(out=kr, in0=P.tvals.t[:, :], scalar1=P.th2p.t[:, g:g + 1], scalar2=MAGIC, op0=ALU.mult, op1=ALU.add),
             reads=[P.tvals.B, P.th2p.B], writes=[kaB])
        S.op("dve", lambda: nc.vector.tensor_scalar(out=kr, in0=kr, scalar1=MAGIC, scalar2=-TWO_PI, op0=ALU.subtract, op1=ALU.mult), reads=[kaB], writes=[kaB])
        S.op("dve", lambda: nc.vector.scalar_tensor_tensor(out=xr, in0=P.tvals.t[:, :], scalar=P.thS.t[:, g:g + 1], in1=kr, op0=ALU.mult, op1=ALU.add),
             reads=[P.tvals.B, P.thS.B, kaB], writes=[xaB])
        S.op("act", lambda: nc.scalar.activation(out=sinT[:, 0:NTAB], in_=xr, func=AF.Sin), reads=[xaB], writes=[sinB])
        S.op("act", lambda: nc.scalar.activation(out=kr, in_=xr, func=AF.Abs), reads=[xaB], writes=[kaB])
        S.op("act", lambda: nc.scalar.activation(out=cosT[:, 0:NTAB], in_=kr, func=AF.Sin, scale=-1.0, bias=C.halfpi[:, 0:1]), reads=[kaB, C.hpB], writes=[cosB])
        for (t0, tn) in TILES:
            s0, sn_ = tslice(t0, tn, cont)
            pz, pzB = C.ps[zr % 4], C.psB[zr % 4]; zr += 1
            pzs, pzsB = C.ps[zr % 4], C.psB[zr % 4]; zr += 1
            S.op("pe", lambda: nc.tensor.matmul(pz[:, 0:tn], bpA[i2].t[:], uT[:, fc, t0:t0 + tn], start=True, stop=True), reads=[bpA[i2].B, uB], writes=[pzB])
            S.op("pe", lambda: nc.tensor.matmul(pzs[:, 0:tn], bpB[i2].t[:], uT[:, fc, t0:t0 + tn], start=True, stop=True), reads=[bpB[i2].B, uB], writes=[pzsB])
            j = rr % 2; rr += 1
            v_tt(S, nc, "dve", t1[j].t[:, 0:tn], pz[:, 0:tn], cosT[:, s0:s0 + tn], ALU.mult, [pzB, cosB], [t1[j].B])
            v_tt(S, nc, "dve", t2[j].t[:, 0:tn], pzs[:, 0:tn], sinT[:, s0:s0 + tn], ALU.mult, [pzsB, sinB], [t2[j].B])
            v_tt(S, nc, "dve", Vb[:, t0:t0 + tn], t1[j].t[:, 0:tn], t2[j].t[:, 0:tn], ALU.add, [t1[j].B, t2[j].B], [VB])
        rho = P.rhoS.t[:, g:g + 1]
        if cont:
            S.op("dve", lambda: nc.vector.tensor_tensor_scan(out=Wb[:, 0:NT], data0=rho.to_broadcast([128, NT]), data1=Vb[:, 0:NT], initial=0.0, op0=ALU.mult, op1=ALU.add),
                 reads=[VB, P.rhoS.B], writes=[WB])
        else:
            S.op("dve", lambda: nc.vector.tensor_tensor_scan(out=Wb[:, 0:16], data0=rho.to_broadcast([128, 16]), data1=Vb[:, 0:16], initial=0.0, op0=ALU.mult, op1=ALU.add),
                 reads=[VB, P.rhoS.B], writes=[WB])
            init = 0.0 if mode == "A" else xs.t[:, g:g + 1]
            S.op("dve", lambda: nc.vector.tensor_tensor_scan(out=Wb[:, 16:NT], data0=rho.to_broadcast([128, NT - 16]), data1=Vb[:, 16:NT], initial=init, op0=ALU.mult, op1=ALU.add),
                 reads=[VB, P.rhoS.B] + ([xs.B] if mode != "A" else []), writes=[WB])
        if mode == "A":
            S.op("act", lambda: nc.scalar.copy(out=wend.t[:, 0, g:g + 1], in_=Wb[:, NT - 1:NT]), reads=[WB], writes=[wend.B])
            S.op("act", lambda: nc.scalar.copy(out=wend.t[:, 1, g:g + 1], in_=Wb[:, 15:16]), reads=[WB], writes=[wend.B])
            continue
        if mode == "F":
            S.op("act", lambda: nc.scalar.copy(out=wend.t[:, g:g + 1], in_=Wb[:, NT - 1:NT]), reads=[WB], writes=[wend.B])
        if cont:
            v_tt(S, nc, "dve", A1.t[:, 0:NT], Wb[:, 0:NT], cosT[:, 1:NT + 1], ALU.mult, [WB, cosB], [A1.B])
            v_tt(S, nc, "pool", A2.t[:, 0:NT], Wb[:, 0:NT], sinT[:, 1:NT + 1], ALU.mult, [WB, sinB], [A2.B])
        else:
            v_tt(S, nc, "dve", A1.t[:, 0:16], Wb[:, 0:16], cosT[:, 1009:1025], ALU.mult, [WB, cosB], [A1.B])
            v_tt(S, nc, "dve", A1.t[:, 16:NT], Wb[:, 16:NT], cosT[:, 1:1025], ALU.mult, [WB, cosB], [A1.B])
            v_tt(S, nc, "pool", A2.t[:, 0:16], Wb[:, 0:16], sinT[:, 1009:1025], ALU.mult, [WB, sinB], [A2.B])
            v_tt(S, nc, "pool", A2.t[:, 16:NT], Wb[:, 16:NT], sinT[:, 1:1025], ALU.mult, [WB, sinB], [A2.B])
        S.op("act", lambda: nc.scalar.copy(out=wbA[i2].t[:, 112:128], in_=P.CA.t[:, g, :]), reads=[P.CA.B], writes=[wbA[i2].B])
        S.op("act", lambda: nc.scalar.copy(out=wbB[i2].t[:, 112:128], in_=P.CB.t[:, g, :]), reads=[P.CB.B], writes=[wbB[i2].B])
        o = 112 - 16 * gl
        for ti, (t0, tn) in enumerate(TILES):
            py, pyB = C.ps[5 + ti], C.psB[5 + ti]
            S.op("pe", lambda: nc.tensor.matmul(py[:, 0:tn], wbA[i2].t[:, o:o + 128], A1.t[:, t0:t0 + tn], start=(gl == 0), stop=False), reads=[wbA[i2].B, A1.B], writes=[pyB])
            S.op("pe", lambda: nc.tensor.matmul(py[:, 0:tn], wbB[i2].t[:, o:o + 128], A2.t[:, t0:t0 + tn], start=False, stop=(gl == 7)), reads=[wbB[i2].B, A2.B], writes=[pyB])
            if gl == 7:
                S.op("dve", lambda: nc.vector.scalar_tensor_tensor(out=yT[:, fc, t0:t0 + tn], in0=uT[:, fc, t0:t0 + tn], scalar=dcol.t[:, fc:fc + 1], in1=py[:, 0:tn],
                                                                   op0=ALU.mult, op1=ALU.add),
                     reads=[uB, dcol.B, pyB], writes=[yB])


def big_rows(regX):
    flat = regX[:].rearrange("p c t -> p (c t)")
    return [(flat[:, i * 1056:(i + 1) * 1056], Buf(f"big{i}")) for i in range(6)]


def u_proj(C, w_in, hT, hB, uT, uB):
    nc, S = C.nc, C.S
    for pc in range(4):
        view, vB = C.load_piece(wpiece(w_in, 0, 16, 1536 + pc * 256, 256), 16, 256)
        for mm in range(2):
            m = pc * 2 + mm
            for (t0, tn) in TILES:
                ps, psB = C.bank(0, 4)
                for k in range(16):
                    S.op("pe", lambda: nc.tensor.matmul(ps[:, 0:tn], view[:, k, mm * 128:(mm + 1) * 128], hT[:, k, t0:t0 + tn], start=(k == 0), stop=(k == 15)),
                         reads=[vB, hB], writes=[psB], inc=(k == 15))
                S.op("act", lambda: nc.scalar.copy(out=uT[:, m, t0:t0 + tn], in_=ps[:, 0:tn]), reads=[psB], writes=[uB])


def gelu_glu(C, es, w_glu, bglu, yT, yB, gT, gB, tmp, tmpB):
    nc, S = C.nc, C.S
    sg = [T(nc, es, f"g_sg{i}", [128, 512]) for i in range(2)]
    for m in range(8):
        y = yT[:, m, :]
        v_tt(S, nc, "dve", tmp, y, y, ALU.mult, [yB], [tmpB])
        v_ts(S, nc, "dve", tmp, tmp, 0.044715, 1.0, ALU.mult, ALU.add, [tmpB], [tmpB])
        v_tt(S, nc, "dve", tmp, tmp, y, ALU.mult, [tmpB, yB], [tmpB])
        S.op("act", lambda: nc.scalar.activation(out=tmp, in_=tmp, func=AF.Sigmoid, scale=1.5957691216057308), reads=[tmpB], writes=[tmpB])
        v_tt(S, nc, "dve", y, y, tmp, ALU.mult, [yB, tmpB], [yB])
        S.op("act", lambda: nc.scalar.copy(out=gT[:, m, :], in_=y), reads=[yB], writes=[gB])
    r = 0
    for pc in range(4):
        view, vB = C.load_piece(wpiece(w_glu, 0, 8, pc * 256, 256), 8, 256)
        for mm in range(2):
            m = pc * 2 + mm
            for (t0, tn) in TILES:
                ps, psB = C.bank(0, 4)
                for k in range(8):
                    S.op("pe", lambda: nc.tensor.matmul(ps[:, 0:tn], view[:, k, mm * 128:(mm + 1) * 128], gT[:, k, t0:t0 + tn], start=(k == 0), stop=(k == 7)),
                         reads=[vB, gB], writes=[psB], inc=(k == 7))
                j = r % 2; r += 1
                S.op("act", lambda: nc.scalar.activation(out=sg[j].t[:, 0:tn], in_=ps[:, 0:tn], func=AF.Sigmoid, bias=bglu.t[:, m:m + 1]),
                     reads=[psB, bglu.B], writes=[sg[j].B])
                v_tt(S, nc, "dve", yT[:, m, t0:t0 + tn], yT[:, m, t0:t0 + tn], sg[j].t[:, 0:tn], ALU.mult, [yB, sg[j].B], [yB])

def _din(nc, name, shape, dt=F32):
    return nc.dram_tensor(name, shape, dt, kind="ExternalInput").ap()


def _dout(nc, name, shape, dt=F32):
    return nc.dram_tensor(name, shape, dt, kind="ExternalOutput").ap()


def _ssm_inputs(nc):
    return dict(lamS=_din(nc, "lamS", [128, 3, 64]), lamB=_din(nc, "lamB", [128, 3, 512]), bB=_din(nc, "bB", [128, 2, 512]),
                cS=_din(nc, "cS", [128, 2, 64, 16]), tvals=_din(nc, "tvals", [128, NTAB]), mask8=_din(nc, "mask8", [128, 8]),
                sgn=_din(nc, "sgn", [128, 1]))


def build_A():
    nc = bass.Bass("TRN2", target_bir_lowering=False)
    xin = _din(nc, "xT_in", [128, 16, NT])
    w_in = _din(nc, "w_in", [D, 2560])
    gmix = _din(nc, "g_mix", [128, 16])
    gk_in = _din(nc, "gk2", [128, 1])
    si = _ssm_inputs(nc)
    kth_out = _dout(nc, "kth_out", [128, 4, 128], BF16)
    vh_out = _dout(nc, "vh_out", [128, 512], BF16)
    wend_out = _dout(nc, "wend_out", [128, 2, 64])
    with ExitStack() as es:
        C = Ctx(nc, es)
        S = C.S
        C.consts()
        regX = es.enter_context(sbt(nc, "regX", [128, 16, NT], F32)); xB = Buf("xT")
        regH = es.enter_context(sbt(nc, "regH", [128, 16, NT], BF16)); hB = Buf("regH")
        uT = es.enter_context(sbt(nc, "uT", [128, 8, NT], BF16)); uB = Buf("uT")
        gm = T(nc, es, "gm", [128, 16]); S.dma("sp", gm.t[:], gmix, writes=[gm.B])
        A = type("A", (), {})()
        A.gk = es.enter_context(sbt(nc, "sb_gk2", [128, 1], F32)); A.gkB = Buf("gk2")
        S.dma("sp", A.gk[:], gk_in, writes=[A.gkB])
        A.blk = es.enter_context(sbt(nc, "blkones", [128, 128], BF16)); A.blkB = Buf("blkones")
        S.op("dve", lambda: nc.vector.memset(A.blk[:], 0.0), writes=[A.blkB])
        S.op("dve", lambda: nc.vector.memset(A.blk[0:64, 0:64], 1.0), writes=[A.blkB])
        S.op("dve", lambda: nc.vector.memset(A.blk[64:128, 64:128], 1.0), writes=[A.blkB])
        S.dma("sp", regX[:], xin, writes=[xB])
        C.rmsnorm(regX, xB, 16, gm.t, gm.B, regH, hB)
        u_proj(C, w_in, regH, hB, uT, uB)
        with ExitStack() as es1:
            kT = [es1.enter_context(sbt(nc, f"kT{i}", [128, 4, KTW], BF16)) for i in range(2)]; kTB = Buf("kT")
            V2 = es1.enter_context(sbt(nc, "V2", [128, 10, 512], BF16)); V2B = Buf("V2")
            tmpq = es1.enter_context(sbt(nc, "tmpq", [128, 512], F32)); tmpqB = Buf("tmpq")
            kv_proj(C, A, w_in, regH, hB, kT, kTB, V2, V2B, None, None, tmpq, tmpqB, ktiles=[(912, 128)], vblocks=[(912, 128, 9)])
            S.dma("sp", kth_out[0:64], kT[0][0:64, :, 1056:1184], reads=[kTB], writes=[Buf("o1")])
            S.dma("sp", kth_out[64:128], kT[1][64:128, :, 1056:1184], reads=[kTB], writes=[Buf("o2")])
            S.dma("sp", vh_out, V2[:, 9, :], reads=[V2B], writes=[Buf("o3")])
            S.barrier_all()
        with ExitStack() as es2:
            P = ssm_prep(C, es2, si["lamS"], si["lamB"], si["bB"], si["cS"], si["tvals"], si["mask8"], si["sgn"])
            wend = T(nc, es2, "wend", [128, 2, 64])
            big = big_rows(regX)
            ssm_loop(C, es2, P, uT, uB, big, "A", wend=wend)
            S.dma("sp", wend_out, wend.t[:], reads=[wend.B], writes=[Buf("o4")])
            S.barrier_all()
        print("A instructions", S.n_ins, "waits", S.n_wait)
    return nc


def build_B(debug=False):
    nc = bass.Bass("TRN2", target_bir_lowering=False)
    xin = _din(nc, "xT_in", [128, 16, NT])
    xout = _dout(nc, "xT_out", [128, 16, NT])
    w_in = _din(nc, "w_in", [D, 2560])
    w_glu = _din(nc, "w_glu", [1024, 1024])
    w_out = _din(nc, "w_out", [D, D])
    w_up = _din(nc, "w_up", [D, DFF])
    w_down = _din(nc, "w_down", [DFF, D])
    gmix, gmlp = _din(nc, "g_mix", [128, 16]), _din(nc, "g_mlp", [128, 16])
    gao, gso = _din(nc, "g_ao", [128, 8]), _din(nc, "g_so", [128, 8])
    dcol_in, bglu_in = _din(nc, "dcol", [128, 8]), _din(nc, "bglu", [128, 8])
    dband, dmeta, dqm = _din(nc, "dband", [128, 2, 128]), _din(nc, "dmeta", [128, 128]), _din(nc, "dqm", [128, 16])
    bias_in, garg_in, valid_in = _din(nc, "abias", [128, 16]), _din(nc, "garg", [128, 128]), _din(nc, "valid", [128, 1])
    gq_in, gk_in = _din(nc, "gq2", [128, 1]), _din(nc, "gk2", [128, 1])
    kth_in, vh_in = _din(nc, "kth", [128, 4, 128], BF16), _din(nc, "vh", [128, 512], BF16)
    nat_in, swp_in = _din(nc, "nat", [128, 4, 64]), _din(nc, "swp", [128, 4, 64])
    si = _ssm_inputs(nc)
    if debug:
        dbg_out = _dout(nc, "dbg_out", [128, 16, NT])
    with ExitStack() as es:
        C = Ctx(nc, es)
        S = C.S
        C.consts()
        C.tmp_rr = 0
        regX = es.enter_context(sbt(nc, "regX", [128, 16, NT], F32)); xB = Buf("xT")
        regH = es.enter_context(sbt(nc, "regH", [128, 16, NT], BF16)); hB = Buf("regH")
        aT, aTB = regX[:, 0:8, :], Buf("aT")
        yT, yB = regX[:, 8:16, :], Buf("yT")
        def small(name, shape, src):
            t = T(nc, es, name, shape); S.dma("sp", t.t[:], src, writes=[t.B]); return t
        gm, gl2 = small("gm", [128, 16], gmix), small("gl2", [128, 16], gmlp)
        ga, gs = small("ga", [128, 8], gao), small("gs", [128, 8], gso)
        dcol, bglu = small("dcolS", [128, 8], dcol_in), small("bgluS", [128, 8], bglu_in)
        S.dma("sp", regX[:], xin, writes=[xB])
        C.rmsnorm(regX, xB, 16, gm.t, gm.B, regH, hB)
        S.barrier_all()
        with ExitStack() as esM:
            regR = esM.enter_context(sbt(nc, "regR", [128, 8, NT], BF16)); rB = Buf("regR")
            with ExitStack() as es2:
                u_proj(C, w_in, regH, hB, regR, rB)
                P = ssm_prep(C, es2, si["lamS"], si["lamB"], si["bB"], si["cS"], si["tvals"], si["mask8"], si["sgn"])
                xs = ssm_xstart(C, es2, P, nat_in, swp_in)
                big = big_rows(regX)
                with ExitStack() as es3:
                    ssm_loop(C, es3, P, regR, rB, big, "B", xs=xs, yT=yT, yB=yB, dcol=dcol)
                    S.barrier_all()
                with ExitStack() as es3:
                    gelu_glu(C, es3, w_glu, bglu, yT, yB, regR, rB, big[0][0][:, 0:NT], big[0][1])
                    S.barrier_all()
            with ExitStack() as es2:
                A = attn_consts(C, es2, dband, dmeta, dqm, bias_in, garg_in, valid_in, gq_in, gk_in)
                kT = [es2.enter_context(sbt(nc, f"kT{i}", [128, 4, KTW], BF16)) for i in range(2)]; kTB = Buf("kT")
                V2 = es2.enter_context(sbt(nc, "V2", [128, 10, 512], BF16)); V2B = Buf("V2")
                tmpq = es2.enter_context(sbt(nc, "tmpq", [128, 512], F32)); tmpqB = Buf("tmpq")
                kv_proj(C, A, w_in, regH, hB, kT, kTB, V2, V2B, kth_in, vh_in, tmpq, tmpqB)
                attention(C, A, es2, w_in, regH, hB, kT, kTB, V2, V2B, aT, aTB, tmpq, tmpqB)
                S.barrier_all()
        if debug:
            S.dma("sp", dbg_out, regX[:], reads=[aTB, yB], writes=[Buf("dbg")])
        C.rmsnorm(aT, aTB, 8, ga.t, ga.B, regH, hB, 0)
        C.rmsnorm(yT, yB, 8, gs.t, gs.B, regH, hB, 8)
        S.barrier_all()
        S.dma("sp", regX[:], xin, writes=[xB])
        dense_acc_into_x(C, w_out, 16, 0, regH, hB, regX, xB, 16, 256)
        S.barrier_all()
        with ExitStack() as es5:
            hid = [es5.enter_context(sbt(nc, f"hid{i}", [128, 8, NT], BF16)) for i in range(2)]
            hidB = [Buf(f"hid{i}") for i in range(2)]
            tmp = [es5.enter_context(sbt(nc, f"ftmp{i}", [128, 512], F32)) for i in range(2)]
            tmpB = [Buf(f"ftmp{i}") for i in range(2)]
            C.rmsnorm(regX, xB, 16, gl2.t, gl2.B, regH, hB)
            ffn(C, w_up, w_down, regH, hB, regX, xB, hid, hidB, tmp, tmpB)
            S.dma("sp", xout, regX[:], reads=[xB], writes=[Buf("xout")])
            S.barrier_all()
        print("B instructions", S.n_ins, "waits", S.n_wait)
    return nc


def ssm_xstart_f(C, es, P, wend_dram, fend):
    nc, S = C.nc, C.S
    xs = T(nc, es, "xstart", [128, 64])
    wd, wdB = wend_dram
    with ExitStack() as es2:
        nat = T(nc, es2, "x_nat", [128, 64]); S.dma("sp", nat.t[:], wd, reads=[wdB], writes=[nat.B])
        swp = T(nc, es2, "x_swp", [128, 64])
        S.dma("sp", swp.t[0:64], wd[64:128], reads=[wdB], writes=[swp.B])
        S.dma("sp", swp.t[64:128], wd[0:64], reads=[wdB], writes=[swp.B])
        an, kk, sn, cs = [T(nc, es2, f"x_t{i}", [128, 64]) for i in range(4)]
        v_ts(S, nc, "dve", an.t[:], P.thS.t[:], float(fend), None, ALU.mult, None, [P.thS.B], [an.B])
        sincos(C, an.t[:], an.B, kk.t[:], kk.B, sn.t[:], sn.B, cs.t[:], cs.B)
        v_ts(S, nc, "dve", sn.t[:], sn.t[:], P.sgn.t[:, 0:1], None, ALU.mult, None, [sn.B, P.sgn.B], [sn.B])
        v_tt(S, nc, "dve", xs.t[:], cs.t[:], nat.t[:], ALU.mult, [cs.B, nat.B], [xs.B])
        v_tt(S, nc, "dve", kk.t[:], sn.t[:], swp.t[:], ALU.mult, [sn.B, swp.B], [kk.B])
        v_tt(S, nc, "dve", xs.t[:], xs.t[:], kk.t[:], ALU.add, [xs.B, kk.B], [xs.B])
        C.S.barrier_all()
    return xs


def emit_pass(C, regX, regH, l, q, W, xin, xout, halo_prev, halo_cur, wend_prev, wend_cur, cst):
    nc, S = C.nc, C.S
    PFX[0] = f"_L{l}Q{q}"
    xB, hB = Buf("xT"), Buf("regH")
    aT, aTB = regX[:, 0:8, :], Buf("aT")
    yT, yB = regX[:, 8:16, :], Buf("yT")
    xin_ap, xinB = xin
    xout_ap, xoutB = xout
    with ExitStack() as es:
        def small(name, shape, src):
            t = T(nc, es, name, shape); S.dma("sp", t.t[:], src, writes=[t.B]); return t
        gm, gl2 = small("gm", [128, 16], W["g_mix"][l]), small("gl2", [128, 16], W["g_mlp"][l])
        ga, gs = small("ga", [128, 8], W["g_ao"][l]), small("gs", [128, 8], W["g_so"][l])
        dcol, bglu = small("dcolS", [128, 8], W["dcol"][l]), small("bgluS", [128, 8], W["bglu"][l])
        S.dma("sp", regX[:], xin_ap, reads=[xinB], writes=[xB])
        C.rmsnorm(regX, xB, 16, gm.t, gm.B, regH, hB)
        S.barrier_all()
        with ExitStack() as esM:
            regR = esM.enter_context(sbt(nc, "regR", [128, 8, NT], BF16)); rB = Buf("regR")
            with ExitStack() as es2:
                u_proj(C, W["w_in"][l], regH, hB, regR, rB)
                P = ssm_prep(C, es2, W["lamS"][l], W["lamB"][l], W["bB"][l], W["cS"][l], cst["tvals"], cst["mask8"], cst["sgn"])
                xs = None
                if q > 0:
                    xs = ssm_xstart_f(C, es2, P, wend_prev, 1040 if q == 1 else 1024)
                wend = T(nc, es2, "wend", [128, 64])
                big = big_rows(regX)
                with ExitStack() as es3:
                    ssm_loop(C, es3, P, regR, rB, big, "F", xs=xs, yT=yT, yB=yB, dcol=dcol, wend=wend, cont=(q == 0))
                    S.dma("sp", wend_cur[0], wend.t[:], reads=[wend.B], writes=[wend_cur[1]])
                    S.barrier_all()
                with ExitStack() as es3:
                    gelu_glu(C, es3, W["w_glu"][l], bglu, yT, yB, regR, rB, big[0][0][:, 0:NT], big[0][1])
                    S.barrier_all()
            with ExitStack() as es2:
                A = attn_consts(C, es2, cst["dband"], cst["dmeta"], cst["dqm"], W["abias"][l], cst["garg"][q], cst["valid"], W["gq2"][l], W["gk2"][l])
                kT = [es2.enter_context(sbt(nc, f"kT{i}", [128, 4, KTW], BF16)) for i in range(2)]; kTB = Buf("kT")
                V2 = es2.enter_context(sbt(nc, "V2", [128, 10, 512], BF16)); V2B = Buf("V2")
                tmpq = es2.enter_context(sbt(nc, "tmpq", [128, 512], F32)); tmpqB = Buf("tmpq")
                S.op("dve", lambda: nc.vector.memset(kT[0][:], 0.0), writes=[kTB])
                S.op("dve", lambda: nc.vector.memset(kT[1][:], 0.0), writes=[kTB])
                S.op("dve", lambda: nc.vector.memset(V2[:, 0, :], 0.0), writes=[V2B])
                if q > 0:
                    (hk, hkB), (hv, hvB) = halo_prev
                    S.dma("sp", kT[0][0:64, :, 32:160], hk[0:64], reads=[hkB], writes=[kTB])
                    S.dma("sp", kT[1][64:128, :, 32:160], hk[64:128], reads=[hkB], writes=[kTB])
                    S.dma("sp", V2[:, 1, :], hv, reads=[hvB], writes=[V2B])
                kv_proj(C, A, W["w_in"][l], regH, hB, kT, kTB, V2, V2B, None, None, tmpq, tmpqB, skip_init=True)
                (hk, hkB), (hv, hvB) = halo_cur
                S.dma("sp", hk[0:64], kT[0][0:64, :, 1056:1184], reads=[kTB], writes=[hkB])
                S.dma("sp", hk[64:128], kT[1][64:128, :, 1056:1184], reads=[kTB], writes=[hkB])
                S.dma("sp", hv, V2[:, 9, :], reads=[V2B], writes=[hvB])
                attention(C, A, es2, W["w_in"][l], regH, hB, kT, kTB, V2, V2B, aT, aTB, tmpq, tmpqB, skip_prev0=(q == 0), use_valid=False)
                S.barrier_all()
        C.rmsnorm(aT, aTB, 8, ga.t, ga.B, regH, hB, 0)
        C.rmsnorm(yT, yB, 8, gs.t, gs.B, regH, hB, 8)
        S.barrier_all()
        S.dma("sp", regX[:], xin_ap, reads=[xinB], writes=[xB])
        dense_acc_into_x(C, W["w_out"][l], 16, 0, regH, hB, regX, xB, 16, 256)
        S.barrier_all()
        with ExitStack() as es5:
            hid = [es5.enter_context(sbt(nc, f"hid{i}", [128, 8, NT], BF16)) for i in range(2)]
            hidB = [Buf(f"hid{i}") for i in range(2)]
            tmp = [es5.enter_context(sbt(nc, f"ftmp{i}", [128, 512], F32)) for i in range(2)]
            tmpB = [Buf(f"ftmp{i}") for i in range(2)]
            C.rmsnorm(regX, xB, 16, gl2.t, gl2.B, regH, hB)
            ffn(C, W["w_up"][l], W["w_down"][l], regH, hB, regX, xB, hid, hidB, tmp, tmpB)
            S.dma("sp", xout_ap, regX[:], reads=[xB], writes=[xoutB])
            S.barrier_all()


def build_F(nlayers=4, nq=4):
    nc = bass.Bass("TRN2", target_bir_lowering=False)
    xin = _din(nc, "xT_in", [4, 128, 16, NT])
    xout = _dout(nc, "xT_out", [4, 128, 16, NT])
    W = dict(w_in=_din(nc, "w_in", [4, D, 2560]), w_glu=_din(nc, "w_glu", [4, 1024, 1024]), w_out=_din(nc, "w_out", [4, D, D]),
             w_up=_din(nc, "w_up", [4, D, DFF]), w_down=_din(nc, "w_down", [4, DFF, D]),
             g_mix=_din(nc, "g_mix", [4, 128, 16]), g_mlp=_din(nc, "g_mlp", [4, 128, 16]), g_ao=_din(nc, "g_ao", [4, 128, 8]),
             g_so=_din(nc, "g_so", [4, 128, 8]), dcol=_din(nc, "dcol", [4, 128, 8]), bglu=_din(nc, "bglu", [4, 128, 8]),
             abias=_din(nc, "abias", [4, 128, 16]), gq2=_din(nc, "gq2", [4, 128, 1]), gk2=_din(nc, "gk2", [4, 128, 1]),
             lamS=_din(nc, "lamS", [4, 128, 3, 64]), lamB=_din(nc, "lamB", [4, 128, 3, 512]), bB=_din(nc, "bB", [4, 128, 2, 512]),
             cS=_din(nc, "cS", [4, 128, 2, 64, 16]))
    cst = dict(tvals=_din(nc, "tvals", [128, NTAB]), mask8=_din(nc, "mask8", [128, 8]), sgn=_din(nc, "sgn", [128, 1]),
               dband=_din(nc, "dband", [128, 2, 128]), dmeta=_din(nc, "dmeta", [128, 128]), dqm=_din(nc, "dqm", [128, 16]),
               garg=_din(nc, "garg", [4, 128, 128]), valid=_din(nc, "valid", [128, 1]))
    scr = [nc.dram_tensor(f"xscr{i}", [4, 128, 16, NT], F32).ap() for i in range(2)]
    scrB = [[Buf(f"xscr{i}_{q}") for q in range(4)] for i in range(2)]
    hk = [nc.dram_tensor(f"hk{i}", [128, 4, 128], BF16).ap() for i in range(2)]
    hv = [nc.dram_tensor(f"hv{i}", [128, 512], BF16).ap() for i in range(2)]
    hB_ = [(Buf(f"hk{i}"), Buf(f"hv{i}")) for i in range(2)]
    wd = [nc.dram_tensor(f"wd{i}", [128, 64], F32).ap() for i in range(2)]
    wdB = [Buf(f"wd{i}") for i in range(2)]
    with ExitStack() as es:
        C = Ctx(nc, es)
        S = C.S
        C.consts()
        C.tmp_rr = 0
        regX = es.enter_context(sbt(nc, "regX", [128, 16, NT], F32))
        regH = es.enter_context(sbt(nc, "regH", [128, 16, NT], BF16))
        xinB = Buf("xin")
        outB = [Buf(f"xout{q}") for q in range(4)]
        for l in range(nlayers):
            for q in range(nq):
                src = (xin[q], xinB) if l == 0 else (scr[(l - 1) % 2][q], scrB[(l - 1) % 2][q])
                dst = (xout[q], outB[q]) if l == nlayers - 1 else (scr[l % 2][q], scrB[l % 2][q])
                i, j = q % 2, (q + 1) % 2
                emit_pass(C, regX, regH, l, q, W, src, dst,
                          ((hk[j], hB_[j][0]), (hv[j], hB_[j][1])), ((hk[i], hB_[i][0]), (hv[i], hB_[i][1])),
                          (wd[j], wdB[j]), (wd[i], wdB[i]), cst)
        PFX[0] = ""
        print("F instructions", S.n_ins, "waits", S.n_wait)
    return nc

def ssm_host_layout(lre, lim, lst, bre, bim, cre, cim):
    lamS = np.zeros((128, 3, 64), np.float32)
    for ri in range(2):
        lamS[ri * 64:(ri + 1) * 64, 0, :] = lre.T
        lamS[ri * 64:(ri + 1) * 64, 1, :] = lim.T
        lamS[ri * 64:(ri + 1) * 64, 2, :] = lst[None, :]
    def layB(a_gn):
        a = a_gn.reshape(8, 8, 64)
        a = a.transpose(1, 0, 2)
        return np.repeat(a[:, None], 16, axis=1).reshape(128, 8 * 64)
    lamB = np.stack([layB(lre), layB(lim), layB(np.repeat(lst[:, None], 64, 1))], 1).astype(np.float32)
    def layBb(b):
        a = b.reshape(8, 8, 64, 16).transpose(1, 3, 0, 2)
        return a.reshape(128, 512)
    bB = np.stack([layBb(bre), layBb(bim)], 1).astype(np.float32)
    def layC(c):
        a = c.transpose(2, 0, 1)
        return np.concatenate([a, a], 0)
    cS = np.stack([layC(cre), layC(cim)], 1).astype(np.float32)
    return dict(lamS=lamS, lamB=np.ascontiguousarray(lamB), bB=np.ascontiguousarray(bB), cS=np.ascontiguousarray(cS))


def ssm_host_consts():
    tvals = np.broadcast_to(np.arange(NTAB, dtype=np.float32)[None], (128, NTAB)).copy()
    mask8 = np.zeros((128, 8), np.float32)
    for p in range(128):
        mask8[p, p // 16] = 1.0
    sgn = np.ones((128, 1), np.float32)
    sgn[:64] = -1.0
    return dict(tvals=tvals, mask8=mask8, sgn=sgn)


_PROGS = {}
NLAYERS = 4
DEBUG_LAST = {}
TRACE = False
STRICT = [True]


def _prog(name):
    if name not in _PROGS:
        _PROGS[name] = build_A() if name == "A" else build_B()
    return _PROGS[name]


def kernel_unfused(x, meta_tokens, norm_mix_g, w_in, q_norm_g, k_norm_g, attn_sinks, ssm_lambda_re, ssm_lambda_im,
           ssm_log_step, ssm_b_re, ssm_b_im, ssm_c_re, ssm_c_im, ssm_d, w_glu, b_glu, attn_out_g, ssm_out_g,
           w_out, norm_mlp_g, w_up, w_down):
    f = lambda a: np.asarray(a, dtype=np.float32)
    x, meta_tokens = f(x), f(meta_tokens)
    ncores = 8
    xs = []
    for c in range(ncores):
        b, q = c // 4, c % 4
        tok = np.concatenate([meta_tokens, x[b, 1024 * q:1024 * (q + 1)]], 0)
        xs.append(to_fm(tok))
    hconst = [attn_host_consts(c % 4) for c in range(ncores)]
    sconst = ssm_host_consts()
    zero_kth = np.zeros((128, 4, 128), np.float32).astype(BF16NP)
    zero_vh = np.zeros((128, 512), np.float32).astype(BF16NP)
    for l in range(NLAYERS):
        sl = ssm_host_layout(f(ssm_lambda_re[l]), f(ssm_lambda_im[l]), f(ssm_log_step[l]), f(ssm_b_re[l]), f(ssm_b_im[l]),
                             f(ssm_c_re[l]), f(ssm_c_im[l]))
        common = dict(w_in=f(w_in[l]), g_mix=gcols(f(norm_mix_g[l])), gk2=np.tile(f(k_norm_g[l]), 2).reshape(128, 1), **sl, **sconst)
        insA = [dict(xT_in=xs[c], **common) for c in range(ncores)]
        resA = run_bass_kernel_spmd(_prog("A"), insA, core_ids=list(range(ncores))).results
        commonB = dict(common, w_glu=f(w_glu[l]), w_out=f(w_out[l]), w_up=f(w_up[l]), w_down=f(w_down[l]),
                       g_mlp=gcols(f(norm_mlp_g[l])), g_ao=gcols(f(attn_out_g[l])), g_so=gcols(f(ssm_out_g[l])),
                       dcol=gcols(f(ssm_d[l])), bglu=gcols(f(b_glu[l])), abias=sink_bias(f(attn_sinks[l])),
                       gq2=np.tile(f(q_norm_g[l]), 2).reshape(128, 1))
        insB = []
        for c in range(ncores):
            b, q = c // 4, c % 4
            nat = np.zeros((128, 4, 64), np.float32)
            for p in range(q):
                nat[:, p, :] = resA[b * 4 + q - 1 - p]["wend_out"][:, 0, :]
            nat[:, q, :] = resA[c]["wend_out"][:, 1, :]
            swp = np.concatenate([nat[64:], nat[:64]], 0)
            kth = resA[c - 1]["kth_out"] if q > 0 else zero_kth
            vh = resA[c - 1]["vh_out"] if q > 0 else zero_vh
            insB.append(dict(xT_in=xs[c], kth=kth, vh=vh, nat=nat, swp=np.ascontiguousarray(swp), **commonB, **hconst[c]))
        rB_ = run_bass_kernel_spmd(_prog("B"), insB, core_ids=list(range(ncores)), trace=TRACE) if TRACE else run_bass_kernel_spmd(_prog("B"), insB, core_ids=list(range(ncores)))
        resB = rB_.results
        DEBUG_LAST["B_ns"] = rB_.exec_time_ns
        xs = [np.asarray(resB[c]["xT_out"], dtype=np.float32) for c in range(ncores)]
        DEBUG_LAST["resA"], DEBUG_LAST["resB"], DEBUG_LAST["xs"] = resA, resB, xs
    out = np.zeros((2, 4096, D), np.float32)
    for c in range(ncores):
        b, q = c // 4, c % 4
        out[b, 1024 * q:1024 * (q + 1)] = from_fm(xs[c])[NMETA:]
    return out


def _fused_inputs(inp):
    f = lambda a: np.asarray(a, dtype=np.float32)
    x, meta = f(inp["x"]), f(inp["meta_tokens"])
    L4 = range(4)
    sl = [ssm_host_layout(f(inp["ssm_lambda_re"][l]), f(inp["ssm_lambda_im"][l]), f(inp["ssm_log_step"][l]), f(inp["ssm_b_re"][l]),
                          f(inp["ssm_b_im"][l]), f(inp["ssm_c_re"][l]), f(inp["ssm_c_im"][l])) for l in L4]
    st = lambda fn: np.ascontiguousarray(np.stack([fn(l) for l in L4], 0))
    common = dict(
        w_in=f(inp["w_in"]), w_glu=f(inp["w_glu"]), w_out=f(inp["w_out"]), w_up=f(inp["w_up"]), w_down=f(inp["w_down"]),
        g_mix=st(lambda l: gcols(f(inp["norm_mix_g"][l]))), g_mlp=st(lambda l: gcols(f(inp["norm_mlp_g"][l]))),
        g_ao=st(lambda l: gcols(f(inp["attn_out_g"][l]))), g_so=st(lambda l: gcols(f(inp["ssm_out_g"][l]))),
        dcol=st(lambda l: gcols(f(inp["ssm_d"][l]))), bglu=st(lambda l: gcols(f(inp["b_glu"][l]))),
        abias=st(lambda l: sink_bias(f(inp["attn_sinks"][l]))),
        gq2=st(lambda l: np.tile(f(inp["q_norm_g"][l]), 2).reshape(128, 1)), gk2=st(lambda l: np.tile(f(inp["k_norm_g"][l]), 2).reshape(128, 1)),
        lamS=st(lambda l: sl[l]["lamS"]), lamB=st(lambda l: sl[l]["lamB"]), bB=st(lambda l: sl[l]["bB"]), cS=st(lambda l: sl[l]["cS"]),
        **ssm_host_consts())
    hc = [attn_host_consts(q) for q in range(4)]
    common.update(dband=hc[0]["dband"], dmeta=hc[0]["dmeta"], dqm=hc[0]["dqm"], valid=hc[1]["valid"],
                  garg=np.ascontiguousarray(np.stack([hc[q]["garg"] for q in range(4)], 0)))
    ins = []
    for c in range(8):
        b = c // 4
        xq = np.stack([to_fm(np.concatenate([meta, x[b, 1024 * q:1024 * (q + 1)]], 0)) for q in range(4)], 0)
        ins.append(dict(xT_in=np.ascontiguousarray(xq), **common))
    return ins


def kernel_fused(**inp):
    if "F" not in _PROGS:
        _PROGS["F"] = build_F(NLAYERS)
    res = run_bass_kernel_spmd(_PROGS["F"], _fused_inputs(inp), core_ids=list(range(8))).results
    out = np.zeros((2, 4096, D), np.float32)
    for b in range(2):
        xo = np.asarray(res[4 * b]["xT_out"], dtype=np.float32)
        for q in range(4):
            out[b, 1024 * q:1024 * (q + 1)] = from_fm(xo[q])[NMETA:]
    DEBUG_LAST["resF"] = res
    return out


def to_fm(tok):
    return np.ascontiguousarray(tok.T.reshape(16, 128, tok.shape[0]).transpose(1, 0, 2))


def from_fm(fm):
    return np.ascontiguousarray(fm.transpose(1, 0, 2).reshape(D, fm.shape[2]).T)


def gcols(g):
    return np.ascontiguousarray(g.reshape(-1, 128).T)


def kernel(**inputs):
    return kernel_fused(**inputs)
```

```python
import numpy as np
from contextlib import ExitStack
import concourse.bass as bass
import concourse.mybir as mybir
from concourse.bass_utils import run_bass_kernel_spmd
import ml_dtypes

BF16NP = ml_dtypes.bfloat16

F32 = mybir.dt.float32
BF16 = mybir.dt.bfloat16
AF = mybir.ActivationFunctionType
ALU = mybir.AluOpType

D = 2048
NT = 1040
NMETA = 16
DFF = 8192
EPS = 1e-6
TILES = [(0, 16), (16, 512), (528, 512)]
SLOT = 4096
NSLOT = 3


PFX = [""]


def sbt(nc, name, shape, dt):
    return nc.sbuf_tensor(name + PFX[0], shape, dt)


class Buf:
    __slots__ = ("name", "w", "r")

    def __init__(self, name):
        self.name = name
        self.w = None
        self.r = {}


class Sched:
    def __init__(self, nc, es, strict_same=True, n_dma_sems=8):
        self.nc = nc
        self.E = {"pe": nc.tensor, "act": nc.scalar, "dve": nc.vector, "pool": nc.gpsimd, "sp": nc.sync}
        self.sem, self.cnt, self.pending = {}, {}, {}
        for e in self.E:
            self.sem[e] = es.enter_context(nc.semaphore("s_" + e))
            self.cnt[e] = 0
            self.pending[e] = False
        self.known = {e: {} for e in self.E}
        self.strict_same = strict_same
        self.dsem, self.dcnt, self.drr = {}, {}, {}
        for e in ("sp", "pool"):
            self.dsem[e] = [es.enter_context(nc.semaphore(f"d_{e}{i}")) for i in range(n_dma_sems)]
            self.dcnt[e] = [0] * n_dma_sems
            self.drr[e] = 0
        self.n_wait = 0
        self.n_ins = 0

    def _wait(self, e, tok):
        sem, val, src = tok
        if src == e and (e == "pe" or not self.strict_same):
            return
        k = id(sem)
        if self.known[e].get(k, 0) >= val:
            return
        self.E[e].wait_ge(sem, val)
        self.known[e][k] = val
        self.n_wait += 1

    def _deps(self, e, reads, writes):
        for b in reads:
            if b.w is not None:
                self._wait(e, b.w)
        for b in writes:
            if b.w is not None:
                self._wait(e, b.w)
            for t in b.r.values():
                self._wait(e, t)

    def _mark(self, tok, reads, writes):
        for b in reads:
            b.r[id(tok[0])] = tok
        for b in writes:
            b.w = tok
            b.r = {}

    def op(self, e, emit, reads=(), writes=(), inc=True):
        self._deps(e, reads, writes)
        ins = emit()
        self.n_ins += 1
        if inc:
            self.cnt[e] += 1
            ins.then_inc(self.sem[e], 1)
            self.pending[e] = False
            tok = (self.sem[e], self.cnt[e], e)
        else:
            self.pending[e] = True
            tok = (self.sem[e], self.cnt[e] + 1, e)
        self._mark(tok, reads, writes)
        return ins

    def dma(self, q, out, in_, reads=(), writes=()):
        self._deps(q, reads, writes)
        i = self.drr[q]
        self.drr[q] = (i + 1) % len(self.dsem[q])
        sem = self.dsem[q][i]
        if self.dcnt[q][i] > 0:
            self._wait(q, (sem, 16 * self.dcnt[q][i], "dma"))
        ins = self.E[q].dma_start(out=out, in_=in_)
        self.n_ins += 1
        self.dcnt[q][i] += 1
        ins.then_inc(sem, 16)
        tok = (sem, 16 * self.dcnt[q][i], "dma")
        self._mark(tok, reads, writes)
        return ins

    def barrier_all(self):
        toks = []
        for e in self.E:
            if e == "sp":
                continue
            assert not self.pending[e], e
            if self.cnt[e] > 0:
                toks.append((self.sem[e], self.cnt[e], e))
        for q in self.dsem:
            for i, sem in enumerate(self.dsem[q]):
                if self.dcnt[q][i] > 0:
                    toks.append((sem, 16 * self.dcnt[q][i], "dma"))
        for e in self.E:
            for t in toks:
                if t[2] == e:
                    continue
                self._wait(e, t)


class Ctx:
    def __init__(self, nc, es):
        self.nc = nc
        self.es = es
        self.S = Sched(nc, es, strict_same=STRICT[0])
        S = self.S
        self.slots = [es.enter_context(sbt(nc, f"wslot{i}", [128, SLOT], BF16)) for i in range(NSLOT)]
        self.slotB = [Buf(f"wslot{i}") for i in range(NSLOT)]
        self.slot_rr = 0
        self.ps = [es.enter_context(nc.psum_tensor(f"ps{i}", [128, 512], F32)) for i in range(8)]
        self.psB = [Buf(f"ps{i}") for i in range(8)]
        self.ps_rr = 0
        self.ones_bf = es.enter_context(sbt(nc, "ones_bf", [128, 128], BF16))
        self.onesB = Buf("ones")
        S.op("dve", lambda: nc.vector.memset(self.ones_bf[:], 1.0), writes=[self.onesB])
        self.sq = [es.enter_context(sbt(nc, f"sq{i}", [128, 512], BF16)) for i in range(2)]
        self.sqB = [Buf(f"sq{i}") for i in range(2)]
        self.sq_rr = 0
        self.rstd = es.enter_context(sbt(nc, "rstd", [128, NT], F32))
        self.rstdB = Buf("rstd")
        self.lnt = es.enter_context(sbt(nc, "lnt", [128, 512], F32))
        self.lntB = Buf("lnt")

    def bank(self, lo=0, hi=6):
        n = hi - lo
        i = lo + (self.ps_rr % n)
        self.ps_rr += 1
        return self.ps[i], self.psB[i]

    def load_piece(self, src_ap, nk, ncols):
        i = self.slot_rr % NSLOT
        self.slot_rr += 1
        view = self.slots[i][:, 0:nk * ncols].rearrange("p (k f) -> p k f", k=nk)
        self.S.dma("pool", view, src_ap, writes=[self.slotB[i]])
        return view, self.slotB[i]

    def rmsnorm(self, src, srcB, ndc, gcol, gB, dst, dstB, dst_dc0=0):
        nc, S = self.nc, self.S
        inv = 1.0 / (ndc * 128)
        for (t0, tn) in TILES:
            ps, psB = self.bank(6, 8)
            for dc in range(ndc):
                j = self.sq_rr % 2
                self.sq_rr += 1
                sq, sqB = self.sq[j], self.sqB[j]
                S.op("act", lambda: nc.scalar.activation(out=sq[:, 0:tn], in_=src[:, dc, t0:t0 + tn], func=AF.Square),
                     reads=[srcB], writes=[sqB])
                S.op("pe", lambda: nc.tensor.matmul(ps[:, 0:tn], self.ones_bf[:], sq[:, 0:tn], start=(dc == 0), stop=(dc == ndc - 1)),
                     reads=[sqB, self.onesB], writes=[psB], inc=True)
            S.op("act", lambda: nc.scalar.activation(out=self.lnt[:, 0:tn], in_=ps[:, 0:tn], func=AF.Ln, scale=inv, bias=self.epsc[:, 0:1]),
                 reads=[psB, self.epsB], writes=[self.lntB])
            S.op("act", lambda: nc.scalar.activation(out=self.rstd[:, t0:t0 + tn], in_=self.lnt[:, 0:tn], func=AF.Exp, scale=-0.5),
                 reads=[self.lntB], writes=[self.rstdB])
        for dc in range(ndc):
            S.op("dve", lambda: nc.vector.scalar_tensor_tensor(out=dst[:, dst_dc0 + dc, :], in0=src[:, dc, :], scalar=gcol[:, dc:dc + 1],
                                                               in1=self.rstd[:, :], op0=ALU.mult, op1=ALU.mult),
                 reads=[srcB, gB, self.rstdB], writes=[dstB])

    def consts(self):
        nc, S = self.nc, self.S
        self.epsc = self.es.enter_context(sbt(nc, "epsc", [128, 1], F32))
        self.epsB = Buf("epsc")
        S.op("dve", lambda: nc.vector.memset(self.epsc[:], EPS), writes=[self.epsB])


def wpiece(w_ap, k0, nk, c0, ncols):
    return w_ap.rearrange("(kc p) f -> p kc f", p=128)[:, k0:k0 + nk, c0:c0 + ncols]


def dense_acc_into_x(C, w_ap, nkc, row0_chunk, act, actB, xT, xB, m_chunks, piece_cols):
    nc, S = C.nc, C.S
    per = piece_cols // 128
    for p0 in range(0, m_chunks, per):
        view, vB = C.load_piece(wpiece(w_ap, row0_chunk, nkc, p0 * 128, piece_cols), nkc, piece_cols)
        for mm in range(per):
            m = p0 + mm
            for (t0, tn) in TILES:
                ps, psB = C.bank()
                for k in range(nkc):
                    S.op("pe", lambda: nc.tensor.matmul(ps[:, 0:tn], view[:, k, mm * 128:(mm + 1) * 128], act[:, k, t0:t0 + tn],
                                                        start=(k == 0), stop=(k == nkc - 1)),
                         reads=[vB, actB], writes=[psB], inc=(k == nkc - 1))
                S.op("dve", lambda: nc.vector.tensor_tensor(out=xT[:, m, t0:t0 + tn], in0=ps[:, 0:tn], in1=xT[:, m, t0:t0 + tn], op=ALU.add),
                     reads=[psB, xB], writes=[xB])


def ffn(C, w_up, w_down, h2T, h2B, xT, xB, hid, hidB, tmp, tmpB):
    nc, S = C.nc, C.S
    NFG = DFF // 1024
    for fg in range(NFG):
        hb, hbB = hid[fg % 2], hidB[fg % 2]
        for pc in range(4):
            view, vB = C.load_piece(wpiece(w_up, 0, 16, fg * 1024 + pc * 256, 256), 16, 256)
            for mm in range(2):
                j = pc * 2 + mm
                for (t0, tn) in TILES:
                    ps, psB = C.bank()
                    for k in range(16):
                        S.op("pe", lambda: nc.tensor.matmul(ps[:, 0:tn], view[:, k, mm * 128:(mm + 1) * 128], h2T[:, k, t0:t0 + tn],
                                                            start=(k == 0), stop=(k == 15)),
                             reads=[vB, h2B], writes=[psB], inc=(k == 15))
                    tb = C.tmp_rr % 2
                    C.tmp_rr += 1
                    S.op("act", lambda: nc.scalar.activation(out=tmp[tb][:, 0:tn], in_=ps[:, 0:tn], func=AF.Relu),
                         reads=[psB], writes=[tmpB[tb]])
                    S.op("dve", lambda: nc.vector.tensor_tensor(out=hb[:, j, t0:t0 + tn], in0=tmp[tb][:, 0:tn], in1=tmp[tb][:, 0:tn], op=ALU.mult),
                         reads=[tmpB[tb]], writes=[hbB])
        dense_acc_into_x(C, w_down, 8, fg * 8, hb, hbB, xT, xB, 16, 512)


KTW = 1184
SLOPES = [2.0 ** (-(h + 1) / 2.0) for h in range(16)]


def dup_cols(ap2, reps):
    a = [list(x) for x in ap2.ap]
    return bass.AP(ap2.tensor, ap2.offset, [a[0], [0, reps]] + a[1:])


def attn_consts(C, es, dband_in, dmeta_in, dqm_in, bias_in, garg_in, valid_in, gq_in, gk_in):
    nc, S = C.nc, C.S
    A = type("A", (), {})()
    def ld(name, shape, src):
        t = es.enter_context(sbt(nc, "sb_" + name, shape, F32))
        b = Buf(name)
        S.dma("sp", t[:], src, writes=[b])
        return t, b
    A.dband, A.dbandB = ld("dband", [128, 2, 128], dband_in)
    A.dmeta, A.dmetaB = ld("dmeta", [128, 128], dmeta_in)
    A.dqm, A.dqmB = ld("dqm", [128, 16], dqm_in)
    A.bias, A.biasB = ld("abias", [128, 16], bias_in)
    A.garg, A.gargB = ld("garg", [128, 128], garg_in)
    A.valid, A.validB = ld("valid", [128, 1], valid_in)
    A.gq, A.gqB = ld("gq2", [128, 1], gq_in)
    A.gk, A.gkB = ld("gk2", [128, 1], gk_in)
    S.op("act", lambda: nc.scalar.activation(out=A.garg[:], in_=A.garg[:], func=AF.Exp), reads=[A.gargB], writes=[A.gargB])
    A.blk = es.enter_context(sbt(nc, "blkones", [128, 128], BF16))
    A.blkB = Buf("blkones")
    S.op("dve", lambda: nc.vector.memset(A.blk[:], 0.0), writes=[A.blkB])
    S.op("dve", lambda: nc.vector.memset(A.blk[0:64, 0:64], 1.0), writes=[A.blkB])
    S.op("dve", lambda: nc.vector.memset(A.blk[64:128, 64:128], 1.0), writes=[A.blkB])
    return A


def headnorm_evac(C, A, ps, psB, tn, gcol, gB, out_ap, outB, tmpq, tmpqB):
    nc, S = C.nc, C.S
    j = C.sq_rr % 2
    C.sq_rr += 1
    sq, sqB = C.sq[j], C.sqB[j]
    S.op("act", lambda: nc.scalar.activation(out=sq[:, 0:tn], in_=ps[:, 0:tn], func=AF.Square), reads=[psB], writes=[sqB])
    st, stB = C.ps[2], C.psB[2]
    S.op("pe", lambda: nc.tensor.matmul(st[:, 0:tn], A.blk[:], sq[:, 0:tn], start=True, stop=True), reads=[sqB, A.blkB], writes=[stB])
    S.op("act", lambda: nc.scalar.activation(out=C.lnt[:, 0:tn], in_=st[:, 0:tn], func=AF.Ln, scale=1.0 / 64, bias=C.epsc[:, 0:1]),
         reads=[stB, C.epsB], writes=[C.lntB])
    S.op("act", lambda: nc.scalar.activation(out=tmpq[:, 0:tn], in_=C.lnt[:, 0:tn], func=AF.Exp, scale=-0.5), reads=[C.lntB], writes=[tmpqB])
    if isinstance(out_ap, tuple):
        for (r0, oap) in ((0, out_ap[0]), (64, out_ap[1])):
            S.op("dve", lambda: nc.vector.scalar_tensor_tensor(out=oap, in0=ps[r0:r0 + 64, 0:tn], scalar=gcol[r0:r0 + 64, 0:1], in1=tmpq[r0:r0 + 64, 0:tn],
                                                               op0=ALU.mult, op1=ALU.mult),
                 reads=[psB, gB, tmpqB], writes=[outB])
    else:
        S.op("dve", lambda: nc.vector.scalar_tensor_tensor(out=out_ap, in0=ps[:, 0:tn], scalar=gcol[:, 0:1], in1=tmpq[:, 0:tn], op0=ALU.mult, op1=ALU.mult),
             reads=[psB, gB, tmpqB], writes=[outB])


def kv_proj(C, A, w_in, hT, hB, kT, kTB, V2, V2B, kth_in, vh_in, tmpq, tmpqB, ktiles=None, vblocks=None, skip_init=False):
    nc, S = C.nc, C.S
    if not skip_init:
        S.op("dve", lambda: nc.vector.memset(kT[0][:], 0.0), writes=[kTB])
        S.op("dve", lambda: nc.vector.memset(kT[1][:], 0.0), writes=[kTB])
        S.op("dve", lambda: nc.vector.memset(V2[:, 0, :], 0.0), writes=[V2B])
    if kth_in is not None:
        S.dma("sp", kT[0][0:64, :, 32:160], kth_in[0:64], writes=[kTB])
        S.dma("sp", kT[1][64:128, :, 32:160], kth_in[64:128], writes=[kTB])
        S.dma("sp", V2[:, 1, :], vh_in, writes=[V2B])
    view, vB = C.load_piece(wpiece(w_in, 0, 16, 1024, 256), 16, 256)
    for kv in range(4):
        for (t0, tn) in (ktiles or TILES):
            ps, psB = C.bank(0, 2)
            for half in range(2):
                for k in range(16):
                    lhsT = view[:, k, kv * 64:(kv + 1) * 64]
                    S.op("pe", lambda: nc.tensor.matmul(ps[half * 64:(half + 1) * 64, 0:tn], lhsT, hT[:, k, t0:t0 + tn], start=(k == 0), stop=(k == 15),
                                                        tile_position=(0, half * 64)),
                         reads=[vB, hB], writes=[psB], inc=(k == 15))
            c0 = t0 if t0 < 16 else t0 + 144
            headnorm_evac(C, A, ps, psB, tn, A.gk, A.gkB, (kT[0][0:64, kv, c0:c0 + tn], kT[1][64:128, kv, c0:c0 + tn]), kTB, tmpq, tmpqB)
    view, vB = C.load_piece(wpiece(w_in, 0, 16, 1280, 256), 16, 256)
    blocks = vblocks or ([(0, 16, 0)] + [(16 + 128 * j, 128, 2 + j) for j in range(8)])
    for (t0, tn, idx) in blocks:
        ps, psB = C.bank(0, 2)
        for k in range(16):
            r = view[:, k, :]
            a = [list(x) for x in r.ap]
            rhs = bass.AP(r.tensor, r.offset, [a[0], [64, 4], [0, 2], [1, 64]])
            S.op("pe", lambda: nc.tensor.matmul(ps[0:tn, :], hT[:, k, t0:t0 + tn], rhs, start=(k == 0), stop=(k == 15)),
                 reads=[vB, hB], writes=[psB], inc=(k == 15))
        S.op("act", lambda: nc.scalar.copy(out=V2[0:tn, idx, :], in_=ps[0:tn, :]), reads=[psB], writes=[V2B])


def attention(C, A, es, w_in, hT, hB, kT, kTB, V2, V2B, aT, aTB, tmpq, tmpqB, nkv=4, jbs=range(-1, 8), skip_prev0=False, use_valid=True):
    nc, S = C.nc, C.S
    def sb(name, shape, dt):
        return es.enter_context(sbt(nc, name, shape, dt)), Buf(name)
    q2, q2B = sb("q2", [128, 2, NT], BF16)
    wtab, wtabB = sb("wtab", [128, 2, 512], F32)
    wmeta, wmetaB = sb("wmeta", [128, 512], F32)
    wqm, wqmB = sb("wqm", [128, 64], F32)
    expS = [sb(f"expS{i}", [128, 512], F32) for i in range(2)]
    pt = [sb(f"pt{i}", [128, 512], BF16) for i in range(6)]
    rec, recB = sb("rec", [128, 512], F32)
    exp_rr = 0
    for kv in range(nkv):
        view, vB = C.load_piece(wpiece(w_in, 0, 16, kv * 256, 256), 16, 256)
        for mm in range(2):
            for (t0, tn) in TILES:
                ps, psB = C.bank(0, 2)
                for k in range(16):
                    S.op("pe", lambda: nc.tensor.matmul(ps[:, 0:tn], view[:, k, mm * 128:(mm + 1) * 128], hT[:, k, t0:t0 + tn],
                                                        start=(k == 0), stop=(k == 15)),
                         reads=[vB, hB], writes=[psB], inc=(k == 15))
                headnorm_evac(C, A, ps, psB, tn, A.gq, A.gqB, q2[:, mm, t0:t0 + tn], q2B, tmpq, tmpqB)
        for hh in range(4):
            h = 4 * kv + hh
            for tl in range(2):
                S.op("act", lambda: nc.scalar.activation(out=wtab[:, tl, hh * 128:(hh + 1) * 128], in_=A.dband[:, tl, :], func=AF.Exp, scale=-SLOPES[h]),
                     reads=[A.dbandB], writes=[wtabB])
            S.op("act", lambda: nc.scalar.activation(out=wmeta[:, hh * 128:(hh + 1) * 128], in_=A.dmeta[:, :], func=AF.Exp, scale=-SLOPES[h], bias=A.bias[:, h:h + 1]),
                 reads=[A.dmetaB, A.biasB], writes=[wmetaB])
            S.op("act", lambda: nc.scalar.activation(out=wqm[:, hh * 16:(hh + 1) * 16], in_=A.dqm[:, :], func=AF.Exp, scale=-SLOPES[h], bias=A.bias[:, h:h + 1]),
                 reads=[A.dqmB, A.biasB], writes=[wqmB])
        jl = list(jbs)

        def stage1(idx):
            nonlocal exp_rr
            jb = jl[idx]
            par = idx % 2
            nq = 16 if jb < 0 else 128
            tq0 = 0 if jb < 0 else 16 + 128 * jb
            W4 = 4 * nq
            tiles = []
            if jb >= 0 and not (skip_prev0 and jb == 0):
                tiles.append((3 * par + 0, 32 + 128 * jb, 128, 1 + jb, "prev"))
            if jb >= 0:
                tiles.append((3 * par + 1, 160 + 128 * jb, 128, 2 + jb, "cur"))
            tiles.append((3 * par + 2, 0, 128, 0, "meta"))
            pts = []
            for ti, (bk, kc0, K, vidx, kind) in enumerate(tiles):
                ps, psB = C.ps[bk], C.psB[bk]
                for hh in range(4):
                    mm, half = hh // 2, hh % 2
                    S.op("pe", lambda: nc.tensor.matmul(ps[0:K, hh * nq:(hh + 1) * nq], kT[half][:, kv, kc0:kc0 + K], q2[:, mm, tq0:tq0 + nq],
                                                        start=True, stop=True),
                         reads=[kTB, q2B], writes=[psB], inc=(hh == 3))
                ex, exB = expS[exp_rr % 2]
                exp_rr += 1
                S.op("act", lambda: nc.scalar.activation(out=ex[0:K, 0:W4], in_=ps[0:K, 0:W4], func=AF.Exp, scale=0.125), reads=[psB], writes=[exB])
                p, pB = pt[3 * par + ti]
                if kind == "prev":
                    if jb == 0 and use_valid:
                        S.op("dve", lambda: nc.vector.scalar_tensor_tensor(out=p[:, 0:512], in0=ex[:, 0:512], scalar=A.valid[:, 0:1], in1=wtab[:, 0, :], op0=ALU.mult, op1=ALU.mult),
                             reads=[exB, A.validB, wtabB], writes=[pB])
                    else:
                        S.op("dve", lambda: nc.vector.tensor_tensor(out=p[:, 0:512], in0=ex[:, 0:512], in1=wtab[:, 0, :], op=ALU.mult), reads=[exB, wtabB], writes=[pB])
                elif kind == "cur":
                    S.op("dve", lambda: nc.vector.tensor_tensor(out=p[:, 0:512], in0=ex[:, 0:512], in1=wtab[:, 1, :], op=ALU.mult), reads=[exB, wtabB], writes=[pB])
                else:
                    if jb < 0:
                        S.op("dve", lambda: nc.vector.tensor_tensor(out=p[:, 0:64], in0=ex[:, 0:64], in1=wqm[:, :], op=ALU.mult), reads=[exB, wqmB], writes=[pB])
                    else:
                        for hh in range(4):
                            h = 4 * kv + hh
                            S.op("dve", lambda: nc.vector.scalar_tensor_tensor(out=p[:, hh * 128:(hh + 1) * 128], in0=ex[:, hh * 128:(hh + 1) * 128],
                                                                               scalar=A.garg[:, h * 8 + jb:h * 8 + jb + 1], in1=wmeta[:, hh * 128:(hh + 1) * 128],
                                                                               op0=ALU.mult, op1=ALU.mult),
                                 reads=[exB, A.gargB, wmetaB], writes=[pB])
                pts.append((p, pB, K, vidx))
            return (nq, tq0, W4, pts)

        def stage2(st):
            nq, tq0, W4, pts = st
            num, numB = C.ps[6], C.psB[6]
            den, denB = C.ps[7], C.psB[7]
            for ti, (p, pB, K, vidx) in enumerate(pts):
                S.op("pe", lambda: nc.tensor.matmul(num[:, 0:W4], V2[0:K, vidx, kv * 128:(kv + 1) * 128], p[0:K, 0:W4], start=(ti == 0), stop=(ti == len(pts) - 1)),
                     reads=[V2B, pB], writes=[numB], inc=True)
                S.op("pe", lambda: nc.tensor.matmul(den[:, 0:W4], C.ones_bf[0:K, :], p[0:K, 0:W4], start=(ti == 0), stop=(ti == len(pts) - 1)),
                     reads=[C.onesB, pB], writes=[denB], inc=True)
            S.op("dve", lambda: nc.vector.reciprocal(out=rec[:, 0:W4], in_=den[:, 0:W4]), reads=[denB], writes=[recB])
            for hh in range(4):
                mm, half = hh // 2, hh % 2
                r0 = half * 64
                c = 2 * kv + mm
                S.op("dve", lambda: nc.vector.tensor_tensor(out=aT[r0:r0 + 64, c, tq0:tq0 + nq], in0=num[r0:r0 + 64, hh * nq:(hh + 1) * nq],
                                                            in1=rec[r0:r0 + 64, hh * nq:(hh + 1) * nq], op=ALU.mult),
                     reads=[numB, recB], writes=[aTB])

        prev = None
        for idx in range(len(jl)):
            cur = stage1(idx)
            if prev is not None:
                stage2(prev)
            prev = cur
        if prev is not None:
            stage2(prev)


def build_attn_test(stage=2, nkv=4, jbs=range(-1, 8)):
    nc = bass.Bass("TRN2", target_bir_lowering=False)
    def din(name, shape, dt=F32):
        return nc.dram_tensor(name, shape, dt, kind="ExternalInput").ap()
    xin = din("xT_in", [128, 16, NT])
    w_in = din("w_in", [D, 2560])
    gmix = din("g_mix", [128, 16])
    dband, dmeta, dqm = din("dband", [128, 2, 128]), din("dmeta", [128, 128]), din("dqm", [128, 16])
    bias_in, garg_in, valid_in = din("abias", [128, 16]), din("garg", [128, 128]), din("valid", [128, 1])
    gq_in, gk_in = din("gq2", [128, 1]), din("gk2", [128, 1])
    kth_in, vh_in = din("kth", [128, 4, 128], BF16), din("vh", [128, 512], BF16)
    aout = nc.dram_tensor("aT_out", [128, 8, NT], F32, kind="ExternalOutput").ap()
    kout = nc.dram_tensor("kT_out", [128, 4, KTW], BF16, kind="ExternalOutput").ap()
    kout2 = nc.dram_tensor("kT_out2", [128, 4, KTW], BF16, kind="ExternalOutput").ap()
    vout = nc.dram_tensor("V2_out", [128, 10, 512], BF16, kind="ExternalOutput").ap()
    with ExitStack() as es:
        C = Ctx(nc, es)
        S = C.S
        C.consts()
        xT = es.enter_context(sbt(nc, "xT", [128, 16, NT], F32)); xB = Buf("xT")
        regH = es.enter_context(sbt(nc, "regH", [128, 16, NT], BF16)); hB = Buf("regH")
        gm = es.enter_context(sbt(nc, "gm", [128, 16], F32)); gmB = Buf("gm")
        kT = [es.enter_context(sbt(nc, f"kT{i}", [128, 4, KTW], BF16)) for i in range(2)]; kTB = Buf("kT")
        V2 = es.enter_context(sbt(nc, "V2", [128, 10, 512], BF16)); V2B = Buf("V2")
        aT = es.enter_context(sbt(nc, "aT", [128, 8, NT], F32)); aTB = Buf("aT")
        tmpq = es.enter_context(sbt(nc, "tmpq", [128, 512], F32)); tmpqB = Buf("tmpq")
        S.dma("sp", xT[:], xin, writes=[xB])
        S.dma("sp", gm[:], gmix, writes=[gmB])
        A = attn_consts(C, es, dband, dmeta, dqm, bias_in, garg_in, valid_in, gq_in, gk_in)
        C.rmsnorm(xT, xB, 16, gm, gmB, regH, hB)
        kv_proj(C, A, w_in, regH, hB, kT, kTB, V2, V2B, kth_in, vh_in, tmpq, tmpqB)
        S.dma("sp", kout, kT[0][:], reads=[kTB], writes=[Buf("kout")])
        S.dma("sp", kout2, kT[1][:], reads=[kTB], writes=[Buf("kout2")])
        S.dma("sp", vout, V2[:], reads=[V2B], writes=[Buf("vout")])
        with ExitStack() as es2:
            if stage >= 2:
                attention(C, A, es2, w_in, regH, hB, kT, kTB, V2, V2B, aT, aTB, tmpq, tmpqB, nkv, jbs)
                S.dma("sp", aout, aT[:], reads=[aTB], writes=[Buf("aout")])
            S.barrier_all()
        print("instructions", S.n_ins, "waits", S.n_wait)
    return nc


def attn_host_consts(q):
    BIG = 1.0e6
    i = np.arange(128)[None, :]
    s = np.arange(128)[:, None]
    dband = np.zeros((128, 2, 128), np.float32)
    dprev = (i - s + 128).astype(np.float32)
    dband[:, 0, :] = np.where(s > i, dprev, BIG)
    dband[:, 1, :] = np.where(s <= i, (i - s).astype(np.float32), BIG)
    dmeta = np.full((128, 128), BIG, np.float32)
    m = np.arange(16)[:, None]
    dmeta[:16] = (i - m + 16)
    dmeta[16] = 0.0
    dqm = np.full((128, 16), BIG, np.float32)
    t = np.arange(16)[None, :]
    dqm[:16] = np.where(m <= t, (t - m).astype(np.float32), BIG)
    dqm[16] = 0.0
    garg = np.zeros((128, 16, 8), np.float32)
    for h in range(16):
        for j in range(8):
            garg[:16, h, j] = -SLOPES[h] * (1024 * q + 128 * j)
    valid = np.full((128, 1), 1.0 if q > 0 else 0.0, np.float32)
    return dict(dband=dband, dmeta=dmeta, dqm=dqm, garg=garg.reshape(128, 128), valid=valid)


def sink_bias(sinks):
    b = np.zeros((128, 16), np.float32)
    b[16, :] = sinks
    return b


TWO_PI = float(2 * np.pi)
MAGIC = 12582912.0
NTAB = 1041


class T:
    def __init__(self, nc, es, name, shape, dt=F32):
        self.t = es.enter_context(sbt(nc, name, shape, dt))
        self.B = Buf(name)


def v_tt(S, nc, e, out, a, b, op, reads, writes):
    eng = nc.vector if e == "dve" else nc.gpsimd
    S.op(e, lambda: eng.tensor_tensor(out=out, in0=a, in1=b, op=op), reads=reads, writes=writes)


def v_ts(S, nc, e, out, a, s1, s2, op0, op1, reads, writes):
    eng = nc.vector if e == "dve" else nc.gpsimd
    if s2 is None:
        S.op(e, lambda: eng.tensor_scalar(out=out, in0=a, scalar1=s1, scalar2=None, op0=op0), reads=reads, writes=writes)
    else:
        S.op(e, lambda: eng.tensor_scalar(out=out, in0=a, scalar1=s1, scalar2=s2, op0=op0, op1=op1), reads=reads, writes=writes)


def range_reduce(S, nc, x, xB, k, kB):
    v_ts(S, nc, "dve", k, x, 1.0 / TWO_PI, MAGIC, ALU.mult, ALU.add, [xB], [kB])
    v_ts(S, nc, "dve", k, k, MAGIC, -TWO_PI, ALU.subtract, ALU.mult, [kB], [kB])
    v_tt(S, nc, "dve", x, x, k, ALU.add, [xB, kB], [xB])


def sincos(C, x, xB, k, kB, sin_out, sinB, cos_out, cosB):
    nc, S = C.nc, C.S
    range_reduce(S, nc, x, xB, k, kB)
    S.op("act", lambda: nc.scalar.activation(out=sin_out, in_=x, func=AF.Sin), reads=[xB], writes=[sinB])
    S.op("act", lambda: nc.scalar.activation(out=k, in_=x, func=AF.Abs), reads=[xB], writes=[kB])
    S.op("act", lambda: nc.scalar.activation(out=cos_out, in_=k, func=AF.Sin, scale=-1.0, bias=C.halfpi[:, 0:1]), reads=[kB, C.hpB], writes=[cosB])


def ssm_prep(C, es, lamS_in, lamB_in, bB_in, cS_in, tvals_in, mask8_in, sgn_in):
    nc, S = C.nc, C.S
    P = type("P", (), {})()
    def ld(name, shape, src):
        t = T(nc, es, "p_" + name, shape)
        S.dma("sp", t.t[:], src, writes=[t.B])
        return t
    C.halfpi = es.enter_context(sbt(nc, "halfpi", [128, 1], F32))
    C.hpB = Buf("halfpi")
    S.op("dve", lambda: nc.vector.memset(C.halfpi[:], float(np.pi / 2)), writes=[C.hpB])
    P.tvals = ld("tvals", [128, NTAB], tvals_in)
    P.mask8 = ld("mask8", [128, 8], mask8_in)
    P.sgn = ld("sgn", [128, 1], sgn_in)
    lamS = ld("lamS", [128, 3, 64], lamS_in)
    P.thS = T(nc, es, "thS", [128, 64])
    P.rhoS = T(nc, es, "rhoS", [128, 64])
    P.lrS = T(nc, es, "lrS", [128, 64])
    P.BA = T(nc, es, "BAfull", [128, 8, 128])
    P.BB = T(nc, es, "BBfull", [128, 8, 128])
    P.CA = T(nc, es, "CAall", [128, 64, 16])
    P.CB = T(nc, es, "CBall", [128, 64, 16])
    dS = T(nc, es, "dS", [128, 64])
    S.op("act", lambda: nc.scalar.activation(out=dS.t[:], in_=lamS.t[:, 2, :], func=AF.Exp), reads=[lamS.B], writes=[dS.B])
    v_tt(S, nc, "dve", P.lrS.t[:], lamS.t[:, 0, :], dS.t[:], ALU.mult, [lamS.B, dS.B], [P.lrS.B])
    v_tt(S, nc, "dve", P.thS.t[:], lamS.t[:, 1, :], dS.t[:], ALU.mult, [lamS.B, dS.B], [P.thS.B])
    S.op("act", lambda: nc.scalar.activation(out=P.rhoS.t[:], in_=P.lrS.t[:], func=AF.Exp), reads=[P.lrS.B], writes=[P.rhoS.B])
    P.th2p = T(nc, es, "th2p", [128, 64])
    v_ts(S, nc, "dve", P.th2p.t[:], P.thS.t[:], 1.0 / TWO_PI, None, ALU.mult, None, [P.thS.B], [P.th2p.B])
    with ExitStack() as es2:
        cS = T(nc, es2, "p_cS", [128, 2, 64, 16])
        S.dma("sp", cS.t[:], cS_in, writes=[cS.B])
        S.op("dve", lambda: nc.vector.tensor_copy(out=P.CA.t[0:64], in_=cS.t[0:64, 0]), reads=[cS.B], writes=[P.CA.B])
        v_ts(S, nc, "dve", P.CA.t[64:128], cS.t[64:128, 1], -1.0, None, ALU.mult, None, [cS.B], [P.CA.B])
        v_ts(S, nc, "dve", P.CB.t[0:64], cS.t[0:64, 1], -1.0, None, ALU.mult, None, [cS.B], [P.CB.B])
        v_ts(S, nc, "dve", P.CB.t[64:128], cS.t[64:128, 0], -1.0, None, ALU.mult, None, [cS.B], [P.CB.B])
        lamB = T(nc, es2, "p_lamB", [128, 3, 512])
        S.dma("sp", lamB.t[:], lamB_in, writes=[lamB.B])
        bB = T(nc, es2, "p_bB", [128, 2, 512])
        S.dma("sp", bB.t[:], bB_in, writes=[bB.B])
        tm = [T(nc, es2, f"p_tm{i}", [128, 512]) for i in range(8)]
        dB, lr, th, kk, sn, cs, mg, t7 = tm
        lre, lim = lamB.t[:, 0, :], lamB.t[:, 1, :]
        S.op("act", lambda: nc.scalar.activation(out=dB.t[:], in_=lamB.t[:, 2, :], func=AF.Exp), reads=[lamB.B], writes=[dB.B])
        v_tt(S, nc, "dve", lr.t[:], lre, dB.t[:], ALU.mult, [lamB.B, dB.B], [lr.B])
        v_tt(S, nc, "dve", th.t[:], lim, dB.t[:], ALU.mult, [lamB.B, dB.B], [th.B])
        S.op("act", lambda: nc.scalar.activation(out=mg.t[:], in_=lr.t[:], func=AF.Exp), reads=[lr.B], writes=[mg.B])
        sincos(C, th.t[:], th.B, kk.t[:], kk.B, sn.t[:], sn.B, cs.t[:], cs.B)
        v_tt(S, nc, "dve", cs.t[:], cs.t[:], mg.t[:], ALU.mult, [cs.B, mg.B], [cs.B])
        v_ts(S, nc, "dve", cs.t[:], cs.t[:], -1.0, None, ALU.add, None, [cs.B], [cs.B])
        v_tt(S, nc, "dve", sn.t[:], sn.t[:], mg.t[:], ALU.mult, [sn.B, mg.B], [sn.B])
        a, b = cs, sn
        v_tt(S, nc, "dve", mg.t[:], lre, lre, ALU.mult, [lamB.B], [mg.B])
        v_tt(S, nc, "dve", kk.t[:], lim, lim, ALU.mult, [lamB.B], [kk.B])
        v_tt(S, nc, "dve", mg.t[:], mg.t[:], kk.t[:], ALU.add, [mg.B, kk.B], [mg.B])
        S.op("dve", lambda: nc.vector.reciprocal(out=mg.t[:], in_=mg.t[:]), reads=[mg.B], writes=[mg.B])
        v_tt(S, nc, "dve", lr.t[:], a.t[:], lre, ALU.mult, [a.B, lamB.B], [lr.B])
        v_tt(S, nc, "dve", kk.t[:], b.t[:], lim, ALU.mult, [b.B, lamB.B], [kk.B])
        v_tt(S, nc, "dve", lr.t[:], lr.t[:], kk.t[:], ALU.add, [lr.B, kk.B], [lr.B])
        v_tt(S, nc, "dve", lr.t[:], lr.t[:], mg.t[:], ALU.mult, [lr.B, mg.B], [lr.B])
        v_tt(S, nc, "dve", th.t[:], b.t[:], lre, ALU.mult, [b.B, lamB.B], [th.B])
        v_tt(S, nc, "dve", kk.t[:], a.t[:], lim, ALU.mult, [a.B, lamB.B], [kk.B])
        v_tt(S, nc, "dve", th.t[:], th.t[:], kk.t[:], ALU.subtract, [th.B, kk.B], [th.B])
        v_tt(S, nc, "dve", th.t[:], th.t[:], mg.t[:], ALU.mult, [th.B, mg.B], [th.B])
        cr, ci = lr, th
        bre, bim = bB.t[:, 0, :], bB.t[:, 1, :]
        v_tt(S, nc, "dve", dB.t[:], cr.t[:], bre, ALU.mult, [cr.B, bB.B], [dB.B])
        v_tt(S, nc, "dve", kk.t[:], ci.t[:], bim, ALU.mult, [ci.B, bB.B], [kk.B])
        v_tt(S, nc, "dve", dB.t[:], dB.t[:], kk.t[:], ALU.subtract, [dB.B, kk.B], [dB.B])
        v_tt(S, nc, "dve", t7.t[:], cr.t[:], bim, ALU.mult, [cr.B, bB.B], [t7.B])
        v_tt(S, nc, "dve", kk.t[:], ci.t[:], bre, ALU.mult, [ci.B, bB.B], [kk.B])
        v_tt(S, nc, "dve", t7.t[:], t7.t[:], kk.t[:], ALU.add, [t7.B, kk.B], [t7.B])
        bbr = dB.t[:].rearrange("p (c n) -> p c n", c=8)
        bbi = t7.t[:].rearrange("p (c n) -> p c n", c=8)
        S.op("dve", lambda: nc.vector.tensor_copy(out=P.BA.t[:, :, 0:64], in_=bbr), reads=[dB.B], writes=[P.BA.B])
        S.op("dve", lambda: nc.vector.tensor_copy(out=P.BA.t[:, :, 64:128], in_=bbi), reads=[t7.B], writes=[P.BA.B])
        S.op("dve", lambda: nc.vector.tensor_copy(out=P.BB.t[:, :, 0:64], in_=bbi), reads=[t7.B], writes=[P.BB.B])
        v_ts(S, nc, "dve", P.BB.t[:, :, 64:128], bbr, -1.0, None, ALU.mult, None, [dB.B], [P.BB.B])
        C.S.barrier_all()
    return P


def ssm_xstart(C, es, P, nat_in, swp_in):
    nc, S = C.nc, C.S
    xs = T(nc, es, "xstart", [128, 64])
    with ExitStack() as es2:
        nat = T(nc, es2, "x_nat", [128, 4, 64]); S.dma("sp", nat.t[:], nat_in, writes=[nat.B])
        swp = T(nc, es2, "x_swp", [128, 4, 64]); S.dma("sp", swp.t[:], swp_in, writes=[swp.B])
        mg, an, kk, sn, cs, t1 = [T(nc, es2, f"x_t{i}", [128, 64]) for i in range(6)]
        S.op("dve", lambda: nc.vector.memset(xs.t[:], 0.0), writes=[xs.B])
        for p in range(4):
            S.op("act", lambda: nc.scalar.activation(out=mg.t[:], in_=P.lrS.t[:], func=AF.Exp, scale=float(1024 * p)), reads=[P.lrS.B], writes=[mg.B])
            v_ts(S, nc, "dve", an.t[:], P.thS.t[:], float(1024 * (p + 1)), None, ALU.mult, None, [P.thS.B], [an.B])
            sincos(C, an.t[:], an.B, kk.t[:], kk.B, sn.t[:], sn.B, cs.t[:], cs.B)
            v_tt(S, nc, "dve", cs.t[:], cs.t[:], mg.t[:], ALU.mult, [cs.B, mg.B], [cs.B])
            v_tt(S, nc, "dve", sn.t[:], sn.t[:], mg.t[:], ALU.mult, [sn.B, mg.B], [sn.B])
            v_ts(S, nc, "dve", sn.t[:], sn.t[:], P.sgn.t[:, 0:1], None, ALU.mult, None, [sn.B, P.sgn.B], [sn.B])
            v_tt(S, nc, "dve", t1.t[:], cs.t[:], nat.t[:, p, :], ALU.mult, [cs.B, nat.B], [t1.B])
            v_tt(S, nc, "dve", xs.t[:], xs.t[:], t1.t[:], ALU.add, [xs.B, t1.B], [xs.B])
            v_tt(S, nc, "dve", t1.t[:], sn.t[:], swp.t[:, p, :], ALU.mult, [sn.B, swp.B], [t1.B])
            v_tt(S, nc, "dve", xs.t[:], xs.t[:], t1.t[:], ALU.add, [xs.B, t1.B], [xs.B])
        C.S.barrier_all()
    return xs


def tslice(t0, tn, cont=False):
    if cont:
        return (t0 + 1, tn)
    return (1009, 16) if t0 == 0 else (t0 - 15, tn)


def ssm_loop(C, es, P, uT, uB, big, mode, xs=None, yT=None, yB=None, dcol=None, wend=None, cont=False, tab=None, tab_mode=None):
    nc, S = C.nc, C.S
    (cosT, cosB), (sinT, sinB), (xa, xaB), (ka, kaB), (Vb, VB), (Wb, WB) = big[0:6]
    tabs2 = [((cosT, cosB), (sinT, sinB)), ((xa, xaB), (ka, kaB))]

    def load_tab(g):
        (cT, cB), (sT, sB) = tabs2[g % 2]
        S.dma("sp", cT[:, 0:NTAB], tab[0][g, 0], reads=[tab[1][g]], writes=[cB])
        S.dma("sp", sT[:, 0:NTAB], tab[0][g, 1], reads=[tab[1][g]], writes=[sB])

    if tab_mode == "load":
        load_tab(0)
    t1 = [T(nc, es, f"s_t1{i}", [128, 512]) for i in range(2)]
    t2 = [T(nc, es, f"s_t2{i}", [128, 512]) for i in range(2)]
    bpA = [T(nc, es, f"s_bpA{i}", [128, 128], BF16) for i in range(2)]
    bpB = [T(nc, es, f"s_bpB{i}", [128, 128], BF16) for i in range(2)]
    if mode != "A":
        A1 = T(nc, es, "s_A1", [128, NT], BF16)
        A2 = T(nc, es, "s_A2", [128, NT], BF16)
        wbA = [T(nc, es, f"s_wbA{i}", [128, 240], BF16) for i in range(2)]
        wbB = [T(nc, es, f"s_wbB{i}", [128, 240], BF16) for i in range(2)]
        for w in wbA + wbB:
            S.op("dve", lambda: nc.vector.memset(w.t[:], 0.0), writes=[w.B])
    rr = 0
    zr = 0
    for g in range(64):
        fc, gl = g // 8, g % 8
        i2 = g % 2
        S.op("pool", lambda: nc.gpsimd.tensor_scalar(out=bpA[i2].t[:], in0=P.BA.t[:, fc, :], scalar1=P.mask8.t[:, gl:gl + 1], scalar2=None, op0=ALU.mult),
             reads=[P.BA.B, P.mask8.B], writes=[bpA[i2].B])
        S.op("pool", lambda: nc.gpsimd.tensor_scalar(out=bpB[i2].t[:], in0=P.BB.t[:, fc, :], scalar1=P.mask8.t[:, gl:gl + 1], scalar2=None, op0=ALU.mult),
             reads=[P.BB.B, P.mask8.B], writes=[bpB[i2].B])
        if tab_mode == "load":
            (cosT, cosB), (sinT, sinB) = tabs2[g % 2]
            if g + 1 < 64:
                load_tab(g + 1)
        else:
            xr, kr = xa[:, 0:NTAB], ka[:, 0:NTAB]
            S.op("dve", lambda: nc.vector.tensor_scalar(out=kr, in0=P.tvals.t[:, :], scalar1=P.th2p.t[:, g:g + 1], scalar2=MAGIC, op0=ALU.mult, op1=ALU.add),
                 reads=[P.tvals.B, P.th2p.B], writes=[kaB])
            S.op("dve", lambda: nc.vector.tensor_scalar(out=kr, in0=kr, scalar1=MAGIC, scalar2=-TWO_PI, op0=ALU.subtract, op1=ALU.mult), reads=[kaB], writes=[kaB])
            S.op("dve", lambda: nc.vector.scalar_tensor_tensor(out=xr, in0=P.tvals.t[:, :], scalar=P.thS.t[:, g:g + 1], in1=kr, op0=ALU.mult, op1=ALU.add),
                 reads=[P.tvals.B, P.thS.B, kaB], writes=[xaB])
            S.op("act", lambda: nc.scalar.activation(out=sinT[:, 0:NTAB], in_=xr, func=AF.Sin), reads=[xaB], writes=[sinB])
            S.op("act", lambda: nc.scalar.activation(out=kr, in_=xr, func=AF.Abs), reads=[xaB], writes=[kaB])
            S.op("act", lambda: nc.scalar.activation(out=cosT[:, 0:NTAB], in_=kr, func=AF.Sin, scale=-1.0, bias=C.halfpi[:, 0:1]), reads=[kaB, C.hpB], writes=[cosB])
            if tab_mode == "store":
                S.dma("sp", tab[0][g, 0], cosT[:, 0:NTAB], reads=[cosB], writes=[tab[1][g]])
                S.dma("sp", tab[0][g, 1], sinT[:, 0:NTAB], reads=[sinB], writes=[tab[1][g]])
        for (t0, tn) in TILES:
            s0, sn_ = tslice(t0, tn, cont)
            pz, pzB = C.ps[zr % 4], C.psB[zr % 4]; zr += 1
            pzs, pzsB = C.ps[zr % 4], C.psB[zr % 4]; zr += 1
            S.op("pe", lambda: nc.tensor.matmul(pz[:, 0:tn], bpA[i2].t[:], uT[:, fc, t0:t0 + tn], start=True, stop=True), reads=[bpA[i2].B, uB], writes=[pzB])
            S.op("pe", lambda: nc.tensor.matmul(pzs[:, 0:tn], bpB[i2].t[:], uT[:, fc, t0:t0 + tn], start=True, stop=True), reads=[bpB[i2].B, uB], writes=[pzsB])
            j = rr % 2; rr += 1
            v_tt(S, nc, "dve", t1[j].t[:, 0:tn], pz[:, 0:tn], cosT[:, s0:s0 + tn], ALU.mult, [pzB, cosB], [t1[j].B])
            v_tt(S, nc, "dve", t2[j].t[:, 0:tn], pzs[:, 0:tn], sinT[:, s0:s0 + tn], ALU.mult, [pzsB, sinB], [t2[j].B])
            v_tt(S, nc, "dve", Vb[:, t0:t0 + tn], t1[j].t[:, 0:tn], t2[j].t[:, 0:tn], ALU.add, [t1[j].B, t2[j].B], [VB])
        rho = P.rhoS.t[:, g:g + 1]
        if cont:
            S.op("dve", lambda: nc.vector.tensor_tensor_scan(out=Wb[:, 0:NT], data0=rho.to_broadcast([128, NT]), data1=Vb[:, 0:NT], initial=0.0, op0=ALU.mult, op1=ALU.add),
                 reads=[VB, P.rhoS.B], writes=[WB])
        else:
            S.op("dve", lambda: nc.vector.tensor_tensor_scan(out=Wb[:, 0:16], data0=rho.to_broadcast([128, 16]), data1=Vb[:, 0:16], initial=0.0, op0=ALU.mult, op1=ALU.add),
                 reads=[VB, P.rhoS.B], writes=[WB])
            init = 0.0 if mode == "A" else xs.t[:, g:g + 1]
            S.op("dve", lambda: nc.vector.tensor_tensor_scan(out=Wb[:, 16:NT], data0=rho.to_broadcast([128, NT - 16]), data1=Vb[:, 16:NT], initial=init, op0=ALU.mult, op1=ALU.add),
                 reads=[VB, P.rhoS.B] + ([xs.B] if mode != "A" else []), writes=[WB])
        if mode == "A":
            S.op("act", lambda: nc.scalar.copy(out=wend.t[:, 0, g:g + 1], in_=Wb[:, NT - 1:NT]), reads=[WB], writes=[wend.B])
            S.op("act", lambda: nc.scalar.copy(out=wend.t[:, 1, g:g + 1], in_=Wb[:, 15:16]), reads=[WB], writes=[wend.B])
            continue
        if mode == "F":
            S.op("act", lambda: nc.scalar.copy(out=wend.t[:, g:g + 1], in_=Wb[:, NT - 1:NT]), reads=[WB], writes=[wend.B])
        if cont:
            v_tt(S, nc, "dve", A1.t[:, 0:NT], Wb[:, 0:NT], cosT[:, 1:NT + 1], ALU.mult, [WB, cosB], [A1.B])
            v_tt(S, nc, "pool", A2.t[:, 0:NT], Wb[:, 0:NT], sinT[:, 1:NT + 1], ALU.mult, [WB, sinB], [A2.B])
        else:
            v_tt(S, nc, "dve", A1.t[:, 0:16], Wb[:, 0:16], cosT[:, 1009:1025], ALU.mult, [WB, cosB], [A1.B])
            v_tt(S, nc, "dve", A1.t[:, 16:NT], Wb[:, 16:NT], cosT[:, 1:1025], ALU.mult, [WB, cosB], [A1.B])
            v_tt(S, nc, "pool", A2.t[:, 0:16], Wb[:, 0:16], sinT[:, 1009:1025], ALU.mult, [WB, sinB], [A2.B])
            v_tt(S, nc, "pool", A2.t[:, 16:NT], Wb[:, 16:NT], sinT[:, 1:1025], ALU.mult, [WB, sinB], [A2.B])
        S.op("act", lambda: nc.scalar.copy(out=wbA[i2].t[:, 112:128], in_=P.CA.t[:, g, :]), reads=[P.CA.B], writes=[wbA[i2].B])
        S.op("act", lambda: nc.scalar.copy(out=wbB[i2].t[:, 112:128], in_=P.CB.t[:, g, :]), reads=[P.CB.B], writes=[wbB[i2].B])
        o = 112 - 16 * gl
        for ti, (t0, tn) in enumerate(TILES):
            py, pyB = C.ps[5 + ti], C.psB[5 + ti]
            S.op("pe", lambda: nc.tensor.matmul(py[:, 0:tn], wbA[i2].t[:, o:o + 128], A1.t[:, t0:t0 + tn], start=(gl == 0), stop=False), reads=[wbA[i2].B, A1.B], writes=[pyB])
            S.op("pe", lambda: nc.tensor.matmul(py[:, 0:tn], wbB[i2].t[:, o:o + 128], A2.t[:, t0:t0 + tn], start=False, stop=(gl == 7)), reads=[wbB[i2].B, A2.B], writes=[pyB])
            if gl == 7:
                S.op("dve", lambda: nc.vector.scalar_tensor_tensor(out=yT[:, fc, t0:t0 + tn], in0=uT[:, fc, t0:t0 + tn], scalar=dcol.t[:, fc:fc + 1], in1=py[:, 0:tn],
                                                                   op0=ALU.mult, op1=ALU.add),
                     reads=[uB, dcol.B, pyB], writes=[yB])


def big_rows(regX):
    flat = regX[:].rearrange("p c t -> p (c t)")
    return [(flat[:, i * 1056:(i + 1) * 1056], Buf(f"big{i}")) for i in range(7)]


def u_proj(C, w_in, hT, hB, uT, uB):
    nc, S = C.nc, C.S
    for pc in range(4):
        view, vB = C.load_piece(wpiece(w_in, 0, 16, 1536 + pc * 256, 256), 16, 256)
        for mm in range(2):
            m = pc * 2 + mm
            for (t0, tn) in TILES:
                ps, psB = C.bank(0, 4)
                for k in range(16):
                    S.op("pe", lambda: nc.tensor.matmul(ps[:, 0:tn], view[:, k, mm * 128:(mm + 1) * 128], hT[:, k, t0:t0 + tn], start=(k == 0), stop=(k == 15)),
                         reads=[vB, hB], writes=[psB], inc=(k == 15))
                S.op("act", lambda: nc.scalar.copy(out=uT[:, m, t0:t0 + tn], in_=ps[:, 0:tn]), reads=[psB], writes=[uB])


def gelu_glu(C, es, w_glu, bglu, yT, yB, gT, gB, tmp, tmpB):
    nc, S = C.nc, C.S
    sg = [T(nc, es, f"g_sg{i}", [128, 512]) for i in range(2)]
    for m in range(8):
        y = yT[:, m, :]
        v_tt(S, nc, "dve", tmp, y, y, ALU.mult, [yB], [tmpB])
        v_ts(S, nc, "dve", tmp, tmp, 0.044715, 1.0, ALU.mult, ALU.add, [tmpB], [tmpB])
        v_tt(S, nc, "dve", tmp, tmp, y, ALU.mult, [tmpB, yB], [tmpB])
        S.op("act", lambda: nc.scalar.activation(out=tmp, in_=tmp, func=AF.Sigmoid, scale=1.5957691216057308), reads=[tmpB], writes=[tmpB])
        v_tt(S, nc, "dve", y, y, tmp, ALU.mult, [yB, tmpB], [yB])
        S.op("act", lambda: nc.scalar.copy(out=gT[:, m, :], in_=y), reads=[yB], writes=[gB])
    r = 0
    for pc in range(4):
        view, vB = C.load_piece(wpiece(w_glu, 0, 8, pc * 256, 256), 8, 256)
        for mm in range(2):
            m = pc * 2 + mm
            for (t0, tn) in TILES:
                ps, psB = C.bank(0, 4)
                for k in range(8):
                    S.op("pe", lambda: nc.tensor.matmul(ps[:, 0:tn], view[:, k, mm * 128:(mm + 1) * 128], gT[:, k, t0:t0 + tn], start=(k == 0), stop=(k == 7)),
                         reads=[vB, gB], writes=[psB], inc=(k == 7))
                j = r % 2; r += 1
                S.op("act", lambda: nc.scalar.activation(out=sg[j].t[:, 0:tn], in_=ps[:, 0:tn], func=AF.Sigmoid, bias=bglu.t[:, m:m + 1]),
                     reads=[psB, bglu.B], writes=[sg[j].B])
                v_tt(S, nc, "dve", yT[:, m, t0:t0 + tn], yT[:, m, t0:t0 + tn], sg[j].t[:, 0:tn], ALU.mult, [yB, sg[j].B], [yB])

def _din(nc, name, shape, dt=F32):
    return nc.dram_tensor(name, shape, dt, kind="ExternalInput").ap()


def _dout(nc, name, shape, dt=F32):
    return nc.dram_tensor(name, shape, dt, kind="ExternalOutput").ap()


def _ssm_inputs(nc):
    return dict(lamS=_din(nc, "lamS", [128, 3, 64]), lamB=_din(nc, "lamB", [128, 3, 512]), bB=_din(nc, "bB", [128, 2, 512]),
                cS=_din(nc, "cS", [128, 2, 64, 16]), tvals=_din(nc, "tvals", [128, NTAB]), mask8=_din(nc, "mask8", [128, 8]),
                sgn=_din(nc, "sgn", [128, 1]))


def build_A():
    nc = bass.Bass("TRN2", target_bir_lowering=False)
    xin = _din(nc, "xT_in", [128, 16, NT])
    w_in = _din(nc, "w_in", [D, 2560])
    gmix = _din(nc, "g_mix", [128, 16])
    gk_in = _din(nc, "gk2", [128, 1])
    si = _ssm_inputs(nc)
    kth_out = _dout(nc, "kth_out", [128, 4, 128], BF16)
    vh_out = _dout(nc, "vh_out", [128, 512], BF16)
    wend_out = _dout(nc, "wend_out", [128, 2, 64])
    with ExitStack() as es:
        C = Ctx(nc, es)
        S = C.S
        C.consts()
        regX = es.enter_context(sbt(nc, "regX", [128, 16, NT], F32)); xB = Buf("xT")
        regH = es.enter_context(sbt(nc, "regH", [128, 16, NT], BF16)); hB = Buf("regH")
        uT = es.enter_context(sbt(nc, "uT", [128, 8, NT], BF16)); uB = Buf("uT")
        gm = T(nc, es, "gm", [128, 16]); S.dma("sp", gm.t[:], gmix, writes=[gm.B])
        A = type("A", (), {})()
        A.gk = es.enter_context(sbt(nc, "sb_gk2", [128, 1], F32)); A.gkB = Buf("gk2")
        S.dma("sp", A.gk[:], gk_in, writes=[A.gkB])
        A.blk = es.enter_context(sbt(nc, "blkones", [128, 128], BF16)); A.blkB = Buf("blkones")
        S.op("dve", lambda: nc.vector.memset(A.blk[:], 0.0), writes=[A.blkB])
        S.op("dve", lambda: nc.vector.memset(A.blk[0:64, 0:64], 1.0), writes=[A.blkB])
        S.op("dve", lambda: nc.vector.memset(A.blk[64:128, 64:128], 1.0), writes=[A.blkB])
        S.dma("sp", regX[:], xin, writes=[xB])
        C.rmsnorm(regX, xB, 16, gm.t, gm.B, regH, hB)
        u_proj(C, w_in, regH, hB, uT, uB)
        with ExitStack() as es1:
            kT = [es1.enter_context(sbt(nc, f"kT{i}", [128, 4, KTW], BF16)) for i in range(2)]; kTB = Buf("kT")
            V2 = es1.enter_context(sbt(nc, "V2", [128, 10, 512], BF16)); V2B = Buf("V2")
            tmpq = es1.enter_context(sbt(nc, "tmpq", [128, 512], F32)); tmpqB = Buf("tmpq")
            kv_proj(C, A, w_in, regH, hB, kT, kTB, V2, V2B, None, None, tmpq, tmpqB, ktiles=[(912, 128)], vblocks=[(912, 128, 9)])
            S.dma("sp", kth_out[0:64], kT[0][0:64, :, 1056:1184], reads=[kTB], writes=[Buf("o1")])
            S.dma("sp", kth_out[64:128], kT[1][64:128, :, 1056:1184], reads=[kTB], writes=[Buf("o2")])
            S.dma("sp", vh_out, V2[:, 9, :], reads=[V2B], writes=[Buf("o3")])
            S.barrier_all()
        with ExitStack() as es2:
            P = ssm_prep(C, es2, si["lamS"], si["lamB"], si["bB"], si["cS"], si["tvals"], si["mask8"], si["sgn"])
            wend = T(nc, es2, "wend", [128, 2, 64])
            big = big_rows(regX)
            ssm_loop(C, es2, P, uT, uB, big, "A", wend=wend)
            S.dma("sp", wend_out, wend.t[:], reads=[wend.B], writes=[Buf("o4")])
            S.barrier_all()
        print("A instructions", S.n_ins, "waits", S.n_wait)
    return nc


def build_B(debug=False):
    nc = bass.Bass("TRN2", target_bir_lowering=False)
    xin = _din(nc, "xT_in", [128, 16, NT])
    xout = _dout(nc, "xT_out", [128, 16, NT])
    w_in = _din(nc, "w_in", [D, 2560])
    w_glu = _din(nc, "w_glu", [1024, 1024])
    w_out = _din(nc, "w_out", [D, D])
    w_up = _din(nc, "w_up", [D, DFF])
    w_down = _din(nc, "w_down", [DFF, D])
    gmix, gmlp = _din(nc, "g_mix", [128, 16]), _din(nc, "g_mlp", [128, 16])
    gao, gso = _din(nc, "g_ao", [128, 8]), _din(nc, "g_so", [128, 8])
    dcol_in, bglu_in = _din(nc, "dcol", [128, 8]), _din(nc, "bglu", [128, 8])
    dband, dmeta, dqm = _din(nc, "dband", [128, 2, 128]), _din(nc, "dmeta", [128, 128]), _din(nc, "dqm", [128, 16])
    bias_in, garg_in, valid_in = _din(nc, "abias", [128, 16]), _din(nc, "garg", [128, 128]), _din(nc, "valid", [128, 1])
    gq_in, gk_in = _din(nc, "gq2", [128, 1]), _din(nc, "gk2", [128, 1])
    kth_in, vh_in = _din(nc, "kth", [128, 4, 128], BF16), _din(nc, "vh", [128, 512], BF16)
    nat_in, swp_in = _din(nc, "nat", [128, 4, 64]), _din(nc, "swp", [128, 4, 64])
    si = _ssm_inputs(nc)
    if debug:
        dbg_out = _dout(nc, "dbg_out", [128, 16, NT])
    with ExitStack() as es:
        C = Ctx(nc, es)
        S = C.S
        C.consts()
        C.tmp_rr = 0
        regX = es.enter_context(sbt(nc, "regX", [128, 16, NT], F32)); xB = Buf("xT")
        regH = es.enter_context(sbt(nc, "regH", [128, 16, NT], BF16)); hB = Buf("regH")
        aT, aTB = regX[:, 0:8, :], Buf("aT")
        yT, yB = regX[:, 8:16, :], Buf("yT")
        def small(name, shape, src):
            t = T(nc, es, name, shape); S.dma("sp", t.t[:], src, writes=[t.B]); return t
        gm, gl2 = small("gm", [128, 16], gmix), small("gl2", [128, 16], gmlp)
        ga, gs = small("ga", [128, 8], gao), small("gs", [128, 8], gso)
        dcol, bglu = small("dcolS", [128, 8], dcol_in), small("bgluS", [128, 8], bglu_in)
        S.dma("sp", regX[:], xin, writes=[xB])
        C.rmsnorm(regX, xB, 16, gm.t, gm.B, regH, hB)
        S.barrier_all()
        with ExitStack() as esM:
            regR = esM.enter_context(sbt(nc, "regR", [128, 8, NT], BF16)); rB = Buf("regR")
            with ExitStack() as es2:
                u_proj(C, w_in, regH, hB, regR, rB)
                P = ssm_prep(C, es2, si["lamS"], si["lamB"], si["bB"], si["cS"], si["tvals"], si["mask8"], si["sgn"])
                xs = ssm_xstart(C, es2, P, nat_in, swp_in)
                big = big_rows(regX)
                with ExitStack() as es3:
                    ssm_loop(C, es3, P, regR, rB, big, "B", xs=xs, yT=yT, yB=yB, dcol=dcol)
                    S.barrier_all()
                with ExitStack() as es3:
                    gelu_glu(C, es3, w_glu, bglu, yT, yB, regR, rB, big[0][0][:, 0:NT], big[0][1])
                    S.barrier_all()
            with ExitStack() as es2:
                A = attn_consts(C, es2, dband, dmeta, dqm, bias_in, garg_in, valid_in, gq_in, gk_in)
                kT = [es2.enter_context(sbt(nc, f"kT{i}", [128, 4, KTW], BF16)) for i in range(2)]; kTB = Buf("kT")
                V2 = es2.enter_context(sbt(nc, "V2", [128, 10, 512], BF16)); V2B = Buf("V2")
                tmpq = es2.enter_context(sbt(nc, "tmpq", [128, 512], F32)); tmpqB = Buf("tmpq")
                kv_proj(C, A, w_in, regH, hB, kT, kTB, V2, V2B, kth_in, vh_in, tmpq, tmpqB)
                attention(C, A, es2, w_in, regH, hB, kT, kTB, V2, V2B, aT, aTB, tmpq, tmpqB)
                S.barrier_all()
        if debug:
            S.dma("sp", dbg_out, regX[:], reads=[aTB, yB], writes=[Buf("dbg")])
        C.rmsnorm(aT, aTB, 8, ga.t, ga.B, regH, hB, 0)
        C.rmsnorm(yT, yB, 8, gs.t, gs.B, regH, hB, 8)
        S.barrier_all()
        S.dma("sp", regX[:], xin, writes=[xB])
        dense_acc_into_x(C, w_out, 16, 0, regH, hB, regX, xB, 16, 256)
        S.barrier_all()
        with ExitStack() as es5:
            hid = [es5.enter_context(sbt(nc, f"hid{i}", [128, 8, NT], BF16)) for i in range(2)]
            hidB = [Buf(f"hid{i}") for i in range(2)]
            tmp = [es5.enter_context(sbt(nc, f"ftmp{i}", [128, 512], F32)) for i in range(2)]
            tmpB = [Buf(f"ftmp{i}") for i in range(2)]
            C.rmsnorm(regX, xB, 16, gl2.t, gl2.B, regH, hB)
            ffn(C, w_up, w_down, regH, hB, regX, xB, hid, hidB, tmp, tmpB)
            S.dma("sp", xout, regX[:], reads=[xB], writes=[Buf("xout")])
            S.barrier_all()
        print("B instructions", S.n_ins, "waits", S.n_wait)
    return nc


def ssm_xstart_f(C, es, P, wend_dram, fend):
    nc, S = C.nc, C.S
    xs = T(nc, es, "xstart", [128, 64])
    wd, wdB = wend_dram
    with ExitStack() as es2:
        nat = T(nc, es2, "x_nat", [128, 64]); S.dma("sp", nat.t[:], wd, reads=[wdB], writes=[nat.B])
        swp = T(nc, es2, "x_swp", [128, 64])
        S.dma("sp", swp.t[0:64], wd[64:128], reads=[wdB], writes=[swp.B])
        S.dma("sp", swp.t[64:128], wd[0:64], reads=[wdB], writes=[swp.B])
        an, kk, sn, cs = [T(nc, es2, f"x_t{i}", [128, 64]) for i in range(4)]
        v_ts(S, nc, "dve", an.t[:], P.thS.t[:], float(fend), None, ALU.mult, None, [P.thS.B], [an.B])
        sincos(C, an.t[:], an.B, kk.t[:], kk.B, sn.t[:], sn.B, cs.t[:], cs.B)
        v_ts(S, nc, "dve", sn.t[:], sn.t[:], P.sgn.t[:, 0:1], None, ALU.mult, None, [sn.B, P.sgn.B], [sn.B])
        v_tt(S, nc, "dve", xs.t[:], cs.t[:], nat.t[:], ALU.mult, [cs.B, nat.B], [xs.B])
        v_tt(S, nc, "dve", kk.t[:], sn.t[:], swp.t[:], ALU.mult, [sn.B, swp.B], [kk.B])
        v_tt(S, nc, "dve", xs.t[:], xs.t[:], kk.t[:], ALU.add, [xs.B, kk.B], [xs.B])
        C.S.barrier_all()
    return xs


def emit_pass(C, regX, regH, l, q, W, xin, xout, halo_prev, halo_cur, wend_prev, wend_cur, cst, tab=None):
    nc, S = C.nc, C.S
    PFX[0] = f"_L{l}Q{q}"
    xB, hB = Buf("xT"), Buf("regH")
    aT, aTB = regX[:, 0:8, :], Buf("aT")
    yT, yB = regX[:, 8:16, :], Buf("yT")
    xin_ap, xinB = xin
    xout_ap, xoutB = xout
    with ExitStack() as es:
        def small(name, shape, src):
            t = T(nc, es, name, shape); S.dma("sp", t.t[:], src, writes=[t.B]); return t
        gm, gl2 = small("gm", [128, 16], W["g_mix"][l]), small("gl2", [128, 16], W["g_mlp"][l])
        ga, gs = small("ga", [128, 8], W["g_ao"][l]), small("gs", [128, 8], W["g_so"][l])
        dcol, bglu = small("dcolS", [128, 8], W["dcol"][l]), small("bgluS", [128, 8], W["bglu"][l])
        S.dma("sp", regX[:], xin_ap, reads=[xinB], writes=[xB])
        C.rmsnorm(regX, xB, 16, gm.t, gm.B, regH, hB)
        S.barrier_all()
        with ExitStack() as esM:
            regR = esM.enter_context(sbt(nc, "regR", [128, 8, NT], BF16)); rB = Buf("regR")
            with ExitStack() as es2:
                u_proj(C, W["w_in"][l], regH, hB, regR, rB)
                P = ssm_prep(C, es2, W["lamS"][l], W["lamB"][l], W["bB"][l], W["cS"][l], cst["tvals"], cst["mask8"], cst["sgn"])
                xs = None
                if q > 0:
                    xs = ssm_xstart_f(C, es2, P, wend_prev, 1040 if q == 1 else 1024)
                wend = T(nc, es2, "wend", [128, 64])
                big = big_rows(regX)
                with ExitStack() as es3:
                    ssm_loop(C, es3, P, regR, rB, big, "F", xs=xs, yT=yT, yB=yB, dcol=dcol, wend=wend, cont=(q == 0), tab=tab,
                             tab_mode=(None if tab is None else ("store" if q == 0 else "load")))
                    S.dma("sp", wend_cur[0], wend.t[:], reads=[wend.B], writes=[wend_cur[1]])
                    S.barrier_all()
                with ExitStack() as es3:
                    gelu_glu(C, es3, W["w_glu"][l], bglu, yT, yB, regR, rB, big[0][0][:, 0:NT], big[0][1])
                    S.barrier_all()
            with ExitStack() as es2:
                A = attn_consts(C, es2, cst["dband"], cst["dmeta"], cst["dqm"], W["abias"][l], cst["garg"][q], cst["valid"], W["gq2"][l], W["gk2"][l])
                kT = [es2.enter_context(sbt(nc, f"kT{i}", [128, 4, KTW], BF16)) for i in range(2)]; kTB = Buf("kT")
                V2 = es2.enter_context(sbt(nc, "V2", [128, 10, 512], BF16)); V2B = Buf("V2")
                tmpq = es2.enter_context(sbt(nc, "tmpq", [128, 512], F32)); tmpqB = Buf("tmpq")
                S.op("dve", lambda: nc.vector.memset(kT[0][:], 0.0), writes=[kTB])
                S.op("dve", lambda: nc.vector.memset(kT[1][:], 0.0), writes=[kTB])
                S.op("dve", lambda: nc.vector.memset(V2[:, 0, :], 0.0), writes=[V2B])
                if q > 0:
                    (hk, hkB), (hv, hvB) = halo_prev
                    S.dma("sp", kT[0][0:64, :, 32:160], hk[0:64], reads=[hkB], writes=[kTB])
                    S.dma("sp", kT[1][64:128, :, 32:160], hk[64:128], reads=[hkB], writes=[kTB])
                    S.dma("sp", V2[:, 1, :], hv, reads=[hvB], writes=[V2B])
                kv_proj(C, A, W["w_in"][l], regH, hB, kT, kTB, V2, V2B, None, None, tmpq, tmpqB, skip_init=True)
                (hk, hkB), (hv, hvB) = halo_cur
                S.dma("sp", hk[0:64], kT[0][0:64, :, 1056:1184], reads=[kTB], writes=[hkB])
                S.dma("sp", hk[64:128], kT[1][64:128, :, 1056:1184], reads=[kTB], writes=[hkB])
                S.dma("sp", hv, V2[:, 9, :], reads=[V2B], writes=[hvB])
                attention(C, A, es2, W["w_in"][l], regH, hB, kT, kTB, V2, V2B, aT, aTB, tmpq, tmpqB, skip_prev0=(q == 0), use_valid=False)
                S.barrier_all()
        C.rmsnorm(aT, aTB, 8, ga.t, ga.B, regH, hB, 0)
        C.rmsnorm(yT, yB, 8, gs.t, gs.B, regH, hB, 8)
        S.barrier_all()
        S.dma("sp", regX[:], xin_ap, reads=[xinB], writes=[xB])
        dense_acc_into_x(C, W["w_out"][l], 16, 0, regH, hB, regX, xB, 16, 256)
        S.barrier_all()
        with ExitStack() as es5:
            hid = [es5.enter_context(sbt(nc, f"hid{i}", [128, 8, NT], BF16)) for i in range(2)]
            hidB = [Buf(f"hid{i}") for i in range(2)]
            tmp = [es5.enter_context(sbt(nc, f"ftmp{i}", [128, 512], F32)) for i in range(2)]
            tmpB = [Buf(f"ftmp{i}") for i in range(2)]
            C.rmsnorm(regX, xB, 16, gl2.t, gl2.B, regH, hB)
            ffn(C, W["w_up"][l], W["w_down"][l], regH, hB, regX, xB, hid, hidB, tmp, tmpB)
            S.dma("sp", xout_ap, regX[:], reads=[xB], writes=[xoutB])
            S.barrier_all()


def build_F(nlayers=4, nq=4):
    nc = bass.Bass("TRN2", target_bir_lowering=False)
    xin = _din(nc, "xT_in", [4, 128, 16, NT])
    xout = _dout(nc, "xT_out", [4, 128, 16, NT])
    W = dict(w_in=_din(nc, "w_in", [4, D, 2560]), w_glu=_din(nc, "w_glu", [4, 1024, 1024]), w_out=_din(nc, "w_out", [4, D, D]),
             w_up=_din(nc, "w_up", [4, D, DFF]), w_down=_din(nc, "w_down", [4, DFF, D]),
             g_mix=_din(nc, "g_mix", [4, 128, 16]), g_mlp=_din(nc, "g_mlp", [4, 128, 16]), g_ao=_din(nc, "g_ao", [4, 128, 8]),
             g_so=_din(nc, "g_so", [4, 128, 8]), dcol=_din(nc, "dcol", [4, 128, 8]), bglu=_din(nc, "bglu", [4, 128, 8]),
             abias=_din(nc, "abias", [4, 128, 16]), gq2=_din(nc, "gq2", [4, 128, 1]), gk2=_din(nc, "gk2", [4, 128, 1]),
             lamS=_din(nc, "lamS", [4, 128, 3, 64]), lamB=_din(nc, "lamB", [4, 128, 3, 512]), bB=_din(nc, "bB", [4, 128, 2, 512]),
             cS=_din(nc, "cS", [4, 128, 2, 64, 16]))
    cst = dict(tvals=_din(nc, "tvals", [128, NTAB]), mask8=_din(nc, "mask8", [128, 8]), sgn=_din(nc, "sgn", [128, 1]),
               dband=_din(nc, "dband", [128, 2, 128]), dmeta=_din(nc, "dmeta", [128, 128]), dqm=_din(nc, "dqm", [128, 16]),
               garg=_din(nc, "garg", [4, 128, 128]), valid=_din(nc, "valid", [128, 1]))
    scr = [nc.dram_tensor(f"xscr{i}", [4, 128, 16, NT], F32).ap() for i in range(2)]
    scrB = [[Buf(f"xscr{i}_{q}") for q in range(4)] for i in range(2)]
    hk = [nc.dram_tensor(f"hk{i}", [128, 4, 128], BF16).ap() for i in range(2)]
    hv = [nc.dram_tensor(f"hv{i}", [128, 512], BF16).ap() for i in range(2)]
    hB_ = [(Buf(f"hk{i}"), Buf(f"hv{i}")) for i in range(2)]
    wd = [nc.dram_tensor(f"wd{i}", [128, 64], F32).ap() for i in range(2)]
    wdB = [Buf(f"wd{i}") for i in range(2)]
    tab = (nc.dram_tensor("tabscr", [64, 2, 128, NTAB], F32).ap(), [Buf(f"tab{g}") for g in range(64)])
    with ExitStack() as es:
        C = Ctx(nc, es)
        S = C.S
        C.consts()
        C.tmp_rr = 0
        regX = es.enter_context(sbt(nc, "regX", [128, 16, NT], F32))
        regH = es.enter_context(sbt(nc, "regH", [128, 16, NT], BF16))
        xinB = Buf("xin")
        outB = [Buf(f"xout{q}") for q in range(4)]
        for l in range(nlayers):
            for q in range(nq):
                src = (xin[q], xinB) if l == 0 else (scr[(l - 1) % 2][q], scrB[(l - 1) % 2][q])
                dst = (xout[q], outB[q]) if l == nlayers - 1 else (scr[l % 2][q], scrB[l % 2][q])
                i, j = q % 2, (q + 1) % 2
                emit_pass(C, regX, regH, l, q, W, src, dst,
                          ((hk[j], hB_[j][0]), (hv[j], hB_[j][1])), ((hk[i], hB_[i][0]), (hv[i], hB_[i][1])),
                          (wd[j], wdB[j]), (wd[i], wdB[i]), cst, tab=(tab if nq > 1 else None))
        PFX[0] = ""
        print("F instructions", S.n_ins, "waits", S.n_wait)
    return nc

def ssm_host_layout(lre, lim, lst, bre, bim, cre, cim):
    lamS = np.zeros((128, 3, 64), np.float32)
    for ri in range(2):
        lamS[ri * 64:(ri + 1) * 64, 0, :] = lre.T
        lamS[ri * 64:(ri + 1) * 64, 1, :] = lim.T
        lamS[ri * 64:(ri + 1) * 64, 2, :] = lst[None, :]
    def layB(a_gn):
        a = a_gn.reshape(8, 8, 64)
        a = a.transpose(1, 0, 2)
        return np.repeat(a[:, None], 16, axis=1).reshape(128, 8 * 64)
    lamB = np.stack([layB(lre), layB(lim), layB(np.repeat(lst[:, None], 64, 1))], 1).astype(np.float32)
    def layBb(b):
        a = b.reshape(8, 8, 64, 16).transpose(1, 3, 0, 2)
        return a.reshape(128, 512)
    bB = np.stack([layBb(bre), layBb(bim)], 1).astype(np.float32)
    def layC(c):
        a = c.transpose(2, 0, 1)
        return np.concatenate([a, a], 0)
    cS = np.stack([layC(cre), layC(cim)], 1).astype(np.float32)
    return dict(lamS=lamS, lamB=np.ascontiguousarray(lamB), bB=np.ascontiguousarray(bB), cS=np.ascontiguousarray(cS))


def ssm_host_consts():
    tvals = np.broadcast_to(np.arange(NTAB, dtype=np.float32)[None], (128, NTAB)).copy()
    mask8 = np.zeros((128, 8), np.float32)
    for p in range(128):
        mask8[p, p // 16] = 1.0
    sgn = np.ones((128, 1), np.float32)
    sgn[:64] = -1.0
    return dict(tvals=tvals, mask8=mask8, sgn=sgn)


_PROGS = {}
NLAYERS = 4
DEBUG_LAST = {}
TRACE = False
STRICT = [True]


def _prog(name):
    if name not in _PROGS:
        _PROGS[name] = build_A() if name == "A" else build_B()
    return _PROGS[name]


def kernel_unfused(x, meta_tokens, norm_mix_g, w_in, q_norm_g, k_norm_g, attn_sinks, ssm_lambda_re, ssm_lambda_im,
           ssm_log_step, ssm_b_re, ssm_b_im, ssm_c_re, ssm_c_im, ssm_d, w_glu, b_glu, attn_out_g, ssm_out_g,
           w_out, norm_mlp_g, w_up, w_down):
    f = lambda a: np.asarray(a, dtype=np.float32)
    x, meta_tokens = f(x), f(meta_tokens)
    ncores = 8
    xs = []
    for c in range(ncores):
        b, q = c // 4, c % 4
        tok = np.concatenate([meta_tokens, x[b, 1024 * q:1024 * (q + 1)]], 0)
        xs.append(to_fm(tok))
    hconst = [attn_host_consts(c % 4) for c in range(ncores)]
    sconst = ssm_host_consts()
    zero_kth = np.zeros((128, 4, 128), np.float32).astype(BF16NP)
    zero_vh = np.zeros((128, 512), np.float32).astype(BF16NP)
    for l in range(NLAYERS):
        sl = ssm_host_layout(f(ssm_lambda_re[l]), f(ssm_lambda_im[l]), f(ssm_log_step[l]), f(ssm_b_re[l]), f(ssm_b_im[l]),
                             f(ssm_c_re[l]), f(ssm_c_im[l]))
        common = dict(w_in=f(w_in[l]), g_mix=gcols(f(norm_mix_g[l])), gk2=np.tile(f(k_norm_g[l]), 2).reshape(128, 1), **sl, **sconst)
        insA = [dict(xT_in=xs[c], **common) for c in range(ncores)]
        resA = run_bass_kernel_spmd(_prog("A"), insA, core_ids=list(range(ncores))).results
        commonB = dict(common, w_glu=f(w_glu[l]), w_out=f(w_out[l]), w_up=f(w_up[l]), w_down=f(w_down[l]),
                       g_mlp=gcols(f(norm_mlp_g[l])), g_ao=gcols(f(attn_out_g[l])), g_so=gcols(f(ssm_out_g[l])),
                       dcol=gcols(f(ssm_d[l])), bglu=gcols(f(b_glu[l])), abias=sink_bias(f(attn_sinks[l])),
                       gq2=np.tile(f(q_norm_g[l]), 2).reshape(128, 1))
        insB = []
        for c in range(ncores):
            b, q = c // 4, c % 4
            nat = np.zeros((128, 4, 64), np.float32)
            for p in range(q):
                nat[:, p, :] = resA[b * 4 + q - 1 - p]["wend_out"][:, 0, :]
            nat[:, q, :] = resA[c]["wend_out"][:, 1, :]
            swp = np.concatenate([nat[64:], nat[:64]], 0)
            kth = resA[c - 1]["kth_out"] if q > 0 else zero_kth
            vh = resA[c - 1]["vh_out"] if q > 0 else zero_vh
            insB.append(dict(xT_in=xs[c], kth=kth, vh=vh, nat=nat, swp=np.ascontiguousarray(swp), **commonB, **hconst[c]))
        rB_ = run_bass_kernel_spmd(_prog("B"), insB, core_ids=list(range(ncores)), trace=TRACE) if TRACE else run_bass_kernel_spmd(_prog("B"), insB, core_ids=list(range(ncores)))
        resB = rB_.results
        DEBUG_LAST["B_ns"] = rB_.exec_time_ns
        xs = [np.asarray(resB[c]["xT_out"], dtype=np.float32) for c in range(ncores)]
        DEBUG_LAST["resA"], DEBUG_LAST["resB"], DEBUG_LAST["xs"] = resA, resB, xs
    out = np.zeros((2, 4096, D), np.float32)
    for c in range(ncores):
        b, q = c // 4, c % 4
        out[b, 1024 * q:1024 * (q + 1)] = from_fm(xs[c])[NMETA:]
    return out


def _fused_inputs(inp):
    f = lambda a: np.asarray(a, dtype=np.float32)
    x, meta = f(inp["x"]), f(inp["meta_tokens"])
    L4 = range(4)
    sl = [ssm_host_layout(f(inp["ssm_lambda_re"][l]), f(inp["ssm_lambda_im"][l]), f(inp["ssm_log_step"][l]), f(inp["ssm_b_re"][l]),
                          f(inp["ssm_b_im"][l]), f(inp["ssm_c_re"][l]), f(inp["ssm_c_im"][l])) for l in L4]
    st = lambda fn: np.ascontiguousarray(np.stack([fn(l) for l in L4], 0))
    common = dict(
        w_in=f(inp["w_in"]), w_glu=f(inp["w_glu"]), w_out=f(inp["w_out"]), w_up=f(inp["w_up"]), w_down=f(inp["w_down"]),
        g_mix=st(lambda l: gcols(f(inp["norm_mix_g"][l]))), g_mlp=st(lambda l: gcols(f(inp["norm_mlp_g"][l]))),
        g_ao=st(lambda l: gcols(f(inp["attn_out_g"][l]))), g_so=st(lambda l: gcols(f(inp["ssm_out_g"][l]))),
        dcol=st(lambda l: gcols(f(inp["ssm_d"][l]))), bglu=st(lambda l: gcols(f(inp["b_glu"][l]))),
        abias=st(lambda l: sink_bias(f(inp["attn_sinks"][l]))),
        gq2=st(lambda l: np.tile(f(inp["q_norm_g"][l]), 2).reshape(128, 1)), gk2=st(lambda l: np.tile(f(inp["k_norm_g"][l]), 2).reshape(128, 1)),
        lamS=st(lambda l: sl[l]["lamS"]), lamB=st(lambda l: sl[l]["lamB"]), bB=st(lambda l: sl[l]["bB"]), cS=st(lambda l: sl[l]["cS"]),
        **ssm_host_consts())
    hc = [attn_host_consts(q) for q in range(4)]
    common.update(dband=hc[0]["dband"], dmeta=hc[0]["dmeta"], dqm=hc[0]["dqm"], valid=hc[1]["valid"],
                  garg=np.ascontiguousarray(np.stack([hc[q]["garg"] for q in range(4)], 0)))
    ins = []
    for c in range(8):
        b = c // 4
        xq = np.stack([to_fm(np.concatenate([meta, x[b, 1024 * q:1024 * (q + 1)]], 0)) for q in range(4)], 0)
        ins.append(dict(xT_in=np.ascontiguousarray(xq), **common))
    return ins


def kernel_fused(**inp):
    if "F" not in _PROGS:
        _PROGS["F"] = build_F(NLAYERS)
    res = run_bass_kernel_spmd(_PROGS["F"], _fused_inputs(inp), core_ids=list(range(8))).results
    out = np.zeros((2, 4096, D), np.float32)
    for b in range(2):
        xo = np.asarray(res[4 * b]["xT_out"], dtype=np.float32)
        for q in range(4):
            out[b, 1024 * q:1024 * (q + 1)] = from_fm(xo[q])[NMETA:]
    DEBUG_LAST["resF"] = res
    return out


def to_fm(tok):
    return np.ascontiguousarray(tok.T.reshape(16, 128, tok.shape[0]).transpose(1, 0, 2))


def from_fm(fm):
    return np.ascontiguousarray(fm.transpose(1, 0, 2).reshape(D, fm.shape[2]).T)


def gcols(g):
    return np.ascontiguousarray(g.reshape(-1, 128).T)


def kernel(**inputs):
    return kernel_fused(**inputs)
```

```python
import numpy as np
from contextlib import ExitStack
import concourse.bass as bass
import concourse.mybir as mybir
from concourse.bass_utils import run_bass_kernel_spmd
import ml_dtypes

BF16NP = ml_dtypes.bfloat16

F32 = mybir.dt.float32
BF16 = mybir.dt.bfloat16
AF = mybir.ActivationFunctionType
ALU = mybir.AluOpType

D = 2048
NT = 1040
NMETA = 16
DFF = 8192
EPS = 1e-6
TILES = [(0, 16), (16, 512), (528, 512)]
SLOT = 4096
NSLOT = 3


PFX = [""]


def sbt(nc, name, shape, dt):
    return nc.sbuf_tensor(name + PFX[0], shape, dt)


class Buf:
    __slots__ = ("name", "w", "r")

    def __init__(self, name):
        self.name = name
        self.w = None
        self.r = {}


class Sched:
    def __init__(self, nc, es, strict_same=True, n_dma_sems=8):
        self.nc = nc
        self.E = {"pe": nc.tensor, "act": nc.scalar, "dve": nc.vector, "pool": nc.gpsimd, "sp": nc.sync}
        self.sem, self.cnt, self.pending = {}, {}, {}
        for e in self.E:
            self.sem[e] = es.enter_context(nc.semaphore("s_" + e))
            self.cnt[e] = 0
            self.pending[e] = False
        self.known = {e: {} for e in self.E}
        self.strict_same = strict_same
        self.dsem, self.dcnt, self.drr = {}, {}, {}
        for e in ("sp", "pool"):
            self.dsem[e] = [es.enter_context(nc.semaphore(f"d_{e}{i}")) for i in range(n_dma_sems)]
            self.dcnt[e] = [0] * n_dma_sems
            self.drr[e] = 0
        self.n_wait = 0
        self.n_ins = 0

    def _wait(self, e, tok):
        sem, val, src = tok
        if src == e and (e == "pe" or not self.strict_same):
            return
        k = id(sem)
        if self.known[e].get(k, 0) >= val:
            return
        self.E[e].wait_ge(sem, val)
        self.known[e][k] = val
        self.n_wait += 1

    def _deps(self, e, reads, writes):
        for b in reads:
            if b.w is not None:
                self._wait(e, b.w)
        for b in writes:
            if b.w is not None:
                self._wait(e, b.w)
            for t in b.r.values():
                self._wait(e, t)

    def _mark(self, tok, reads, writes):
        for b in reads:
            b.r[id(tok[0])] = tok
        for b in writes:
            b.w = tok
            b.r = {}

    def op(self, e, emit, reads=(), writes=(), inc=True):
        self._deps(e, reads, writes)
        ins = emit()
        self.n_ins += 1
        if inc:
            self.cnt[e] += 1
            ins.then_inc(self.sem[e], 1)
            self.pending[e] = False
            tok = (self.sem[e], self.cnt[e], e)
        else:
            self.pending[e] = True
            tok = (self.sem[e], self.cnt[e] + 1, e)
        self._mark(tok, reads, writes)
        return ins

    def dma(self, q, out, in_, reads=(), writes=()):
        self._deps(q, reads, writes)
        i = self.drr[q]
        self.drr[q] = (i + 1) % len(self.dsem[q])
        sem = self.dsem[q][i]
        if self.dcnt[q][i] > 0:
            self._wait(q, (sem, 16 * self.dcnt[q][i], "dma"))
        ins = self.E[q].dma_start(out=out, in_=in_)
        self.n_ins += 1
        self.dcnt[q][i] += 1
        ins.then_inc(sem, 16)
        tok = (sem, 16 * self.dcnt[q][i], "dma")
        self._mark(tok, reads, writes)
        return ins

    def barrier_all(self):
        toks = []
        for e in self.E:
            if e == "sp":
                continue
            assert not self.pending[e], e
            if self.cnt[e] > 0:
                toks.append((self.sem[e], self.cnt[e], e))
        for q in self.dsem:
            for i, sem in enumerate(self.dsem[q]):
                if self.dcnt[q][i] > 0:
                    toks.append((sem, 16 * self.dcnt[q][i], "dma"))
        for e in self.E:
            for t in toks:
                if t[2] == e:
                    continue
                self._wait(e, t)


class Ctx:
    def __init__(self, nc, es):
        self.nc = nc
        self.es = es
        self.S = Sched(nc, es, strict_same=STRICT[0])
        S = self.S
        self.slots = [es.enter_context(sbt(nc, f"wslot{i}", [128, SLOT], BF16)) for i in range(NSLOT)]
        self.slotB = [Buf(f"wslot{i}") for i in range(NSLOT)]
        self.slot_rr = 0
        self.ps = [es.enter_context(nc.psum_tensor(f"ps{i}", [128, 512], F32)) for i in range(8)]
        self.psB = [Buf(f"ps{i}") for i in range(8)]
        self.ps_rr = 0
        self.ones_bf = es.enter_context(sbt(nc, "ones_bf", [128, 128], BF16))
        self.onesB = Buf("ones")
        S.op("dve", lambda: nc.vector.memset(self.ones_bf[:], 1.0), writes=[self.onesB])
        self.sq = [es.enter_context(sbt(nc, f"sq{i}", [128, 512], BF16)) for i in range(2)]
        self.sqB = [Buf(f"sq{i}") for i in range(2)]
        self.sq_rr = 0
        self.rstd = es.enter_context(sbt(nc, "rstd", [128, NT], F32))
        self.rstdB = Buf("rstd")
        self.lnt = es.enter_context(sbt(nc, "lnt", [128, 512], F32))
        self.lntB = Buf("lnt")

    def bank(self, lo=0, hi=6):
        n = hi - lo
        i = lo + (self.ps_rr % n)
        self.ps_rr += 1
        return self.ps[i], self.psB[i]

    def load_piece(self, src_ap, nk, ncols):
        i = self.slot_rr % NSLOT
        self.slot_rr += 1
        view = self.slots[i][:, 0:nk * ncols].rearrange("p (k f) -> p k f", k=nk)
        self.S.dma("pool", view, src_ap, writes=[self.slotB[i]])
        return view, self.slotB[i]

    def rmsnorm(self, src, srcB, ndc, gcol, gB, dst, dstB, dst_dc0=0):
        nc, S = self.nc, self.S
        inv = 1.0 / (ndc * 128)
        for (t0, tn) in TILES:
            ps, psB = self.bank(6, 8)
            for dc in range(ndc):
                j = self.sq_rr % 2
                self.sq_rr += 1
                sq, sqB = self.sq[j], self.sqB[j]
                S.op("act", lambda: nc.scalar.activation(out=sq[:, 0:tn], in_=src[:, dc, t0:t0 + tn], func=AF.Square),
                     reads=[srcB], writes=[sqB])
                S.op("pe", lambda: nc.tensor.matmul(ps[:, 0:tn], self.ones_bf[:], sq[:, 0:tn], start=(dc == 0), stop=(dc == ndc - 1)),
                     reads=[sqB, self.onesB], writes=[psB], inc=True)
            S.op("act", lambda: nc.scalar.activation(out=self.lnt[:, 0:tn], in_=ps[:, 0:tn], func=AF.Ln, scale=inv, bias=self.epsc[:, 0:1]),
                 reads=[psB, self.epsB], writes=[self.lntB])
            S.op("act", lambda: nc.scalar.activation(out=self.rstd[:, t0:t0 + tn], in_=self.lnt[:, 0:tn], func=AF.Exp, scale=-0.5),
                 reads=[self.lntB], writes=[self.rstdB])
        for dc in range(ndc):
            S.op("dve", lambda: nc.vector.scalar_tensor_tensor(out=dst[:, dst_dc0 + dc, :], in0=src[:, dc, :], scalar=gcol[:, dc:dc + 1],
                                                               in1=self.rstd[:, :], op0=ALU.mult, op1=ALU.mult),
                 reads=[srcB, gB, self.rstdB], writes=[dstB])

    def consts(self):
        nc, S = self.nc, self.S
        self.epsc = self.es.enter_context(sbt(nc, "epsc", [128, 1], F32))
        self.epsB = Buf("epsc")
        S.op("dve", lambda: nc.vector.memset(self.epsc[:], EPS), writes=[self.epsB])


def wpiece(w_ap, k0, nk, c0, ncols):
    return w_ap.rearrange("(kc p) f -> p kc f", p=128)[:, k0:k0 + nk, c0:c0 + ncols]


def dense_acc_into_x(C, w_ap, nkc, row0_chunk, act, actB, xT, xB, m_chunks, piece_cols):
    nc, S = C.nc, C.S
    per = piece_cols // 128
    for p0 in range(0, m_chunks, per):
        view, vB = C.load_piece(wpiece(w_ap, row0_chunk, nkc, p0 * 128, piece_cols), nkc, piece_cols)
        for mm in range(per):
            m = p0 + mm
            for (t0, tn) in TILES:
                ps, psB = C.bank()
                for k in range(nkc):
                    S.op("pe", lambda: nc.tensor.matmul(ps[:, 0:tn], view[:, k, mm * 128:(mm + 1) * 128], act[:, k, t0:t0 + tn],
                                                        start=(k == 0), stop=(k == nkc - 1)),
                         reads=[vB, actB], writes=[psB], inc=(k == nkc - 1))
                S.op("dve", lambda: nc.vector.tensor_tensor(out=xT[:, m, t0:t0 + tn], in0=ps[:, 0:tn], in1=xT[:, m, t0:t0 + tn], op=ALU.add),
                     reads=[psB, xB], writes=[xB])


def ffn(C, w_up, w_down, h2T, h2B, xT, xB, hid, hidB, tmp, tmpB):
    nc, S = C.nc, C.S
    NFG = DFF // 1024
    for fg in range(NFG):
        hb, hbB = hid[fg % 2], hidB[fg % 2]
        for pc in range(4):
            view, vB = C.load_piece(wpiece(w_up, 0, 16, fg * 1024 + pc * 256, 256), 16, 256)
            for mm in range(2):
                j = pc * 2 + mm
                for (t0, tn) in TILES:
                    ps, psB = C.bank()
                    for k in range(16):
                        S.op("pe", lambda: nc.tensor.matmul(ps[:, 0:tn], view[:, k, mm * 128:(mm + 1) * 128], h2T[:, k, t0:t0 + tn],
                                                            start=(k == 0), stop=(k == 15)),
                             reads=[vB, h2B], writes=[psB], inc=(k == 15))
                    tb = C.tmp_rr % 2
                    C.tmp_rr += 1
                    S.op("act", lambda: nc.scalar.activation(out=tmp[tb][:, 0:tn], in_=ps[:, 0:tn], func=AF.Relu),
                         reads=[psB], writes=[tmpB[tb]])
                    S.op("dve", lambda: nc.vector.tensor_tensor(out=hb[:, j, t0:t0 + tn], in0=tmp[tb][:, 0:tn], in1=tmp[tb][:, 0:tn], op=ALU.mult),
                         reads=[tmpB[tb]], writes=[hbB])
        dense_acc_into_x(C, w_down, 8, fg * 8, hb, hbB, xT, xB, 16, 512)


KTW = 1184
SLOPES = [2.0 ** (-(h + 1) / 2.0) for h in range(16)]


def dup_cols(ap2, reps):
    a = [list(x) for x in ap2.ap]
    return bass.AP(ap2.tensor, ap2.offset, [a[0], [0, reps]] + a[1:])


def attn_consts(C, es, dband_in, dmeta_in, dqm_in, bias_in, garg_in, valid_in, gq_in, gk_in):
    nc, S = C.nc, C.S
    A = type("A", (), {})()
    def ld(name, shape, src):
        t = es.enter_context(sbt(nc, "sb_" + name, shape, F32))
        b = Buf(name)
        S.dma("sp", t[:], src, writes=[b])
        return t, b
    A.dband, A.dbandB = ld("dband", [128, 2, 128], dband_in)
    A.dmeta, A.dmetaB = ld("dmeta", [128, 128], dmeta_in)
    A.dqm, A.dqmB = ld("dqm", [128, 16], dqm_in)
    A.bias, A.biasB = ld("abias", [128, 16], bias_in)
    A.garg, A.gargB = ld("garg", [128, 128], garg_in)
    A.valid, A.validB = ld("valid", [128, 1], valid_in)
    A.gq, A.gqB = ld("gq2", [128, 1], gq_in)
    A.gk, A.gkB = ld("gk2", [128, 1], gk_in)
    S.op("act", lambda: nc.scalar.activation(out=A.garg[:], in_=A.garg[:], func=AF.Exp), reads=[A.gargB], writes=[A.gargB])
    A.blk = es.enter_context(sbt(nc, "blkones", [128, 128], BF16))
    A.blkB = Buf("blkones")
    S.op("dve", lambda: nc.vector.memset(A.blk[:], 0.0), writes=[A.blkB])
    S.op("dve", lambda: nc.vector.memset(A.blk[0:64, 0:64], 1.0), writes=[A.blkB])
    S.op("dve", lambda: nc.vector.memset(A.blk[64:128, 64:128], 1.0), writes=[A.blkB])
    return A


def headnorm_evac(C, A, ps, psB, tn, gcol, gB, out_ap, outB, tmpq, tmpqB):
    nc, S = C.nc, C.S
    j = C.sq_rr % 2
    C.sq_rr += 1
    sq, sqB = C.sq[j], C.sqB[j]
    S.op("act", lambda: nc.scalar.activation(out=sq[:, 0:tn], in_=ps[:, 0:tn], func=AF.Square), reads=[psB], writes=[sqB])
    st, stB = C.ps[2], C.psB[2]
    S.op("pe", lambda: nc.tensor.matmul(st[:, 0:tn], A.blk[:], sq[:, 0:tn], start=True, stop=True), reads=[sqB, A.blkB], writes=[stB])
    S.op("act", lambda: nc.scalar.activation(out=C.lnt[:, 0:tn], in_=st[:, 0:tn], func=AF.Ln, scale=1.0 / 64, bias=C.epsc[:, 0:1]),
         reads=[stB, C.epsB], writes=[C.lntB])
    S.op("act", lambda: nc.scalar.activation(out=tmpq[:, 0:tn], in_=C.lnt[:, 0:tn], func=AF.Exp, scale=-0.5), reads=[C.lntB], writes=[tmpqB])
    if isinstance(out_ap, tuple):
        for (r0, oap) in ((0, out_ap[0]), (64, out_ap[1])):
            S.op("dve", lambda: nc.vector.scalar_tensor_tensor(out=oap, in0=ps[r0:r0 + 64, 0:tn], scalar=gcol[r0:r0 + 64, 0:1], in1=tmpq[r0:r0 + 64, 0:tn],
                                                               op0=ALU.mult, op1=ALU.mult),
                 reads=[psB, gB, tmpqB], writes=[outB])
    else:
        S.op("dve", lambda: nc.vector.scalar_tensor_tensor(out=out_ap, in0=ps[:, 0:tn], scalar=gcol[:, 0:1], in1=tmpq[:, 0:tn], op0=ALU.mult, op1=ALU.mult),
             reads=[psB, gB, tmpqB], writes=[outB])


def kv_proj(C, A, w_in, hT, hB, kT, kTB, V2, V2B, kth_in, vh_in, tmpq, tmpqB, ktiles=None, vblocks=None, skip_init=False):
    nc, S = C.nc, C.S
    if not skip_init:
        S.op("dve", lambda: nc.vector.memset(kT[0][:], 0.0), writes=[kTB])
        S.op("dve", lambda: nc.vector.memset(kT[1][:], 0.0), writes=[kTB])
        S.op("dve", lambda: nc.vector.memset(V2[:, 0, :], 0.0), writes=[V2B])
    if kth_in is not None:
        S.dma("sp", kT[0][0:64, :, 32:160], kth_in[0:64], writes=[kTB])
        S.dma("sp", kT[1][64:128, :, 32:160], kth_in[64:128], writes=[kTB])
        S.dma("sp", V2[:, 1, :], vh_in, writes=[V2B])
    view, vB = C.load_piece(wpiece(w_in, 0, 16, 1024, 256), 16, 256)
    for kv in range(4):
        for (t0, tn) in (ktiles or TILES):
            ps, psB = C.bank(0, 2)
            for half in range(2):
                for k in range(16):
                    lhsT = view[:, k, kv * 64:(kv + 1) * 64]
                    S.op("pe", lambda: nc.tensor.matmul(ps[half * 64:(half + 1) * 64, 0:tn], lhsT, hT[:, k, t0:t0 + tn], start=(k == 0), stop=(k == 15),
                                                        tile_position=(0, half * 64)),
                         reads=[vB, hB], writes=[psB], inc=(k == 15))
            c0 = t0 if t0 < 16 else t0 + 144
            headnorm_evac(C, A, ps, psB, tn, A.gk, A.gkB, (kT[0][0:64, kv, c0:c0 + tn], kT[1][64:128, kv, c0:c0 + tn]), kTB, tmpq, tmpqB)
    view, vB = C.load_piece(wpiece(w_in, 0, 16, 1280, 256), 16, 256)
    blocks = vblocks or ([(0, 16, 0)] + [(16 + 128 * j, 128, 2 + j) for j in range(8)])
    for (t0, tn, idx) in blocks:
        ps, psB = C.bank(0, 2)
        for k in range(16):
            r = view[:, k, :]
            a = [list(x) for x in r.ap]
            rhs = bass.AP(r.tensor, r.offset, [a[0], [64, 4], [0, 2], [1, 64]])
            S.op("pe", lambda: nc.tensor.matmul(ps[0:tn, :], hT[:, k, t0:t0 + tn], rhs, start=(k == 0), stop=(k == 15)),
                 reads=[vB, hB], writes=[psB], inc=(k == 15))
        S.op("act", lambda: nc.scalar.copy(out=V2[0:tn, idx, :], in_=ps[0:tn, :]), reads=[psB], writes=[V2B])


def attention(C, A, es, w_in, hT, hB, kT, kTB, V2, V2B, aT, aTB, tmpq, tmpqB, nkv=4, jbs=range(-1, 8), skip_prev0=False, use_valid=True):
    nc, S = C.nc, C.S
    def sb(name, shape, dt):
        return es.enter_context(sbt(nc, name, shape, dt)), Buf(name)
    q2, q2B = sb("q2", [128, 2, NT], BF16)
    wtab, wtabB = sb("wtab", [128, 2, 512], F32)
    wmeta, wmetaB = sb("wmeta", [128, 512], F32)
    wqm, wqmB = sb("wqm", [128, 64], F32)
    expS = [sb(f"expS{i}", [128, 512], F32) for i in range(2)]
    pt = [sb(f"pt{i}", [128, 512], BF16) for i in range(6)]
    rec, recB = sb("rec", [128, 512], F32)
    exp_rr = 0
    for kv in range(nkv):
        view, vB = C.load_piece(wpiece(w_in, 0, 16, kv * 256, 256), 16, 256)
        for mm in range(2):
            for (t0, tn) in TILES:
                ps, psB = C.bank(0, 2)
                for k in range(16):
                    S.op("pe", lambda: nc.tensor.matmul(ps[:, 0:tn], view[:, k, mm * 128:(mm + 1) * 128], hT[:, k, t0:t0 + tn],
                                                        start=(k == 0), stop=(k == 15)),
                         reads=[vB, hB], writes=[psB], inc=(k == 15))
                headnorm_evac(C, A, ps, psB, tn, A.gq, A.gqB, q2[:, mm, t0:t0 + tn], q2B, tmpq, tmpqB)
        for hh in range(4):
            h = 4 * kv + hh
            for tl in range(2):
                S.op("act", lambda: nc.scalar.activation(out=wtab[:, tl, hh * 128:(hh + 1) * 128], in_=A.dband[:, tl, :], func=AF.Exp, scale=-SLOPES[h]),
                     reads=[A.dbandB], writes=[wtabB])
            S.op("act", lambda: nc.scalar.activation(out=wmeta[:, hh * 128:(hh + 1) * 128], in_=A.dmeta[:, :], func=AF.Exp, scale=-SLOPES[h], bias=A.bias[:, h:h + 1]),
                 reads=[A.dmetaB, A.biasB], writes=[wmetaB])
            S.op("act", lambda: nc.scalar.activation(out=wqm[:, hh * 16:(hh + 1) * 16], in_=A.dqm[:, :], func=AF.Exp, scale=-SLOPES[h], bias=A.bias[:, h:h + 1]),
                 reads=[A.dqmB, A.biasB], writes=[wqmB])
        jl = list(jbs)

        def stage1(idx):
            nonlocal exp_rr
            jb = jl[idx]
            par = idx % 2
            nq = 16 if jb < 0 else 128
            tq0 = 0 if jb < 0 else 16 + 128 * jb
            W4 = 4 * nq
            tiles = []
            if jb >= 0 and not (skip_prev0 and jb == 0):
                tiles.append((3 * par + 0, 32 + 128 * jb, 128, 1 + jb, "prev"))
            if jb >= 0:
                tiles.append((3 * par + 1, 160 + 128 * jb, 128, 2 + jb, "cur"))
            tiles.append((3 * par + 2, 0, 128, 0, "meta"))
            pts = []
            for ti, (bk, kc0, K, vidx, kind) in enumerate(tiles):
                ps, psB = C.ps[bk], C.psB[bk]
                for hh in range(4):
                    mm, half = hh // 2, hh % 2
                    S.op("pe", lambda: nc.tensor.matmul(ps[0:K, hh * nq:(hh + 1) * nq], kT[half][:, kv, kc0:kc0 + K], q2[:, mm, tq0:tq0 + nq],
                                                        start=True, stop=True),
                         reads=[kTB, q2B], writes=[psB], inc=(hh == 3))
                ex, exB = expS[exp_rr % 2]
                exp_rr += 1
                S.op("act", lambda: nc.scalar.activation(out=ex[0:K, 0:W4], in_=ps[0:K, 0:W4], func=AF.Exp, scale=0.125), reads=[psB], writes=[exB])
                p, pB = pt[3 * par + ti]
                if kind == "prev":
                    if jb == 0 and use_valid:
                        S.op("dve", lambda: nc.vector.scalar_tensor_tensor(out=p[:, 0:512], in0=ex[:, 0:512], scalar=A.valid[:, 0:1], in1=wtab[:, 0, :], op0=ALU.mult, op1=ALU.mult),
                             reads=[exB, A.validB, wtabB], writes=[pB])
                    else:
                        S.op("dve", lambda: nc.vector.tensor_tensor(out=p[:, 0:512], in0=ex[:, 0:512], in1=wtab[:, 0, :], op=ALU.mult), reads=[exB, wtabB], writes=[pB])
                elif kind == "cur":
                    S.op("dve", lambda: nc.vector.tensor_tensor(out=p[:, 0:512], in0=ex[:, 0:512], in1=wtab[:, 1, :], op=ALU.mult), reads=[exB, wtabB], writes=[pB])
                else:
                    if jb < 0:
                        S.op("dve", lambda: nc.vector.tensor_tensor(out=p[:, 0:64], in0=ex[:, 0:64], in1=wqm[:, :], op=ALU.mult), reads=[exB, wqmB], writes=[pB])
                    else:
                        for hh in range(4):
                            h = 4 * kv + hh
                            S.op("dve", lambda: nc.vector.scalar_tensor_tensor(out=p[:, hh * 128:(hh + 1) * 128], in0=ex[:, hh * 128:(hh + 1) * 128],
                                                                               scalar=A.garg[:, h * 8 + jb:h * 8 + jb + 1], in1=wmeta[:, hh * 128:(hh + 1) * 128],
                                                                               op0=ALU.mult, op1=ALU.mult),
                                 reads=[exB, A.gargB, wmetaB], writes=[pB])
                pts.append((p, pB, K, vidx))
            return (nq, tq0, W4, pts)

        def stage2(st):
            nq, tq0, W4, pts = st
            num, numB = C.ps[6], C.psB[6]
            den, denB = C.ps[7], C.psB[7]
            for ti, (p, pB, K, vidx) in enumerate(pts):
                S.op("pe", lambda: nc.tensor.matmul(num[:, 0:W4], V2[0:K, vidx, kv * 128:(kv + 1) * 128], p[0:K, 0:W4], start=(ti == 0), stop=(ti == len(pts) - 1)),
                     reads=[V2B, pB], writes=[numB], inc=True)
                S.op("pe", lambda: nc.tensor.matmul(den[:, 0:W4], C.ones_bf[0:K, :], p[0:K, 0:W4], start=(ti == 0), stop=(ti == len(pts) - 1)),
                     reads=[C.onesB, pB], writes=[denB], inc=True)
            S.op("dve", lambda: nc.vector.reciprocal(out=rec[:, 0:W4], in_=den[:, 0:W4]), reads=[denB], writes=[recB])
            for hh in range(4):
                mm, half = hh // 2, hh % 2
                r0 = half * 64
                c = 2 * kv + mm
                S.op("dve", lambda: nc.vector.tensor_tensor(out=aT[r0:r0 + 64, c, tq0:tq0 + nq], in0=num[r0:r0 + 64, hh * nq:(hh + 1) * nq],
                                                            in1=rec[r0:r0 + 64, hh * nq:(hh + 1) * nq], op=ALU.mult),
                     reads=[numB, recB], writes=[aTB])

        prev = None
        for idx in range(len(jl)):
            cur = stage1(idx)
            if prev is not None:
                stage2(prev)
            prev = cur
        if prev is not None:
            stage2(prev)


def build_attn_test(stage=2, nkv=4, jbs=range(-1, 8)):
    nc = bass.Bass("TRN2", target_bir_lowering=False)
    def din(name, shape, dt=F32):
        return nc.dram_tensor(name, shape, dt, kind="ExternalInput").ap()
    xin = din("xT_in", [128, 16, NT])
    w_in = din("w_in", [D, 2560])
    gmix = din("g_mix", [128, 16])
    dband, dmeta, dqm = din("dband", [128, 2, 128]), din("dmeta", [128, 128]), din("dqm", [128, 16])
    bias_in, garg_in, valid_in = din("abias", [128, 16]), din("garg", [128, 128]), din("valid", [128, 1])
    gq_in, gk_in = din("gq2", [128, 1]), din("gk2", [128, 1])
    kth_in, vh_in = din("kth", [128, 4, 128], BF16), din("vh", [128, 512], BF16)
    aout = nc.dram_tensor("aT_out", [128, 8, NT], F32, kind="ExternalOutput").ap()
    kout = nc.dram_tensor("kT_out", [128, 4, KTW], BF16, kind="ExternalOutput").ap()
    kout2 = nc.dram_tensor("kT_out2", [128, 4, KTW], BF16, kind="ExternalOutput").ap()
    vout = nc.dram_tensor("V2_out", [128, 10, 512], BF16, kind="ExternalOutput").ap()
    with ExitStack() as es:
        C = Ctx(nc, es)
        S = C.S
        C.consts()
        xT = es.enter_context(sbt(nc, "xT", [128, 16, NT], F32)); xB = Buf("xT")
        regH = es.enter_context(sbt(nc, "regH", [128, 16, NT], BF16)); hB = Buf("regH")
        gm = es.enter_context(sbt(nc, "gm", [128, 16], F32)); gmB = Buf("gm")
        kT = [es.enter_context(sbt(nc, f"kT{i}", [128, 4, KTW], BF16)) for i in range(2)]; kTB = Buf("kT")
        V2 = es.enter_context(sbt(nc, "V2", [128, 10, 512], BF16)); V2B = Buf("V2")
        aT = es.enter_context(sbt(nc, "aT", [128, 8, NT], F32)); aTB = Buf("aT")
        tmpq = es.enter_context(sbt(nc, "tmpq", [128, 512], F32)); tmpqB = Buf("tmpq")
        S.dma("sp", xT[:], xin, writes=[xB])
        S.dma("sp", gm[:], gmix, writes=[gmB])
        A = attn_consts(C, es, dband, dmeta, dqm, bias_in, garg_in, valid_in, gq_in, gk_in)
        C.rmsnorm(xT, xB, 16, gm, gmB, regH, hB)
        kv_proj(C, A, w_in, regH, hB, kT, kTB, V2, V2B, kth_in, vh_in, tmpq, tmpqB)
        S.dma("sp", kout, kT[0][:], reads=[kTB], writes=[Buf("kout")])
        S.dma("sp", kout2, kT[1][:], reads=[kTB], writes=[Buf("kout2")])
        S.dma("sp", vout, V2[:], reads=[V2B], writes=[Buf("vout")])
        with ExitStack() as es2:
            if stage >= 2:
                attention(C, A, es2, w_in, regH, hB, kT, kTB, V2, V2B, aT, aTB, tmpq, tmpqB, nkv, jbs)
                S.dma("sp", aout, aT[:], reads=[aTB], writes=[Buf("aout")])
            S.barrier_all()
        print("instructions", S.n_ins, "waits", S.n_wait)
    return nc


def attn_host_consts(q):
    BIG = 1.0e6
    i = np.arange(128)[None, :]
    s = np.arange(128)[:, None]
    dband = np.zeros((128, 2, 128), np.float32)
    dprev = (i - s + 128).astype(np.float32)
    dband[:, 0, :] = np.where(s > i, dprev, BIG)
    dband[:, 1, :] = np.where(s <= i, (i - s).astype(np.float32), BIG)
    dmeta = np.full((128, 128), BIG, np.float32)
    m = np.arange(16)[:, None]
    dmeta[:16] = (i - m + 16)
    dmeta[16] = 0.0
    dqm = np.full((128, 16), BIG, np.float32)
    t = np.arange(16)[None, :]
    dqm[:16] = np.where(m <= t, (t - m).astype(np.float32), BIG)
    dqm[16] = 0.0
    garg = np.zeros((128, 16, 8), np.float32)
    for h in range(16):
        for j in range(8):
            garg[:16, h, j] = -SLOPES[h] * (1024 * q + 128 * j)
    valid = np.full((128, 1), 1.0 if q > 0 else 0.0, np.float32)
    return dict(dband=dband, dmeta=dmeta, dqm=dqm, garg=garg.reshape(128, 128), valid=valid)


def sink_bias(sinks):
    b = np.zeros((128, 16), np.float32)
    b[16, :] = sinks
    return b


TWO_PI = float(2 * np.pi)
MAGIC = 12582912.0
NTAB = 1041


class T:
    def __init__(self, nc, es, name, shape, dt=F32):
        self.t = es.enter_context(sbt(nc, name, shape, dt))
        self.B = Buf(name)


def v_tt(S, nc, e, out, a, b, op, reads, writes):
    eng = nc.vector if e == "dve" else nc.gpsimd
    S.op(e, lambda: eng.tensor_tensor(out=out, in0=a, in1=b, op=op), reads=reads, writes=writes)


def v_ts(S, nc, e, out, a, s1, s2, op0, op1, reads, writes):
    eng = nc.vector if e == "dve" else nc.gpsimd
    if s2 is None:
        S.op(e, lambda: eng.tensor_scalar(out=out, in0=a, scalar1=s1, scalar2=None, op0=op0), reads=reads, writes=writes)
    else:
        S.op(e, lambda: eng.tensor_scalar(out=out, in0=a, scalar1=s1, scalar2=s2, op0=op0, op1=op1), reads=reads, writes=writes)


def range_reduce(S, nc, x, xB, k, kB):
    v_ts(S, nc, "dve", k, x, 1.0 / TWO_PI, MAGIC, ALU.mult, ALU.add, [xB], [kB])
    v_ts(S, nc, "dve", k, k, MAGIC, -TWO_PI, ALU.subtract, ALU.mult, [kB], [kB])
    v_tt(S, nc, "dve", x, x, k, ALU.add, [xB, kB], [xB])


def sincos(C, x, xB, k, kB, sin_out, sinB, cos_out, cosB):
    nc, S = C.nc, C.S
    range_reduce(S, nc, x, xB, k, kB)
    S.op("act", lambda: nc.scalar.activation(out=sin_out, in_=x, func=AF.Sin), reads=[xB], writes=[sinB])
    S.op("act", lambda: nc.scalar.activation(out=k, in_=x, func=AF.Abs), reads=[xB], writes=[kB])
    S.op("act", lambda: nc.scalar.activation(out=cos_out, in_=k, func=AF.Sin, scale=-1.0, bias=C.halfpi[:, 0:1]), reads=[kB, C.hpB], writes=[cosB])


def ssm_prep(C, es, lamS_in, lamB_in, bB_in, cS_in, tvals_in, mask8_in, sgn_in):
    nc, S = C.nc, C.S
    P = type("P", (), {})()
    def ld(name, shape, src):
        t = T(nc, es, "p_" + name, shape)
        S.dma("sp", t.t[:], src, writes=[t.B])
        return t
    C.halfpi = es.enter_context(sbt(nc, "halfpi", [128, 1], F32))
    C.hpB = Buf("halfpi")
    S.op("dve", lambda: nc.vector.memset(C.halfpi[:], float(np.pi / 2)), writes=[C.hpB])
    P.tvals = ld("tvals", [128, NTAB], tvals_in)
    P.mask8 = ld("mask8", [128, 8], mask8_in)
    P.sgn = ld("sgn", [128, 1], sgn_in)
    lamS = ld("lamS", [128, 3, 64], lamS_in)
    P.thS = T(nc, es, "thS", [128, 64])
    P.rhoS = T(nc, es, "rhoS", [128, 64])
    P.lrS = T(nc, es, "lrS", [128, 64])
    P.BA = T(nc, es, "BAfull", [128, 8, 128])
    P.BB = T(nc, es, "BBfull", [128, 8, 128])
    P.CA = T(nc, es, "CAall", [128, 64, 16])
    P.CB = T(nc, es, "CBall", [128, 64, 16])
    dS = T(nc, es, "dS", [128, 64])
    S.op("act", lambda: nc.scalar.activation(out=dS.t[:], in_=lamS.t[:, 2, :], func=AF.Exp), reads=[lamS.B], writes=[dS.B])
    v_tt(S, nc, "dve", P.lrS.t[:], lamS.t[:, 0, :], dS.t[:], ALU.mult, [lamS.B, dS.B], [P.lrS.B])
    v_tt(S, nc, "dve", P.thS.t[:], lamS.t[:, 1, :], dS.t[:], ALU.mult, [lamS.B, dS.B], [P.thS.B])
    S.op("act", lambda: nc.scalar.activation(out=P.rhoS.t[:], in_=P.lrS.t[:], func=AF.Exp), reads=[P.lrS.B], writes=[P.rhoS.B])
    P.th2p = T(nc, es, "th2p", [128, 64])
    v_ts(S, nc, "dve", P.th2p.t[:], P.thS.t[:], 1.0 / TWO_PI, None, ALU.mult, None, [P.thS.B], [P.th2p.B])
    with ExitStack() as es2:
        cS = T(nc, es2, "p_cS", [128, 2, 64, 16])
        S.dma("sp", cS.t[:], cS_in, writes=[cS.B])
        S.op("dve", lambda: nc.vector.tensor_copy(out=P.CA.t[0:64], in_=cS.t[0:64, 0]), reads=[cS.B], writes=[P.CA.B])
        v_ts(S, nc, "dve", P.CA.t[64:128], cS.t[64:128, 1], -1.0, None, ALU.mult, None, [cS.B], [P.CA.B])
        v_ts(S, nc, "dve", P.CB.t[0:64], cS.t[0:64, 1], -1.0, None, ALU.mult, None, [cS.B], [P.CB.B])
        v_ts(S, nc, "dve", P.CB.t[64:128], cS.t[64:128, 0], -1.0, None, ALU.mult, None, [cS.B], [P.CB.B])
        lamB = T(nc, es2, "p_lamB", [128, 3, 512])
        S.dma("sp", lamB.t[:], lamB_in, writes=[lamB.B])
        bB = T(nc, es2, "p_bB", [128, 2, 512])
        S.dma("sp", bB.t[:], bB_in, writes=[bB.B])
        tm = [T(nc, es2, f"p_tm{i}", [128, 512]) for i in range(8)]
        dB, lr, th, kk, sn, cs, mg, t7 = tm
        lre, lim = lamB.t[:, 0, :], lamB.t[:, 1, :]
        S.op("act", lambda: nc.scalar.activation(out=dB.t[:], in_=lamB.t[:, 2, :], func=AF.Exp), reads=[lamB.B], writes=[dB.B])
        v_tt(S, nc, "dve", lr.t[:], lre, dB.t[:], ALU.mult, [lamB.B, dB.B], [lr.B])
        v_tt(S, nc, "dve", th.t[:], lim, dB.t[:], ALU.mult, [lamB.B, dB.B], [th.B])
        S.op("act", lambda: nc.scalar.activation(out=mg.t[:], in_=lr.t[:], func=AF.Exp), reads=[lr.B], writes=[mg.B])
        sincos(C, th.t[:], th.B, kk.t[:], kk.B, sn.t[:], sn.B, cs.t[:], cs.B)
        v_tt(S, nc, "dve", cs.t[:], cs.t[:], mg.t[:], ALU.mult, [cs.B, mg.B], [cs.B])
        v_ts(S, nc, "dve", cs.t[:], cs.t[:], -1.0, None, ALU.add, None, [cs.B], [cs.B])
        v_tt(S, nc, "dve", sn.t[:], sn.t[:], mg.t[:], ALU.mult, [sn.B, mg.B], [sn.B])
        a, b = cs, sn
        v_tt(S, nc, "dve", mg.t[:], lre, lre, ALU.mult, [lamB.B], [mg.B])
        v_tt(S, nc, "dve", kk.t[:], lim, lim, ALU.mult, [lamB.B], [kk.B])
        v_tt(S, nc, "dve", mg.t[:], mg.t[:], kk.t[:], ALU.add, [mg.B, kk.B], [mg.B])
        S.op("dve", lambda: nc.vector.reciprocal(out=mg.t[:], in_=mg.t[:]), reads=[mg.B], writes=[mg.B])
        v_tt(S, nc, "dve", lr.t[:], a.t[:], lre, ALU.mult, [a.B, lamB.B], [lr.B])
        v_tt(S, nc, "dve", kk.t[:], b.t[:], lim, ALU.mult, [b.B, lamB.B], [kk.B])
        v_tt(S, nc, "dve", lr.t[:], lr.t[:], kk.t[:], ALU.add, [lr.B, kk.B], [lr.B])
        v_tt(S, nc, "dve", lr.t[:], lr.t[:], mg.t[:], ALU.mult, [lr.B, mg.B], [lr.B])
        v_tt(S, nc, "dve", th.t[:], b.t[:], lre, ALU.mult, [b.B, lamB.B], [th.B])
        v_tt(S, nc, "dve", kk.t[:], a.t[:], lim, ALU.mult, [a.B, lamB.B], [kk.B])
        v_tt(S, nc, "dve", th.t[:], th.t[:], kk.t[:], ALU.subtract, [th.B, kk.B], [th.B])
        v_tt(S, nc, "dve", th.t[:], th.t[:], mg.t[:], ALU.mult, [th.B, mg.B], [th.B])
        cr, ci = lr, th
        bre, bim = bB.t[:, 0, :], bB.t[:, 1, :]
        v_tt(S, nc, "dve", dB.t[:], cr.t[:], bre, ALU.mult, [cr.B, bB.B], [dB.B])
        v_tt(S, nc, "dve", kk.t[:], ci.t[:], bim, ALU.mult, [ci.B, bB.B], [kk.B])
        v_tt(S, nc, "dve", dB.t[:], dB.t[:], kk.t[:], ALU.subtract, [dB.B, kk.B], [dB.B])
        v_tt(S, nc, "dve", t7.t[:], cr.t[:], bim, ALU.mult, [cr.B, bB.B], [t7.B])
        v_tt(S, nc, "dve", kk.t[:], ci.t[:], bre, ALU.mult, [ci.B, bB.B], [kk.B])
        v_tt(S, nc, "dve", t7.t[:], t7.t[:], kk.t[:], ALU.add, [t7.B, kk.B], [t7.B])
        bbr = dB.t[:].rearrange("p (c n) -> p c n", c=8)
        bbi = t7.t[:].rearrange("p (c n) -> p c n", c=8)
        S.op("dve", lambda: nc.vector.tensor_copy(out=P.BA.t[:, :, 0:64], in_=bbr), reads=[dB.B], writes=[P.BA.B])
        S.op("dve", lambda: nc.vector.tensor_copy(out=P.BA.t[:, :, 64:128], in_=bbi), reads=[t7.B], writes=[P.BA.B])
        S.op("dve", lambda: nc.vector.tensor_copy(out=P.BB.t[:, :, 0:64], in_=bbi), reads=[t7.B], writes=[P.BB.B])
        v_ts(S, nc, "dve", P.BB.t[:, :, 64:128], bbr, -1.0, None, ALU.mult, None, [dB.B], [P.BB.B])
        C.S.barrier_all()
    return P


def ssm_xstart(C, es, P, nat_in, swp_in):
    nc, S = C.nc, C.S
    xs = T(nc, es, "xstart", [128, 64])
    with ExitStack() as es2:
        nat = T(nc, es2, "x_nat", [128, 4, 64]); S.dma("sp", nat.t[:], nat_in, writes=[nat.B])
        swp = T(nc, es2, "x_swp", [128, 4, 64]); S.dma("sp", swp.t[:], swp_in, writes=[swp.B])
        mg, an, kk, sn, cs, t1 = [T(nc, es2, f"x_t{i}", [128, 64]) for i in range(6)]
        S.op("dve", lambda: nc.vector.memset(xs.t[:], 0.0), writes=[xs.B])
        for p in range(4):
            S.op("act", lambda: nc.scalar.activation(out=mg.t[:], in_=P.lrS.t[:], func=AF.Exp, scale=float(1024 * p)), reads=[P.lrS.B], writes=[mg.B])
            v_ts(S, nc, "dve", an.t[:], P.thS.t[:], float(1024 * (p + 1)), None, ALU.mult, None, [P.thS.B], [an.B])
            sincos(C, an.t[:], an.B, kk.t[:], kk.B, sn.t[:], sn.B, cs.t[:], cs.B)
            v_tt(S, nc, "dve", cs.t[:], cs.t[:], mg.t[:], ALU.mult, [cs.B, mg.B], [cs.B])
            v_tt(S, nc, "dve", sn.t[:], sn.t[:], mg.t[:], ALU.mult, [sn.B, mg.B], [sn.B])
            v_ts(S, nc, "dve", sn.t[:], sn.t[:], P.sgn.t[:, 0:1], None, ALU.mult, None, [sn.B, P.sgn.B], [sn.B])
            v_tt(S, nc, "dve", t1.t[:], cs.t[:], nat.t[:, p, :], ALU.mult, [cs.B, nat.B], [t1.B])
            v_tt(S, nc, "dve", xs.t[:], xs.t[:], t1.t[:], ALU.add, [xs.B, t1.B], [xs.B])
            v_tt(S, nc, "dve", t1.t[:], sn.t[:], swp.t[:, p, :], ALU.mult, [sn.B, swp.B], [t1.B])
            v_tt(S, nc, "dve", xs.t[:], xs.t[:], t1.t[:], ALU.add, [xs.B, t1.B], [xs.B])
        C.S.barrier_all()
    return xs


def tslice(t0, tn, cont=False):
    if cont:
        return (t0 + 1, tn)
    return (1009, 16) if t0 == 0 else (t0 - 15, tn)


def ssm_loop(C, es, P, uT, uB, big, mode, xs=None, yT=None, yB=None, dcol=None, wend=None, cont=False, tab=None, tab_mode=None):
    nc, S = C.nc, C.S
    (cosT, cosB), (sinT, sinB), (xa, xaB), (ka, kaB), (Vb, VB), (Wb, WB) = big[0:6]
    tabs2 = [((cosT, cosB), (sinT, sinB)), ((xa, xaB), (ka, kaB))]

    def load_tab(g):
        (cT, cB), (sT, sB) = tabs2[g % 2]
        S.dma("sp", cT[:, 0:NTAB], tab[0][g, 0], reads=[tab[1][g]], writes=[cB])
        S.dma("sp", sT[:, 0:NTAB], tab[0][g, 1], reads=[tab[1][g]], writes=[sB])

    if tab_mode == "load":
        load_tab(0)
    t1 = [T(nc, es, f"s_t1{i}", [128, 512]) for i in range(2)]
    t2 = [T(nc, es, f"s_t2{i}", [128, 512]) for i in range(2)]
    bpA = [T(nc, es, f"s_bpA{i}", [128, 128], BF16) for i in range(2)]
    bpB = [T(nc, es, f"s_bpB{i}", [128, 128], BF16) for i in range(2)]
    if mode != "A":
        A1 = T(nc, es, "s_A1", [128, NT], BF16)
        A2 = T(nc, es, "s_A2", [128, NT], BF16)
        wbA = [T(nc, es, f"s_wbA{i}", [128, 240], BF16) for i in range(2)]
        wbB = [T(nc, es, f"s_wbB{i}", [128, 240], BF16) for i in range(2)]
        for w in wbA + wbB:
            S.op("dve", lambda: nc.vector.memset(w.t[:], 0.0), writes=[w.B])
    rr = 0
    zr = 0
    for g in range(64):
        fc, gl = g // 8, g % 8
        i2 = g % 2
        S.op("pool", lambda: nc.gpsimd.tensor_scalar(out=bpA[i2].t[:], in0=P.BA.t[:, fc, :], scalar1=P.mask8.t[:, gl:gl + 1], scalar2=None, op0=ALU.mult),
             reads=[P.BA.B, P.mask8.B], writes=[bpA[i2].B])
        S.op("pool", lambda: nc.gpsimd.tensor_scalar(out=bpB[i2].t[:], in0=P.BB.t[:, fc, :], scalar1=P.mask8.t[:, gl:gl + 1], scalar2=None, op0=ALU.mult),
             reads=[P.BB.B, P.mask8.B], writes=[bpB[i2].B])
        if tab_mode == "load":
            (cosT, cosB), (sinT, sinB) = tabs2[g % 2]
            if g + 1 < 64:
                load_tab(g + 1)
        else:
            xr, kr = xa[:, 0:NTAB], ka[:, 0:NTAB]
            S.op("dve", lambda: nc.vector.tensor_scalar(out=kr, in0=P.tvals.t[:, :], scalar1=P.th2p.t[:, g:g + 1], scalar2=MAGIC, op0=ALU.mult, op1=ALU.add),
                 reads=[P.tvals.B, P.th2p.B], writes=[kaB])
            S.op("dve", lambda: nc.vector.tensor_scalar(out=kr, in0=kr, scalar1=MAGIC, scalar2=-TWO_PI, op0=ALU.subtract, op1=ALU.mult), reads=[kaB], writes=[kaB])
            S.op("dve", lambda: nc.vector.scalar_tensor_tensor(out=xr, in0=P.tvals.t[:, :], scalar=P.thS.t[:, g:g + 1], in1=kr, op0=ALU.mult, op1=ALU.add),
                 reads=[P.tvals.B, P.thS.B, kaB], writes=[xaB])
            S.op("act", lambda: nc.scalar.activation(out=sinT[:, 0:NTAB], in_=xr, func=AF.Sin), reads=[xaB], writes=[sinB])
            S.op("act", lambda: nc.scalar.activation(out=kr, in_=xr, func=AF.Abs), reads=[xaB], writes=[kaB])
            S.op("act", lambda: nc.scalar.activation(out=cosT[:, 0:NTAB], in_=kr, func=AF.Sin, scale=-1.0, bias=C.halfpi[:, 0:1]), reads=[kaB, C.hpB], writes=[cosB])
            if tab_mode == "store":
                S.dma("sp", tab[0][g, 0], cosT[:, 0:NTAB], reads=[cosB], writes=[tab[1][g]])
                S.dma("sp", tab[0][g, 1], sinT[:, 0:NTAB], reads=[sinB], writes=[tab[1][g]])
        for (t0, tn) in TILES:
            s0, sn_ = tslice(t0, tn, cont)
            pz, pzB = C.ps[zr % 4], C.psB[zr % 4]; zr += 1
            pzs, pzsB = C.ps[zr % 4], C.psB[zr % 4]; zr += 1
            S.op("pe", lambda: nc.tensor.matmul(pz[:, 0:tn], bpA[i2].t[:], uT[:, fc, t0:t0 + tn], start=True, stop=True), reads=[bpA[i2].B, uB], writes=[pzB])
            S.op("pe", lambda: nc.tensor.matmul(pzs[:, 0:tn], bpB[i2].t[:], uT[:, fc, t0:t0 + tn], start=True, stop=True), reads=[bpB[i2].B, uB], writes=[pzsB])
            j = rr % 2; rr += 1
            v_tt(S, nc, "dve", t1[j].t[:, 0:tn], pz[:, 0:tn], cosT[:, s0:s0 + tn], ALU.mult, [pzB, cosB], [t1[j].B])
            v_tt(S, nc, "dve", t2[j].t[:, 0:tn], pzs[:, 0:tn], sinT[:, s0:s0 + tn], ALU.mult, [pzsB, sinB], [t2[j].B])
            v_tt(S, nc, "dve", Vb[:, t0:t0 + tn], t1[j].t[:, 0:tn], t2[j].t[:, 0:tn], ALU.add, [t1[j].B, t2[j].B], [VB])
        rho = P.rhoS.t[:, g:g + 1]
        if cont:
            S.op("dve", lambda: nc.vector.tensor_tensor_scan(out=Wb[:, 0:NT], data0=rho.to_broadcast([128, NT]), data1=Vb[:, 0:NT], initial=0.0, op0=ALU.mult, op1=ALU.add),
                 reads=[VB, P.rhoS.B], writes=[WB])
        else:
            S.op("dve", lambda: nc.vector.tensor_tensor_scan(out=Wb[:, 0:16], data0=rho.to_broadcast([128, 16]), data1=Vb[:, 0:16], initial=0.0, op0=ALU.mult, op1=ALU.add),
                 reads=[VB, P.rhoS.B], writes=[WB])
            init = 0.0 if mode == "A" else xs.t[:, g:g + 1]
            S.op("dve", lambda: nc.vector.tensor_tensor_scan(out=Wb[:, 16:NT], data0=rho.to_broadcast([128, NT - 16]), data1=Vb[:, 16:NT], initial=init, op0=ALU.mult, op1=ALU.add),
                 reads=[VB, P.rhoS.B] + ([xs.B] if mode != "A" else []), writes=[WB])
        if mode == "A":
            S.op("act", lambda: nc.scalar.copy(out=wend.t[:, 0, g:g + 1], in_=Wb[:, NT - 1:NT]), reads=[WB], writes=[wend.B])
            S.op("act", lambda: nc.scalar.copy(out=wend.t[:, 1, g:g + 1], in_=Wb[:, 15:16]), reads=[WB], writes=[wend.B])
            continue
        if mode == "F":
            S.op("act", lambda: nc.scalar.copy(out=wend.t[:, g:g + 1], in_=Wb[:, NT - 1:NT]), reads=[WB], writes=[wend.B])
        if cont:
            v_tt(S, nc, "dve", A1.t[:, 0:NT], Wb[:, 0:NT], cosT[:, 1:NT + 1], ALU.mult, [WB, cosB], [A1.B])
            v_tt(S, nc, "pool", A2.t[:, 0:NT], Wb[:, 0:NT], sinT[:, 1:NT + 1], ALU.mult, [WB, sinB], [A2.B])
        else:
            v_tt(S, nc, "dve", A1.t[:, 0:16], Wb[:, 0:16], cosT[:, 1009:1025], ALU.mult, [WB, cosB], [A1.B])
            v_tt(S, nc, "dve", A1.t[:, 16:NT], Wb[:, 16:NT], cosT[:, 1:1025], ALU.mult, [WB, cosB], [A1.B])
            v_tt(S, nc, "pool", A2.t[:, 0:16], Wb[:, 0:16], sinT[:, 1009:1025], ALU.mult, [WB, sinB], [A2.B])
            v_tt(S, nc, "pool", A2.t[:, 16:NT], Wb[:, 16:NT], sinT[:, 1:1025], ALU.mult, [WB, sinB], [A2.B])
        S.op("act", lambda: nc.scalar.copy(out=wbA[i2].t[:, 112:128], in_=P.CA.t[:, g, :]), reads=[P.CA.B], writes=[wbA[i2].B])
        S.op("act", lambda: nc.scalar.copy(out=wbB[i2].t[:, 112:128], in_=P.CB.t[:, g, :]), reads=[P.CB.B], writes=[wbB[i2].B])
        o = 112 - 16 * gl
        for ti, (t0, tn) in enumerate(TILES):
            py, pyB = C.ps[5 + ti], C.psB[5 + ti]
            S.op("pe", lambda: nc.tensor.matmul(py[:, 0:tn], wbA[i2].t[:, o:o + 128], A1.t[:, t0:t0 + tn], start=(gl == 0), stop=False), reads=[wbA[i2].B, A1.B], writes=[pyB])
            S.op("pe", lambda: nc.tensor.matmul(py[:, 0:tn], wbB[i2].t[:, o:o + 128], A2.t[:, t0:t0 + tn], start=False, stop=(gl == 7)), reads=[wbB[i2].B, A2.B], writes=[pyB])
            if gl == 7:
                S.op("dve", lambda: nc.vector.scalar_tensor_tensor(out=yT[:, fc, t0:t0 + tn], in0=uT[:, fc, t0:t0 + tn], scalar=dcol.t[:, fc:fc + 1], in1=py[:, 0:tn],
                                                                   op0=ALU.mult, op1=ALU.add),
                     reads=[uB, dcol.B, pyB], writes=[yB])


def big_rows(regX):
    flat = regX[:].rearrange("p c t -> p (c t)")
    return [(flat[:, i * 1056:(i + 1) * 1056], Buf(f"big{i}")) for i in range(7)]


def u_proj(C, w_in, hT, hB, uT, uB):
    nc, S = C.nc, C.S
    for pc in range(4):
        view, vB = C.load_piece(wpiece(w_in, 0, 16, 1536 + pc * 256, 256), 16, 256)
        for mm in range(2):
            m = pc * 2 + mm
            for (t0, tn) in TILES:
                ps, psB = C.bank(0, 4)
                for k in range(16):
                    S.op("pe", lambda: nc.tensor.matmul(ps[:, 0:tn], view[:, k, mm * 128:(mm + 1) * 128], hT[:, k, t0:t0 + tn], start=(k == 0), stop=(k == 15)),
                         reads=[vB, hB], writes=[psB], inc=(k == 15))
                S.op("act", lambda: nc.scalar.copy(out=uT[:, m, t0:t0 + tn], in_=ps[:, 0:tn]), reads=[psB], writes=[uB])


def gelu_glu(C, es, w_glu, bglu, yT, yB, gT, gB, tmp, tmpB):
    nc, S = C.nc, C.S
    sg = [T(nc, es, f"g_sg{i}", [128, 512]) for i in range(2)]
    for m in range(8):
        y = yT[:, m, :]
        v_tt(S, nc, "dve", tmp, y, y, ALU.mult, [yB], [tmpB])
        v_ts(S, nc, "dve", tmp, tmp, 0.044715, 1.0, ALU.mult, ALU.add, [tmpB], [tmpB])
        v_tt(S, nc, "dve", tmp, tmp, y, ALU.mult, [tmpB, yB], [tmpB])
        S.op("act", lambda: nc.scalar.activation(out=tmp, in_=tmp, func=AF.Sigmoid, scale=1.5957691216057308), reads=[tmpB], writes=[tmpB])
        v_tt(S, nc, "dve", y, y, tmp, ALU.mult, [yB, tmpB], [yB])
        S.op("act", lambda: nc.scalar.copy(out=gT[:, m, :], in_=y), reads=[yB], writes=[gB])
    r = 0
    for pc in range(4):
        view, vB = C.load_piece(wpiece(w_glu, 0, 8, pc * 256, 256), 8, 256)
        for mm in range(2):
            m = pc * 2 + mm
            for (t0, tn) in TILES:
                ps, psB = C.bank(0, 4)
                for k in range(8):
                    S.op("pe", lambda: nc.tensor.matmul(ps[:, 0:tn], view[:, k, mm * 128:(mm + 1) * 128], gT[:, k, t0:t0 + tn], start=(k == 0), stop=(k == 7)),
                         reads=[vB, gB], writes=[psB], inc=(k == 7))
                j = r % 2; r += 1
                S.op("act", lambda: nc.scalar.activation(out=sg[j].t[:, 0:tn], in_=ps[:, 0:tn], func=AF.Sigmoid, bias=bglu.t[:, m:m + 1]),
                     reads=[psB, bglu.B], writes=[sg[j].B])
                v_tt(S, nc, "dve", yT[:, m, t0:t0 + tn], yT[:, m, t0:t0 + tn], sg[j].t[:, 0:tn], ALU.mult, [yB, sg[j].B], [yB])

def _din(nc, name, shape, dt=F32):
    return nc.dram_tensor(name, shape, dt, kind="ExternalInput").ap()


def _dout(nc, name, shape, dt=F32):
    return nc.dram_tensor(name, shape, dt, kind="ExternalOutput").ap()


def _ssm_inputs(nc):
    return dict(lamS=_din(nc, "lamS", [128, 3, 64]), lamB=_din(nc, "lamB", [128, 3, 512]), bB=_din(nc, "bB", [128, 2, 512]),
                cS=_din(nc, "cS", [128, 2, 64, 16]), tvals=_din(nc, "tvals", [128, NTAB]), mask8=_din(nc, "mask8", [128, 8]),
                sgn=_din(nc, "sgn", [128, 1]))


def build_A():
    nc = bass.Bass("TRN2", target_bir_lowering=False)
    xin = _din(nc, "xT_in", [128, 16, NT])
    w_in = _din(nc, "w_in", [D, 2560])
    gmix = _din(nc, "g_mix", [128, 16])
    gk_in = _din(nc, "gk2", [128, 1])
    si = _ssm_inputs(nc)
    kth_out = _dout(nc, "kth_out", [128, 4, 128], BF16)
    vh_out = _dout(nc, "vh_out", [128, 512], BF16)
    wend_out = _dout(nc, "wend_out", [128, 2, 64])
    with ExitStack() as es:
        C = Ctx(nc, es)
        S = C.S
        C.consts()
        regX = es.enter_context(sbt(nc, "regX", [128, 16, NT], F32)); xB = Buf("xT")
        regH = es.enter_context(sbt(nc, "regH", [128, 16, NT], BF16)); hB = Buf("regH")
        uT = es.enter_context(sbt(nc, "uT", [128, 8, NT], BF16)); uB = Buf("uT")
        gm = T(nc, es, "gm", [128, 16]); S.dma("sp", gm.t[:], gmix, writes=[gm.B])
        A = type("A", (), {})()
        A.gk = es.enter_context(sbt(nc, "sb_gk2", [128, 1], F32)); A.gkB = Buf("gk2")
        S.dma("sp", A.gk[:], gk_in, writes=[A.gkB])
        A.blk = es.enter_context(sbt(nc, "blkones", [128, 128], BF16)); A.blkB = Buf("blkones")
        S.op("dve", lambda: nc.vector.memset(A.blk[:], 0.0), writes=[A.blkB])
        S.op("dve", lambda: nc.vector.memset(A.blk[0:64, 0:64], 1.0), writes=[A.blkB])
        S.op("dve", lambda: nc.vector.memset(A.blk[64:128, 64:128], 1.0), writes=[A.blkB])
        S.dma("sp", regX[:], xin, writes=[xB])
        C.rmsnorm(regX, xB, 16, gm.t, gm.B, regH, hB)
        u_proj(C, w_in, regH, hB, uT, uB)
        with ExitStack() as es1:
            kT = [es1.enter_context(sbt(nc, f"kT{i}", [128, 4, KTW], BF16)) for i in range(2)]; kTB = Buf("kT")
            V2 = es1.enter_context(sbt(nc, "V2", [128, 10, 512], BF16)); V2B = Buf("V2")
            tmpq = es1.enter_context(sbt(nc, "tmpq", [128, 512], F32)); tmpqB = Buf("tmpq")
            kv_proj(C, A, w_in, regH, hB, kT, kTB, V2, V2B, None, None, tmpq, tmpqB, ktiles=[(912, 128)], vblocks=[(912, 128, 9)])
            S.dma("sp", kth_out[0:64], kT[0][0:64, :, 1056:1184], reads=[kTB], writes=[Buf("o1")])
            S.dma("sp", kth_out[64:128], kT[1][64:128, :, 1056:1184], reads=[kTB], writes=[Buf("o2")])
            S.dma("sp", vh_out, V2[:, 9, :], reads=[V2B], writes=[Buf("o3")])
            S.barrier_all()
        with ExitStack() as es2:
            P = ssm_prep(C, es2, si["lamS"], si["lamB"], si["bB"], si["cS"], si["tvals"], si["mask8"], si["sgn"])
            wend = T(nc, es2, "wend", [128, 2, 64])
            big = big_rows(regX)
            ssm_loop(C, es2, P, uT, uB, big, "A", wend=wend)
            S.dma("sp", wend_out, wend.t[:], reads=[wend.B], writes=[Buf("o4")])
            S.barrier_all()
        print("A instructions", S.n_ins, "waits", S.n_wait)
    return nc


def build_B(debug=False):
    nc = bass.Bass("TRN2", target_bir_lowering=False)
    xin = _din(nc, "xT_in", [128, 16, NT])
    xout = _dout(nc, "xT_out", [128, 16, NT])
    w_in = _din(nc, "w_in", [D, 2560])
    w_glu = _din(nc, "w_glu", [1024, 1024])
    w_out = _din(nc, "w_out", [D, D])
    w_up = _din(nc, "w_up", [D, DFF])
    w_down = _din(nc, "w_down", [DFF, D])
    gmix, gmlp = _din(nc, "g_mix", [128, 16]), _din(nc, "g_mlp", [128, 16])
    gao, gso = _din(nc, "g_ao", [128, 8]), _din(nc, "g_so", [128, 8])
    dcol_in, bglu_in = _din(nc, "dcol", [128, 8]), _din(nc, "bglu", [128, 8])
    dband, dmeta, dqm = _din(nc, "dband", [128, 2, 128]), _din(nc, "dmeta", [128, 128]), _din(nc, "dqm", [128, 16])
    bias_in, garg_in, valid_in = _din(nc, "abias", [128, 16]), _din(nc, "garg", [128, 128]), _din(nc, "valid", [128, 1])
    gq_in, gk_in = _din(nc, "gq2", [128, 1]), _din(nc, "gk2", [128, 1])
    kth_in, vh_in = _din(nc, "kth", [128, 4, 128], BF16), _din(nc, "vh", [128, 512], BF16)
    nat_in, swp_in = _din(nc, "nat", [128, 4, 64]), _din(nc, "swp", [128, 4, 64])
    si = _ssm_inputs(nc)
    if debug:
        dbg_out = _dout(nc, "dbg_out", [128, 16, NT])
    with ExitStack() as es:
        C = Ctx(nc, es)
        S = C.S
        C.consts()
        C.tmp_rr = 0
        regX = es.enter_context(sbt(nc, "regX", [128, 16, NT], F32)); xB = Buf("xT")
        regH = es.enter_context(sbt(nc, "regH", [128, 16, NT], BF16)); hB = Buf("regH")
        aT, aTB = regX[:, 0:8, :], Buf("aT")
        yT, yB = regX[:, 8:16, :], Buf("yT")
        def small(name, shape, src):
            t = T(nc, es, name, shape); S.dma("sp", t.t[:], src, writes=[t.B]); return t
        gm, gl2 = small("gm", [128, 16], gmix), small("gl2", [128, 16], gmlp)
        ga, gs = small("ga", [128, 8], gao), small("gs", [128, 8], gso)
        dcol, bglu = small("dcolS", [128, 8], dcol_in), small("bgluS", [128, 8], bglu_in)
        S.dma("sp", regX[:], xin, writes=[xB])
        C.rmsnorm(regX, xB, 16, gm.t, gm.B, regH, hB)
        S.barrier_all()
        with ExitStack() as esM:
            regR = esM.enter_context(sbt(nc, "regR", [128, 8, NT], BF16)); rB = Buf("regR")
            with ExitStack() as es2:
                u_proj(C, w_in, regH, hB, regR, rB)
                P = ssm_prep(C, es2, si["lamS"], si["lamB"], si["bB"], si["cS"], si["tvals"], si["mask8"], si["sgn"])
                xs = ssm_xstart(C, es2, P, nat_in, swp_in)
                big = big_rows(regX)
                with ExitStack() as es3:
                    ssm_loop(C, es3, P, regR, rB, big, "B", xs=xs, yT=yT, yB=yB, dcol=dcol)
                    S.barrier_all()
                with ExitStack() as es3:
                    gelu_glu(C, es3, w_glu, bglu, yT, yB, regR, rB, big[0][0][:, 0:NT], big[0][1])
                    S.barrier_all()
            with ExitStack() as es2:
                A = attn_consts(C, es2, dband, dmeta, dqm, bias_in, garg_in, valid_in, gq_in, gk_in)
                kT = [es2.enter_context(sbt(nc, f"kT{i}", [128, 4, KTW], BF16)) for i in range(2)]; kTB = Buf("kT")
                V2 = es2.enter_context(sbt(nc, "V2", [128, 10, 512], BF16)); V2B = Buf("V2")
                tmpq = es2.enter_context(sbt(nc, "tmpq", [128, 512], F32)); tmpqB = Buf("tmpq")
                kv_proj(C, A, w_in, regH, hB, kT, kTB, V2, V2B, kth_in, vh_in, tmpq, tmpqB)
                attention(C, A, es2, w_in, regH, hB, kT, kTB, V2, V2B, aT, aTB, tmpq, tmpqB)
                S.barrier_all()
        if debug:
            S.dma("sp", dbg_out, regX[:], reads=[aTB, yB], writes=[Buf("dbg")])
        C.rmsnorm(aT, aTB, 8, ga.t, ga.B, regH, hB, 0)
        C.rmsnorm(yT, yB, 8, gs.t, gs.B, regH, hB, 8)
        S.barrier_all()
        S.dma("sp", regX[:], xin, writes=[xB])
        dense_acc_into_x(C, w_out, 16, 0, regH, hB, regX, xB, 16, 256)
        S.barrier_all()
        with ExitStack() as es5:
            hid = [es5.enter_context(sbt(nc, f"hid{i}", [128, 8, NT], BF16)) for i in range(2)]
            hidB = [Buf(f"hid{i}") for i in range(2)]
            tmp = [es5.enter_context(sbt(nc, f"ftmp{i}", [128, 512], F32)) for i in range(2)]
            tmpB = [Buf(f"ftmp{i}") for i in range(2)]
            C.rmsnorm(regX, xB, 16, gl2.t, gl2.B, regH, hB)
            ffn(C, w_up, w_down, regH, hB, regX, xB, hid, hidB, tmp, tmpB)
            S.dma("sp", xout, regX[:], reads=[xB], writes=[Buf("xout")])
            S.barrier_all()
        print("B instructions", S.n_ins, "waits", S.n_wait)
    return nc


def ssm_xstart_f(C, es, P, wend_dram, fend):
    nc, S = C.nc, C.S
    xs = T(nc, es, "xstart", [128, 64])
    wd, wdB = wend_dram
    with ExitStack() as es2:
        nat = T(nc, es2, "x_nat", [128, 64]); S.dma("sp", nat.t[:], wd, reads=[wdB], writes=[nat.B])
        swp = T(nc, es2, "x_swp", [128, 64])
        S.dma("sp", swp.t[0:64], wd[64:128], reads=[wdB], writes=[swp.B])
        S.dma("sp", swp.t[64:128], wd[0:64], reads=[wdB], writes=[swp.B])
        an, kk, sn, cs = [T(nc, es2, f"x_t{i}", [128, 64]) for i in range(4)]
        v_ts(S, nc, "dve", an.t[:], P.thS.t[:], float(fend), None, ALU.mult, None, [P.thS.B], [an.B])
        sincos(C, an.t[:], an.B, kk.t[:], kk.B, sn.t[:], sn.B, cs.t[:], cs.B)
        v_ts(S, nc, "dve", sn.t[:], sn.t[:], P.sgn.t[:, 0:1], None, ALU.mult, None, [sn.B, P.sgn.B], [sn.B])
        v_tt(S, nc, "dve", xs.t[:], cs.t[:], nat.t[:], ALU.mult, [cs.B, nat.B], [xs.B])
        v_tt(S, nc, "dve", kk.t[:], sn.t[:], swp.t[:], ALU.mult, [sn.B, swp.B], [kk.B])
        v_tt(S, nc, "dve", xs.t[:], xs.t[:], kk.t[:], ALU.add, [xs.B, kk.B], [xs.B])
        C.S.barrier_all()
    return xs


def emit_pass(C, regX, regH, l, q, W, xin, xout, halo_prev, halo_cur, wend_prev, wend_cur, cst, tab=None):
    nc, S = C.nc, C.S
    PFX[0] = f"_L{l}Q{q}"
    xB, hB = Buf("xT"), Buf("regH")
    aT, aTB = regX[:, 0:8, :], Buf("aT")
    yT, yB = regX[:, 8:16, :], Buf("yT")
    xin_ap, xinB = xin
    xout_ap, xoutB = xout
    with ExitStack() as es:
        def small(name, shape, src):
            t = T(nc, es, name, shape); S.dma("sp", t.t[:], src, writes=[t.B]); return t
        gm, gl2 = small("gm", [128, 16], W["g_mix"][l]), small("gl2", [128, 16], W["g_mlp"][l])
        ga, gs = small("ga", [128, 8], W["g_ao"][l]), small("gs", [128, 8], W["g_so"][l])
        dcol, bglu = small("dcolS", [128, 8], W["dcol"][l]), small("bgluS", [128, 8], W["bglu"][l])
        S.dma("sp", regX[:], xin_ap, reads=[xinB], writes=[xB])
        C.rmsnorm(regX, xB, 16, gm.t, gm.B, regH, hB)
        S.barrier_all()
        with ExitStack() as esM:
            regR = esM.enter_context(sbt(nc, "regR", [128, 8, NT], BF16)); rB = Buf("regR")
            with ExitStack() as es2:
                u_proj(C, W["w_in"][l], regH, hB, regR, rB)
                P = ssm_prep(C, es2, W["lamS"][l], W["lamB"][l], W["bB"][l], W["cS"][l], cst["tvals"], cst["mask8"], cst["sgn"])
                xs = None
                if q > 0:
                    xs = ssm_xstart_f(C, es2, P, wend_prev, 1040 if q == 1 else 1024)
                wend = T(nc, es2, "wend", [128, 64])
                big = big_rows(regX)
                with ExitStack() as es3:
                    ssm_loop(C, es3, P, regR, rB, big, "F", xs=xs, yT=yT, yB=yB, dcol=dcol, wend=wend, cont=(q == 0), tab=tab,
                             tab_mode=(None if tab is None else ("store" if q == 0 else "load")))
                    S.dma("sp", wend_cur[0], wend.t[:], reads=[wend.B], writes=[wend_cur[1]])
                    S.barrier_all()
                with ExitStack() as es3:
                    gelu_glu(C, es3, W["w_glu"][l], bglu, yT, yB, regR, rB, big[0][0][:, 0:NT], big[0][1])
                    S.barrier_all()
            with ExitStack() as es2:
                A = attn_consts(C, es2, cst["dband"], cst["dmeta"], cst["dqm"], W["abias"][l], cst["garg"][q], cst["valid"], W["gq2"][l], W["gk2"][l])
                kT = [es2.enter_context(sbt(nc, f"kT{i}", [128, 4, KTW], BF16)) for i in range(2)]; kTB = Buf("kT")
                V2 = es2.enter_context(sbt(nc, "V2", [128, 10, 512], BF16)); V2B = Buf("V2")
                tmpq = es2.enter_context(sbt(nc, "tmpq", [128, 512], F32)); tmpqB = Buf("tmpq")
                S.op("dve", lambda: nc.vector.memset(kT[0][:], 0.0), writes=[kTB])
                S.op("dve", lambda: nc.vector.memset(kT[1][:], 0.0), writes=[kTB])
                S.op("dve", lambda: nc.vector.memset(V2[:, 0, :], 0.0), writes=[V2B])
                if q > 0:
                    (hk, hkB), (hv, hvB) = halo_prev
                    S.dma("sp", kT[0][0:64, :, 32:160], hk[0:64], reads=[hkB], writes=[kTB])
                    S.dma("sp", kT[1][64:128, :, 32:160], hk[64:128], reads=[hkB], writes=[kTB])
                    S.dma("sp", V2[:, 1, :], hv, reads=[hvB], writes=[V2B])
                kv_proj(C, A, W["w_in"][l], regH, hB, kT, kTB, V2, V2B, None, None, tmpq, tmpqB, skip_init=True)
                (hk, hkB), (hv, hvB) = halo_cur
                S.dma("sp", hk[0:64], kT[0][0:64, :, 1056:1184], reads=[kTB], writes=[hkB])
                S.dma("sp", hk[64:128], kT[1][64:128, :, 1056:1184], reads=[kTB], writes=[hkB])
                S.dma("sp", hv, V2[:, 9, :], reads=[V2B], writes=[hvB])
                attention(C, A, es2, W["w_in"][l], regH, hB, kT, kTB, V2, V2B, aT, aTB, tmpq, tmpqB, skip_prev0=(q == 0), use_valid=False)
                S.barrier_all()
        C.rmsnorm(aT, aTB, 8, ga.t, ga.B, regH, hB, 0)
        C.rmsnorm(yT, yB, 8, gs.t, gs.B, regH, hB, 8)
        S.barrier_all()
        S.dma("sp", regX[:], xin_ap, reads=[xinB], writes=[xB])
        dense_acc_into_x(C, W["w_out"][l], 16, 0, regH, hB, regX, xB, 16, 256)
        S.barrier_all()
        with ExitStack() as es5:
            hid = [es5.enter_context(sbt(nc, f"hid{i}", [128, 8, NT], BF16)) for i in range(2)]
            hidB = [Buf(f"hid{i}") for i in range(2)]
            tmp = [es5.enter_context(sbt(nc, f"ftmp{i}", [128, 512], F32)) for i in range(2)]
            tmpB = [Buf(f"ftmp{i}") for i in range(2)]
            C.rmsnorm(regX, xB, 16, gl2.t, gl2.B, regH, hB)
            ffn(C, W["w_up"][l], W["w_down"][l], regH, hB, regX, xB, hid, hidB, tmp, tmpB)
            S.dma("sp", xout_ap, regX[:], reads=[xB], writes=[xoutB])
            S.barrier_all()


def build_F(nlayers=4, nq=4):
    nc = bass.Bass("TRN2", target_bir_lowering=False)
    xin = _din(nc, "xT_in", [4, 128, 16, NT])
    xout = _dout(nc, "xT_out", [4, 128, 16, NT])
    W = dict(w_in=_din(nc, "w_in", [4, D, 2560]), w_glu=_din(nc, "w_glu", [4, 1024, 1024]), w_out=_din(nc, "w_out", [4, D, D]),
             w_up=_din(nc, "w_up", [4, D, DFF]), w_down=_din(nc, "w_down", [4, DFF, D]),
             g_mix=_din(nc, "g_mix", [4, 128, 16]), g_mlp=_din(nc, "g_mlp", [4, 128, 16]), g_ao=_din(nc, "g_ao", [4, 128, 8]),
             g_so=_din(nc, "g_so", [4, 128, 8]), dcol=_din(nc, "dcol", [4, 128, 8]), bglu=_din(nc, "bglu", [4, 128, 8]),
             abias=_din(nc, "abias", [4, 128, 16]), gq2=_din(nc, "gq2", [4, 128, 1]), gk2=_din(nc, "gk2", [4, 128, 1]),
             lamS=_din(nc, "lamS", [4, 128, 3, 64]), lamB=_din(nc, "lamB", [4, 128, 3, 512]), bB=_din(nc, "bB", [4, 128, 2, 512]),
             cS=_din(nc, "cS", [4, 128, 2, 64, 16]))
    cst = dict(tvals=_din(nc, "tvals", [128, NTAB]), mask8=_din(nc, "mask8", [128, 8]), sgn=_din(nc, "sgn", [128, 1]),
               dband=_din(nc, "dband", [128, 2, 128]), dmeta=_din(nc, "dmeta", [128, 128]), dqm=_din(nc, "dqm", [128, 16]),
               garg=_din(nc, "garg", [4, 128, 128]), valid=_din(nc, "valid", [128, 1]))
    scr = [nc.dram_tensor(f"xscr{i}", [4, 128, 16, NT], F32).ap() for i in range(2)]
    scrB = [[Buf(f"xscr{i}_{q}") for q in range(4)] for i in range(2)]
    hk = [nc.dram_tensor(f"hk{i}", [128, 4, 128], BF16).ap() for i in range(2)]
    hv = [nc.dram_tensor(f"hv{i}", [128, 512], BF16).ap() for i in range(2)]
    hB_ = [(Buf(f"hk{i}"), Buf(f"hv{i}")) for i in range(2)]
    wd = [nc.dram_tensor(f"wd{i}", [128, 64], F32).ap() for i in range(2)]
    wdB = [Buf(f"wd{i}") for i in range(2)]
    tab = (nc.dram_tensor("tabscr", [64, 2, 128, NTAB], F32).ap(), [Buf(f"tab{g}") for g in range(64)])
    with ExitStack() as es:
        C = Ctx(nc, es)
        S = C.S
        C.consts()
        C.tmp_rr = 0
        regX = es.enter_context(sbt(nc, "regX", [128, 16, NT], F32))
        regH = es.enter_context(sbt(nc, "regH", [128, 16, NT], BF16))
        xinB = Buf("xin")
        outB = [Buf(f"xout{q}") for q in range(4)]
        for l in range(nlayers):
            for q in range(nq):
                src = (xin[q], xinB) if l == 0 else (scr[(l - 1) % 2][q], scrB[(l - 1) % 2][q])
                dst = (xout[q], outB[q]) if l == nlayers - 1 else (scr[l % 2][q], scrB[l % 2][q])
                i, j = q % 2, (q + 1) % 2
                emit_pass(C, regX, regH, l, q, W, src, dst,
                          ((hk[j], hB_[j][0]), (hv[j], hB_[j][1])), ((hk[i], hB_[i][0]), (hv[i], hB_[i][1])),
                          (wd[j], wdB[j]), (wd[i], wdB[i]), cst, tab=(tab if nq > 1 else None))
        PFX[0] = ""
        print("F instructions", S.n_ins, "waits", S.n_wait)
    return nc

def ssm_host_layout(lre, lim, lst, bre, bim, cre, cim):
    lamS = np.zeros((128, 3, 64), np.float32)
    for ri in range(2):
        lamS[ri * 64:(ri + 1) * 64, 0, :] = lre.T
        lamS[ri * 64:(ri + 1) * 64, 1, :] = lim.T
        lamS[ri * 64:(ri + 1) * 64, 2, :] = lst[None, :]
    def layB(a_gn):
        a = a_gn.reshape(8, 8, 64)
        a = a.transpose(1, 0, 2)
        return np.repeat(a[:, None], 16, axis=1).reshape(128, 8 * 64)
    lamB = np.stack([layB(lre), layB(lim), layB(np.repeat(lst[:, None], 64, 1))], 1).astype(np.float32)
    def layBb(b):
        a = b.reshape(8, 8, 64, 16).transpose(1, 3, 0, 2)
        return a.reshape(128, 512)
    bB = np.stack([layBb(bre), layBb(bim)], 1).astype(np.float32)
    def layC(c):
        a = c.transpose(2, 0, 1)
        return np.concatenate([a, a], 0)
    cS = np.stack([layC(cre), layC(cim)], 1).astype(np.float32)
    return dict(lamS=lamS, lamB=np.ascontiguousarray(lamB), bB=np.ascontiguousarray(bB), cS=np.ascontiguousarray(cS))


def ssm_host_consts():
    tvals = np.broadcast_to(np.arange(NTAB, dtype=np.float32)[None], (128, NTAB)).copy()
    mask8 = np.zeros((128, 8), np.float32)
    for p in range(128):
        mask8[p, p // 16] = 1.0
    sgn = np.ones((128, 1), np.float32)
    sgn[:64] = -1.0
    return dict(tvals=tvals, mask8=mask8, sgn=sgn)


_PROGS = {}
NLAYERS = 4
DEBUG_LAST = {}
TRACE = False
STRICT = [True]


def _prog(name):
    if name not in _PROGS:
        _PROGS[name] = build_A() if name == "A" else build_B()
    return _PROGS[name]


def kernel_unfused(x, meta_tokens, norm_mix_g, w_in, q_norm_g, k_norm_g, attn_sinks, ssm_lambda_re, ssm_lambda_im,
           ssm_log_step, ssm_b_re, ssm_b_im, ssm_c_re, ssm_c_im, ssm_d, w_glu, b_glu, attn_out_g, ssm_out_g,
           w_out, norm_mlp_g, w_up, w_down):
    f = lambda a: np.asarray(a, dtype=np.float32)
    x, meta_tokens = f(x), f(meta_tokens)
    ncores = 8
    xs = []
    for c in range(ncores):
        b, q = c // 4, c % 4
        tok = np.concatenate([meta_tokens, x[b, 1024 * q:1024 * (q + 1)]], 0)
        xs.append(to_fm(tok))
    hconst = [attn_host_consts(c % 4) for c in range(ncores)]
    sconst = ssm_host_consts()
    zero_kth = np.zeros((128, 4, 128), np.float32).astype(BF16NP)
    zero_vh = np.zeros((128, 512), np.float32).astype(BF16NP)
    for l in range(NLAYERS):
        sl = ssm_host_layout(f(ssm_lambda_re[l]), f(ssm_lambda_im[l]), f(ssm_log_step[l]), f(ssm_b_re[l]), f(ssm_b_im[l]),
                             f(ssm_c_re[l]), f(ssm_c_im[l]))
        common = dict(w_in=f(w_in[l]), g_mix=gcols(f(norm_mix_g[l])), gk2=np.tile(f(k_norm_g[l]), 2).reshape(128, 1), **sl, **sconst)
        insA = [dict(xT_in=xs[c], **common) for c in range(ncores)]
        resA = run_bass_kernel_spmd(_prog("A"), insA, core_ids=list(range(ncores))).results
        commonB = dict(common, w_glu=f(w_glu[l]), w_out=f(w_out[l]), w_up=f(w_up[l]), w_down=f(w_down[l]),
                       g_mlp=gcols(f(norm_mlp_g[l])), g_ao=gcols(f(attn_out_g[l])), g_so=gcols(f(ssm_out_g[l])),
                       dcol=gcols(f(ssm_d[l])), bglu=gcols(f(b_glu[l])), abias=sink_bias(f(attn_sinks[l])),
                       gq2=np.tile(f(q_norm_g[l]), 2).reshape(128, 1))
        insB = []
        for c in range(ncores):
            b, q = c // 4, c % 4
            nat = np.zeros((128, 4, 64), np.float32)
            for p in range(q):
                nat[:, p, :] = resA[b * 4 + q - 1 - p]["wend_out"][:, 0, :]
            nat[:, q, :] = resA[c]["wend_out"][:, 1, :]
            swp = np.concatenate([nat[64:], nat[:64]], 0)
            kth = resA[c - 1]["kth_out"] if q > 0 else zero_kth
            vh = resA[c - 1]["vh_out"] if q > 0 else zero_vh
            insB.append(dict(xT_in=xs[c], kth=kth, vh=vh, nat=nat, swp=np.ascontiguousarray(swp), **commonB, **hconst[c]))
        rB_ = run_bass_kernel_spmd(_prog("B"), insB, core_ids=list(range(ncores)), trace=TRACE) if TRACE else run_bass_kernel_spmd(_prog("B"), insB, core_ids=list(range(ncores)))
        resB = rB_.results
        DEBUG_LAST["B_ns"] = rB_.exec_time_ns
        xs = [np.asarray(resB[c]["xT_out"], dtype=np.float32) for c in range(ncores)]
        DEBUG_LAST["resA"], DEBUG_LAST["resB"], DEBUG_LAST["xs"] = resA, resB, xs
    out = np.zeros((2, 4096, D), np.float32)
    for c in range(ncores):
        b, q = c // 4, c % 4
        out[b, 1024 * q:1024 * (q + 1)] = from_fm(xs[c])[NMETA:]
    return out


def _fused_inputs(inp):
    f = lambda a: np.asarray(a, dtype=np.float32)
    x, meta = f(inp["x"]), f(inp["meta_tokens"])
    L4 = range(4)
    sl = [ssm_host_layout(f(inp["ssm_lambda_re"][l]), f(inp["ssm_lambda_im"][l]), f(inp["ssm_log_step"][l]), f(inp["ssm_b_re"][l]),
                          f(inp["ssm_b_im"][l]), f(inp["ssm_c_re"][l]), f(inp["ssm_c_im"][l])) for l in L4]
    st = lambda fn: np.ascontiguousarray(np.stack([fn(l) for l in L4], 0))
    common = dict(
        w_in=f(inp["w_in"]), w_glu=f(inp["w_glu"]), w_out=f(inp["w_out"]), w_up=f(inp["w_up"]), w_down=f(inp["w_down"]),
        g_mix=st(lambda l: gcols(f(inp["norm_mix_g"][l]))), g_mlp=st(lambda l: gcols(f(inp["norm_mlp_g"][l]))),
        g_ao=st(lambda l: gcols(f(inp["attn_out_g"][l]))), g_so=st(lambda l: gcols(f(inp["ssm_out_g"][l]))),
        dcol=st(lambda l: gcols(f(inp["ssm_d"][l]))), bglu=st(lambda l: gcols(f(inp["b_glu"][l]))),
        abias=st(lambda l: sink_bias(f(inp["attn_sinks"][l]))),
        gq2=st(lambda l: np.tile(f(inp["q_norm_g"][l]), 2).reshape(128, 1)), gk2=st(lambda l: np.tile(f(inp["k_norm_g"][l]), 2).reshape(128, 1)),
        lamS=st(lambda l: sl[l]["lamS"]), lamB=st(lambda l: sl[l]["lamB"]), bB=st(lambda l: sl[l]["bB"]), cS=st(lambda l: sl[l]["cS"]),
        **ssm_host_consts())
    hc = [attn_host_consts(q) for q in range(4)]
    common.update(dband=hc[0]["dband"], dmeta=hc[0]["dmeta"], dqm=hc[0]["dqm"], valid=hc[1]["valid"],
                  garg=np.ascontiguousarray(np.stack([hc[q]["garg"] for q in range(4)], 0)))
    ins = []
    for c in range(8):
        b = c // 4
        xq = np.stack([to_fm(np.concatenate([meta, x[b, 1024 * q:1024 * (q + 1)]], 0)) for q in range(4)], 0)
        ins.append(dict(xT_in=np.ascontiguousarray(xq), **common))
    return ins


def kernel_fused(**inp):
    if "F" not in _PROGS:
        _PROGS["F"] = build_F(NLAYERS)
    res = run_bass_kernel_spmd(_PROGS["F"], _fused_inputs(inp), core_ids=list(range(8))).results
    out = np.zeros((2, 4096, D), np.float32)
    for b in range(2):
        xo = np.asarray(res[4 * b]["xT_out"], dtype=np.float32)
        for q in range(4):
            out[b, 1024 * q:1024 * (q + 1)] = from_fm(xo[q])[NMETA:]
    DEBUG_LAST["resF"] = res
    return out


def to_fm(tok):
    return np.ascontiguousarray(tok.T.reshape(16, 128, tok.shape[0]).transpose(1, 0, 2))


def from_fm(fm):
    return np.ascontiguousarray(fm.transpose(1, 0, 2).reshape(D, fm.shape[2]).T)


def gcols(g):
    return np.ascontiguousarray(g.reshape(-1, 128).T)


def kernel(**inputs):
    return kernel_unfused(**inputs)
```
